# Optimizing a Trainium2 kernel written in Bass

```python
import jax, jax.numpy as jnp
from jax import lax
import numpy as np

D_MODEL = 2048
BATCH = 4
SEQ = 2048
DEPTH = 4

HEAD_DIM = 128
EPS = 1e-6
DN_QK_HEADS = 8
DN_V_HEADS = 16
DN_DK = 128
DN_DV = 128
DN_CONV = 4
DN_CHUNK = 64
SW_HEADS = 8
SW_PATTERNS = ((128, 1), (512, 4), (2048, 16))
SW_GROUPS = 3
SW_BLOCK = 128
ROPE_THETA = 500000.0
ROPE_DIM = HEAD_DIM // 4
SC_WIDTH = 3072
SC_CONV = 3

A_Q = DN_QK_HEADS * DN_DK
A_K = DN_QK_HEADS * DN_DK
A_V = DN_V_HEADS * DN_DV
A_Z = A_V
B_Q = SW_GROUPS * SW_HEADS * HEAD_DIM
B_K = SW_HEADS * HEAD_DIM
B_V = SW_HEADS * HEAD_DIM
B_Z = B_V
HYB_IN = A_Q + A_K + A_V + A_Z + 2 * DN_V_HEADS + B_Q + B_K + B_V + B_Z
HYB_MIX = A_V + B_V
N_EVEN = (DEPTH + 1) // 2
N_ODD = DEPTH // 2

kernel_name = "hybrid_deltanet_dilated_shortconv"


def rmsnorm(x, w):
    xf = x.astype(jnp.float32)
    y = xf * lax.rsqrt(jnp.mean(xf * xf, axis=-1, keepdims=True) + EPS)
    return (y * w.astype(jnp.float32)).astype(x.dtype)


def causal_depthwise_conv(x, w):
    k = w.shape[1]
    rhs = jnp.transpose(w)[:, None, :].astype(x.dtype)
    return lax.conv_general_dilated(x, rhs, window_strides=(1,), padding=[(k - 1, 0)],
                                    dimension_numbers=('NWC', 'WIO', 'NWC'),
                                    feature_group_count=x.shape[-1])


def partial_rope(x, pos):
    half = ROPE_DIM // 2
    inv = jnp.power(jnp.float32(ROPE_THETA), -jnp.arange(half, dtype=jnp.float32) * 2.0 / ROPE_DIM)
    ang = pos.astype(jnp.float32)[:, None] * inv[None, :]
    cos = jnp.cos(ang)[None, :, None, :]
    sin = jnp.sin(ang)[None, :, None, :]
    x1 = x[..., :half]
    x2 = x[..., half:ROPE_DIM]
    return jnp.concatenate([x1 * cos - x2 * sin, x2 * cos + x1 * sin, x[..., ROPE_DIM:]], axis=-1)


def l2norm(x):
    return x * lax.rsqrt(jnp.sum(x * x, axis=-1, keepdims=True) + EPS)


def gated_delta_chunked(q, k, v, beta, g):
    b, s, h, dk = q.shape
    dv = v.shape[-1]
    n = s // DN_CHUNK
    c = DN_CHUNK

    def chunks(t):
        return t.reshape(b, n, c, h, t.shape[-1]).transpose(0, 1, 3, 2, 4)

    q, k, v = chunks(q), chunks(k), chunks(v)
    beta = beta.reshape(b, n, c, h).transpose(0, 1, 3, 2)
    g = jnp.cumsum(g.reshape(b, n, c, h).transpose(0, 1, 3, 2), axis=-1)
    idx = jnp.arange(c)
    causal = idx[:, None] >= idx[None, :]
    strict = idx[:, None] > idx[None, :]
    diff = g[..., :, None] - g[..., None, :]
    decay = jnp.exp(jnp.where(causal, diff, -jnp.inf))
    kk = jnp.einsum('bnhik,bnhjk->bnhij', k, k)
    a_strict = jnp.where(strict, beta[..., :, None] * kk * decay, 0.0)
    rhs = jnp.concatenate([v * beta[..., None], k * (beta * jnp.exp(g))[..., None]], axis=-1)
    sol = lax.linalg.triangular_solve(a_strict, rhs, left_side=True, lower=True, unit_diagonal=True)
    u = sol[..., :dv]
    w = sol[..., dv:]
    qk = jnp.einsum('bnhik,bnhjk->bnhij', q, k) * decay

    def step(state, inp):
        q_c, k_c, u_c, w_c, g_c, qk_c = inp
        v_new = u_c - jnp.einsum('bhck,bhkv->bhcv', w_c, state)
        o = jnp.einsum('bhck,bhkv->bhcv', q_c * jnp.exp(g_c)[..., None], state) + \
            jnp.einsum('bhij,bhjv->bhiv', qk_c, v_new)
        g_last = g_c[..., -1]
        state = state * jnp.exp(g_last)[..., None, None] + jnp.einsum(
            'bhck,bhcv->bhkv', k_c * jnp.exp(g_last[..., None] - g_c)[..., None], v_new)
        return state, o

    xs = tuple(jnp.moveaxis(t, 1, 0) for t in (q, k, u, w, g, qk))
    state0 = jnp.zeros((b, h, dk, dv), jnp.float32)
    _, o = lax.scan(step, state0, xs)
    return o.transpose(1, 0, 3, 2, 4).reshape(b, s, h, dv)


def dilated_window_group(q, k, v, dil):
    b, s, h, dh = q.shape
    wb = SW_BLOCK
    ln = s // dil
    nb = -(-ln // wb)
    lp = nb * wb

    def strided(t):
        t = t.reshape(b, ln, dil, h, dh).transpose(0, 2, 3, 1, 4)
        t = jnp.pad(t, ((0, 0), (0, 0), (0, 0), (0, lp - ln), (0, 0)))
        return t.reshape(b, dil, h, nb, wb, dh)

    def with_prev(t):
        prev = jnp.concatenate([jnp.zeros_like(t[:, :, :, :1]), t[:, :, :, :-1]], axis=3)
        return jnp.concatenate([prev, t], axis=4)

    qs = strided(q)
    kb = with_prev(strided(k))
    vb = with_prev(strided(v))
    sc = jnp.einsum('brhnqc,brhnkc->brhnqk', qs, kb) * (dh ** -0.5)
    i = jnp.arange(wb)[:, None]
    j = jnp.arange(2 * wb)[None, :]
    dist = i + wb - j
    blk = jnp.arange(nb)[:, None, None]
    valid = (dist >= 0) & (dist <= wb) & (blk * wb - wb + j >= 0)
    sc = jnp.where(valid, sc, -jnp.inf)
    m = jnp.max(sc, axis=-1, keepdims=True)
    p = jnp.exp(sc - m)
    den = jnp.sum(p, axis=-1)
    num = jnp.einsum('brhnqk,brhnkc->brhnqc', p, vb)

    def unstride(t):
        rest = t.shape[5:]
        t = t.reshape((b, dil, h, lp) + rest)[:, :, :, :ln]
        return jnp.moveaxis(t, 3, 1).reshape((b, s, h) + rest)

    return unstride(num), unstride(den), unstride(m[..., 0])


def dilated_attention(q, k, v):
    nums, dens, maxs = [], [], []
    for gi, (window, dil) in enumerate(SW_PATTERNS):
        n_, d_, m_ = dilated_window_group(q[:, :, gi], k, v, dil)
        nums.append(n_)
        dens.append(d_)
        maxs.append(m_)
    mx = jnp.max(jnp.stack(maxs), axis=0)
    wts = [jnp.exp(m_ - mx) for m_ in maxs]
    num = sum(w_[..., None] * n_ for w_, n_ in zip(wts, nums))
    den = sum(w_ * d_ for w_, d_ in zip(wts, dens))
    return num / den[..., None]


def hybrid_layer(h, w_in, conv_w, a_log, dt_bias, dn_norm_w, w_out, pos):
    b, s, _ = h.shape
    f32 = jnp.float32
    proj = h @ w_in
    sizes = [A_Q, A_K, A_V, A_Z, DN_V_HEADS, DN_V_HEADS, B_Q, B_K, B_V, B_Z]
    offs = []
    acc = 0
    for sz in sizes[:-1]:
        acc += sz
        offs.append(acc)
    aq, ak, av, az, ab, aa, bq, bk, bv, bz = jnp.split(proj, offs, axis=-1)

    qkv = jax.nn.silu(causal_depthwise_conv(jnp.concatenate([aq, ak, av], axis=-1), conv_w)).astype(f32)
    q_a = l2norm(qkv[..., :A_Q].reshape(b, s, DN_QK_HEADS, DN_DK)) * (DN_DK ** -0.5)
    k_a = l2norm(qkv[..., A_Q:A_Q + A_K].reshape(b, s, DN_QK_HEADS, DN_DK))
    v_a = qkv[..., A_Q + A_K:].reshape(b, s, DN_V_HEADS, DN_DV)
    rep = DN_V_HEADS // DN_QK_HEADS
    q_a = jnp.repeat(q_a, rep, axis=2)
    k_a = jnp.repeat(k_a, rep, axis=2)
    beta = jax.nn.sigmoid(ab.astype(f32))
    g = -jnp.exp(a_log.astype(f32)) * jax.nn.softplus(aa.astype(f32) + dt_bias.astype(f32))
    o_a = gated_delta_chunked(q_a, k_a, v_a, beta, g)
    o_a = o_a * lax.rsqrt(jnp.mean(o_a * o_a, axis=-1, keepdims=True) + EPS) * dn_norm_w.astype(f32)
    y_a = (o_a.reshape(b, s, A_V) * jax.nn.silu(az.astype(f32))).astype(h.dtype)

    q_b = partial_rope(bq.astype(f32).reshape(b, s, SW_GROUPS * SW_HEADS, HEAD_DIM), pos)
    q_b = q_b.reshape(b, s, SW_GROUPS, SW_HEADS, HEAD_DIM)
    k_b = partial_rope(bk.astype(f32).reshape(b, s, SW_HEADS, HEAD_DIM), pos)
    v_b = bv.astype(f32).reshape(b, s, SW_HEADS, HEAD_DIM)
    o_b = dilated_attention(q_b, k_b, v_b)
    y_b = (o_b.reshape(b, s, B_V) * jax.nn.silu(bz.astype(f32))).astype(h.dtype)

    return jnp.concatenate([y_a, y_b], axis=-1) @ w_out


def shortconv_layer(h, w_in, conv_w, w_out):
    gate_b, gate_c, u, z = jnp.split(h @ w_in, 4, axis=-1)
    y = gate_b * causal_depthwise_conv(gate_c * u, conv_w)
    return (y * jax.nn.silu(z)) @ w_out


def setup_inputs(seed: int = 0) -> dict:
    key = jax.random.key(seed)
    ks = jax.random.split(key, 14)
    f32 = jnp.float32

    def nrm(k, shape, scale):
        return jax.random.normal(k, shape, f32) * scale

    x = nrm(ks[0], (BATCH, SEQ, D_MODEL), 1.0)
    norm_w = 1.0 + nrm(ks[1], (DEPTH, D_MODEL), 0.05)
    hyb_w_in = nrm(ks[2], (N_EVEN, D_MODEL, HYB_IN), D_MODEL ** -0.5)
    dn_conv_w = nrm(ks[3], (N_EVEN, A_Q + A_K + A_V, DN_CONV), DN_CONV ** -0.5)
    dn_a_log = jnp.log(jax.random.uniform(ks[4], (N_EVEN, DN_V_HEADS), f32, 1.0, 16.0))
    dt = jnp.exp(jax.random.uniform(ks[5], (N_EVEN, DN_V_HEADS), f32, jnp.log(1e-3), jnp.log(1e-1)))
    dn_dt_bias = dt + jnp.log(-jnp.expm1(-dt))
    dn_norm_w = 1.0 + nrm(ks[6], (N_EVEN, DN_DV), 0.05)
    hyb_w_out = nrm(ks[7], (N_EVEN, HYB_MIX, D_MODEL), HYB_MIX ** -0.5)
    sc_w_in = nrm(ks[8], (N_ODD, D_MODEL, 4 * SC_WIDTH), D_MODEL ** -0.5)
    sc_conv_w = nrm(ks[9], (N_ODD, SC_WIDTH, SC_CONV), SC_CONV ** -0.5)
    sc_w_out = nrm(ks[10], (N_ODD, SC_WIDTH, D_MODEL), SC_WIDTH ** -0.5)
    final_norm_w = 1.0 + nrm(ks[11], (D_MODEL,), 0.05)
    return {"x": x, "norm_w": norm_w, "hyb_w_in": hyb_w_in, "dn_conv_w": dn_conv_w,
            "dn_a_log": dn_a_log, "dn_dt_bias": dn_dt_bias, "dn_norm_w": dn_norm_w,
            "hyb_w_out": hyb_w_out, "sc_w_in": sc_w_in, "sc_conv_w": sc_conv_w,
            "sc_w_out": sc_w_out, "final_norm_w": final_norm_w}


def reference(x, norm_w, hyb_w_in, dn_conv_w, dn_a_log, dn_dt_bias, dn_norm_w, hyb_w_out,
              sc_w_in, sc_conv_w, sc_w_out, final_norm_w):
    pos = jnp.arange(x.shape[1], dtype=jnp.int32)
    h = x
    for layer in range(DEPTH):
        hn = rmsnorm(h, norm_w[layer])
        li = layer // 2
        if layer % 2 == 0:
            h = h + hybrid_layer(hn, hyb_w_in[li], dn_conv_w[li], dn_a_log[li], dn_dt_bias[li],
                                 dn_norm_w[li], hyb_w_out[li], pos)
        else:
            h = h + shortconv_layer(hn, sc_w_in[li], sc_conv_w[li], sc_w_out[li])
    return rmsnorm(h, final_norm_w)
```

```python
import numpy as np
from contextlib import ExitStack
import concourse.bass as bass
import concourse.mybir as mybir
from concourse.bass_utils import run_bass_kernel_spmd

F32 = mybir.dt.float32
BF16 = mybir.dt.bfloat16
AF = mybir.ActivationFunctionType
ALU = mybir.AluOpType

T = 2048
D = 2048
NT = 16
EPS = 1e-6
NEG = -30000.0

EPOCH = 12000
DMA_SLOTS = 8


class Buf:
    __slots__ = ("name", "w", "rs", "excl")

    def __init__(self, name="", excl=False):
        self.name = name
        self.w = None
        self.rs = []
        self.excl = excl


class Op:
    __slots__ = ("stream", "fn", "deps", "dma", "flagged", "fidx", "slot", "use", "n", "cc")

    def __init__(self, stream, fn, dma):
        self.stream = stream
        self.fn = fn
        self.dma = dma
        self.deps = []
        self.flagged = False
        self.fidx = 0
        self.slot = 0
        self.use = 0
        self.n = 0
        self.cc = False


class Sched:
    STREAMS = ("pe", "dve", "act", "pool", "sp")

    def __init__(self):
        self.ops = {s: [] for s in self.STREAMS}
        self.ndma = {s: 0 for s in self.STREAMS}
        self.nops = 0

    def op(self, stream, fn, reads=(), writes=(), dma=False, cc=False):
        o = Op(stream, fn, dma or cc)
        o.cc = cc
        o.n = self.nops
        self.nops += 1
        ex = [b for b in reads if b.excl]
        if ex:
            reads = [b for b in reads if not b.excl]
            writes = list(writes) + ex
        deps = {}
        for b in reads:
            if b.w is not None:
                deps[id(b.w)] = b.w
        for b in writes:
            if b.w is not None:
                deps[id(b.w)] = b.w
            for r in b.rs:
                deps[id(r)] = r
        best = {}
        for d in deps.values():
            if d is o:
                continue
            if d.dma:
                o.deps.append(d)
                continue
            if d.stream == stream and stream == "pe" and not dma:
                continue
            cur = best.get(d.stream)
            if cur is None or d.n > cur.n:
                best[d.stream] = d
        for d in best.values():
            o.deps.append(d)
            d.flagged = True
        for b in reads:
            b.rs.append(o)
        for b in writes:
            b.w = o
            b.rs = []
        if cc:
            self.ncc = getattr(self, "ncc", 0) + 1
            o.use = self.ncc
        elif dma:
            n = self.ndma[stream]
            self.ndma[stream] = n + 1
            o.slot = n % DMA_SLOTS
            o.use = n // DMA_SLOTS + 1
        self.ops[stream].append(o)
        return o

    def emit(self, nc, stack):
        nsem = {}
        for s in self.STREAMS:
            c = 0
            for o in self.ops[s]:
                if o.flagged and not o.dma:
                    c += 1
                    o.fidx = c
            nsem[s] = (max(c - 1, 0) // EPOCH) + 1
        csem = {}
        for s in self.STREAMS:
            for e in range(nsem[s]):
                csem[(s, e)] = stack.enter_context(nc.semaphore(f"c_{s}_{e}"))
        dsem = {}
        for s in self.STREAMS:
            if self.ndma[s] > 0:
                for k in range(DMA_SLOTS):
                    dsem[(s, k)] = stack.enter_context(nc.semaphore(f"d_{s}_{k}"))
        ccsem = stack.enter_context(nc.semaphore("ccsem")) if getattr(self, "ncc", 0) else None
        block = stack.enter_context(nc.Block())

        def run_stream(s, eng):
            waited = {}
            for o in self.ops[s]:
                need = {}
                for d in o.deps:
                    if d.cc:
                        key = ("cc", 0, 0)
                        val = d.use
                    elif d.dma:
                        key = ("d", d.stream, d.slot)
                        val = 16 * d.use
                    else:
                        e = (d.fidx - 1) // EPOCH
                        key = ("c", d.stream, e)
                        val = d.fidx - e * EPOCH
                    if need.get(key, 0) < val:
                        need[key] = val
                if o.cc:
                    if o.use > 1:
                        need[("cc", 0, 0)] = max(need.get(("cc", 0, 0), 0), o.use - 1)
                elif o.dma and o.use > 1:
                    key = ("d", s, o.slot)
                    val = 16 * (o.use - 1)
                    if need.get(key, 0) < val:
                        need[key] = val
                for key, val in need.items():
                    if waited.get(key, 0) >= val:
                        continue
                    waited[key] = val
                    sem = ccsem if key[0] == "cc" else (dsem[(key[1], key[2])] if key[0] == "d" else csem[(key[1], key[2])])
                    eng.wait_ge(sem, val)
                if o.fn is None:
                    continue
                ins = o.fn(eng)
                if o.cc:
                    ins.then_inc(ccsem, 1)
                elif o.dma:
                    ins.then_inc(dsem[(s, o.slot)], 16)
                elif o.flagged:
                    e = (o.fidx - 1) // EPOCH
                    ins.then_inc(csem[(s, e)], 1)

        if self.ops["pe"]:
            block.tensor(lambda eng: run_stream("pe", eng))
        if self.ops["dve"]:
            block.vector(lambda eng: run_stream("dve", eng))
        if self.ops["act"]:
            block.scalar(lambda eng: run_stream("act", eng))
        if self.ops["pool"]:
            block.gpsimd(lambda eng: run_stream("pool", eng))
        if self.ops["sp"]:
            block.sync(lambda eng: run_stream("sp", eng))


NQK = 4
NV = 8
NSW = 4
SWG = ((1, 0), (4, 1), (16, 2))


def build_program(layers=(0, 1, 2, 3), final_norm=True):
    nc = bass.Bass("TRN2", target_bir_lowering=False)

    def din(name, shape, dt=F32):
        return nc.dram_tensor(name, list(shape), dt, kind="ExternalInput").ap()

    def dscr(name, shape, dt=F32):
        return nc.dram_tensor(name, list(shape), dt).ap()

    x_d = din("x", [T, D])
    normw_d = din("normw", [128, 64])
    fnw_d = din("fnw", [1, D])
    consts_d = din("consts", [128, 9, 128])
    rope_d = din("rope", [128, 2, T])
    hyb_w = [din(f"hw{i}", [12, D, 512]) for i in range(2)]
    hyb_ab = [din(f"hab{i}", [D, 16]) for i in range(2)]
    hyb_cw = [din(f"hcw{i}", [128, 16, 4]) for i in range(2)]
    hyb_al = [din(f"hal{i}", [1, 8]) for i in range(2)]
    hyb_dt = [din(f"hdt{i}", [1, 8]) for i in range(2)]
    hyb_nw = [din(f"hnw{i}", [1, 128]) for i in range(2)]
    hyb_wo = [din(f"hwo{i}", [1536, D]) for i in range(2)]
    sc_w = [din(f"sw{i}", [12, D, 512]) for i in range(2)]
    sc_cw = [din(f"scw{i}", [128, 12, 3]) for i in range(2)]
    sc_wo = [din(f"swo{i}", [1536, D]) for i in range(2)]
    out_d = nc.dram_tensor("out", [T, D], F32, kind="ExternalOutput").ap()

    H = [dscr("Hs0", [T, D]), dscr("Hs1", [T, D])]
    YT = dscr("YT", [1536, T], BF16)
    ZS = dscr("ZS", [T, 1536])
    OA = dscr("OA", [T, 1024])
    Pp = dscr("Pp", [T, D])
    Ps = dscr("Ps", [T, D])
    OB = dscr("OB", [3, T, NSW, 129])

    S = Sched()
    with ExitStack() as st:
        def sb(name, shape, dt=F32):
            return st.enter_context(nc.sbuf_tensor(name, list(shape), dt))

        def bufs(name, n):
            return [Buf(f"{name}{i}") for i in range(n)]

        ps_f = [st.enter_context(nc.psum_tensor(f"ps{i}", [128, 512], F32)) for i in range(6)]
        ps_b = [Buf(f"ps{i}", excl=True) for i in range(6)]
        ps16 = [st.enter_context(nc.psum_tensor(f"ps16_{i}", [128, 1024], BF16)) for i in range(2)]
        ps16_b = [Buf(f"ps16_{i}", excl=True) for i in range(2)]
        ps_ctr = [0, 0]
        ACC0 = 4

        def psum():
            i = ps_ctr[0] % 4
            ps_ctr[0] += 1
            return ps_f[i], ps_b[i]

        def psum16(full=False):
            i = ps_ctr[1] % 2
            ps_ctr[1] += 1
            if full:
                return ps16[i], ps16_b[i]
            return ps16[i][:, 0:128], ps16_b[i]

        cst = sb("cst", [128, 4, 128])
        cst_b = Buf("cst")
        S.op("sp", lambda e: e.dma_start(out=cst[:], in_=consts_d[:, 0:4, :]), writes=[cst_b], dma=True)
        cstb = sb("cstb", [128, 5, 128], BF16)
        cstb_b = Buf("cstb")
        S.op("pool", lambda e: e.dma_start(out=cstb[:], in_=consts_d[:, 4:9, :]), writes=[cstb_b], dma=True)
        ident_f, UT, MBT, SMT = cst[:, 0, :], cst[:, 1, :], cst[:, 2, :], cst[:, 3, :]
        ident_b, permT_b, ones_b = cstb[:, 0, :], cstb[:, 1, :], cstb[:, 2, :]
        normw = sb("normw_s", [128, 64])
        normw_b = Buf("normw")
        S.op("sp", lambda e: e.dma_start(out=normw[:], in_=normw_d[:, :]), writes=[normw_b], dma=True)

        arena = sb("arena", [128, 49152], BF16)
        hnT = arena[:, 0:32768].rearrange("p (c t) -> p c t", c=16)
        wblk = [arena[:, 32768 + i * 8192: 32768 + (i + 1) * 8192].rearrange("p (c n) -> p c n", c=16)
                for i in range(2)]
        wo = arena[:, 0:24576].rearrange("p (c n) -> p c n", c=12)
        tokA = Buf("tokA")
        hnT_b = bufs("hnT", NT)
        wblk_b = bufs("wblk", 2)
        wo_b = bufs("wo", 12)
        fence = sb("fence", [128, 4])
        wctr = [0]

        def load_wblk(src_ap, ncols=512):
            i = wctr[0] % 2
            wctr[0] += 1
            S.op("pool", lambda e: e.dma_start(out=wblk[i][:, :, 0:ncols],
                                               in_=src_ap.rearrange("(c p) n -> p c n", p=128)),
                 reads=[tokA], writes=[wblk_b[i]], dma=True)
            return wblk[i], wblk_b[i]

        FW = sb("FW", [128, 4 * 2052])
        Fv = [FW[:, i * 2052:(i + 1) * 2052] for i in range(4)]
        F_b = bufs("F", 4)
        BT = [sb(f"BT{i}", [128, T], BF16) for i in range(6)]
        BT_b = bufs("BT", 6)
        hbuf = sb("hbuf", [128, D])
        hbuf_b = Buf("hbuf")
        SM = [sb(f"SM{i}", [128, 512]) for i in range(4)]
        SM_b = bufs("SM", 4)
        col = sb("colstat", [128, 64])
        col_b = bufs("col", 64)
        ytile = sb("ytile", [128, 12, 128], BF16)
        ytile_b = Buf("ytile")
        Hb = {}
        YT_b = bufs("YT", 12)
        Pp_b = bufs("Pp", NT)
        Ps_b = bufs("Ps", 4)
        ZS_b = bufs("ZS", NT)
        OA_b = bufs("OA", NT)
        OB_b = bufs("OB", NT)
        out_b = bufs("out", NT)
        hs, hs_b = Fv[3][:, 0:D], F_b[3]
        junk, junk_b = BT[5], BT_b[5]

        def load_h(h_src, h_dst, tt, delta):
            S.op("sp", lambda e: e.dma_start(out=hbuf[:], in_=h_src[tt * 128:(tt + 1) * 128, :]),
                 reads=[Hb[id(h_src)][tt]], writes=[hbuf_b], dma=True)
            if delta:
                S.op("sp", lambda e: e.dma_start(out=hs, in_=Ps[tt * 128:(tt + 1) * 128, :]),
                     reads=[Ps_b[tt // 4]], writes=[hs_b], dma=True)
                S.op("dve", lambda e: e.tensor_tensor(out=hbuf[:], in0=hbuf[:], in1=hs, op=ALU.add),
                     reads=[hbuf_b, hs_b], writes=[hbuf_b])
                if h_dst is not None:
                    S.op("sp", lambda e: e.dma_start(out=h_dst[tt * 128:(tt + 1) * 128, :], in_=hbuf[:]),
                         reads=[hbuf_b], writes=[Hb[id(h_dst)][tt]], dma=True)

        def norm_phase(h_src, h_dst, delta, layer):
            S.op("dve", lambda e: e.memset(fence[:, 0:1], 0.0), writes=[tokA])
            for tt in range(NT):
                load_h(h_src, h_dst, tt, delta)
                c0, c0b = col[:, 0:1], col_b[0]
                c1, c1b = col[:, 1:2], col_b[1]
                S.op("act", lambda e: e.activation(out=junk[:], in_=hbuf[:], func=AF.Square, accum_out=c0),
                     reads=[hbuf_b], writes=[junk_b, c0b])
                S.op("act", lambda e: e.activation(out=c1, in_=c0, func=AF.Sqrt, bias=EPS, scale=1.0 / D),
                     reads=[c0b], writes=[c1b])
                S.op("dve", lambda e: e.reciprocal(c1, c1), reads=[c1b], writes=[c1b])
                S.op("act", lambda e: e.mul(hs, hbuf[:], c1), reads=[hbuf_b, c1b], writes=[hs_b])
                for q in range(4):
                    pt, ptb = psum()
                    for j in range(4):
                        dc = q * 4 + j
                        S.op("pe", lambda e, pt=pt, j=j, dc=dc: e.transpose(
                            pt[:, j * 128:(j + 1) * 128], hs[:, dc * 128:(dc + 1) * 128], ident_f),
                            reads=[hs_b, cst_b], writes=[ptb])
                    nwv = normw[:, layer * 16 + q * 4: layer * 16 + q * 4 + 4].unsqueeze(2).broadcast_to([128, 4, 128])
                    S.op("dve", lambda e, pt=pt, q=q, tt=tt, nwv=nwv: e.tensor_tensor(
                        out=hnT[:, q * 4:(q + 1) * 4, tt * 128:(tt + 1) * 128],
                        in0=pt[:, :].rearrange("p (a b) -> p a b", a=4), in1=nwv, op=ALU.mult),
                        reads=[ptb, normw_b, tokA], writes=[hnT_b[tt]])

        def load_wo(wo_d):
            for c in range(12):
                S.op("pool", lambda e, c=c: e.dma_start(out=wo[:, c, :], in_=wo_d[c * 128:(c + 1) * 128, :]),
                     writes=[tokA, wo_b[c]] if c == 0 else [wo_b[c]], reads=[] if c == 0 else [tokA], dma=True)

        RG = [[0, 1], [2, 3], [4, 5], [6, 7]]

        def outproj_tile(tt):
            po, po_b = Fv[1][:, 0:D], F_b[1]
            for cb in range(4):
                pt, ptb = psum()
                for c in range(12):
                    S.op("pe", lambda e, pt=pt, c=c, cb=cb: e.matmul(
                        pt[:, :], ytile[:, c, :], wo[:, c, cb * 512:(cb + 1) * 512], start=(c == 0), stop=(c == 11)),
                        reads=[ytile_b, wo_b[c], tokA], writes=[ptb])
                if cb % 2 == 0:
                    S.op("act", lambda e, pt=pt, cb=cb: e.copy(po[:, cb * 512:(cb + 1) * 512], pt[:, :]),
                         reads=[ptb], writes=[po_b])
                else:
                    S.op("dve", lambda e, pt=pt, cb=cb: e.tensor_copy(po[:, cb * 512:(cb + 1) * 512], pt[:, :]),
                         reads=[ptb], writes=[po_b])
            S.op("sp", lambda e: e.dma_start(out=Pp[tt * 128:(tt + 1) * 128, :], in_=po),
                 reads=[po_b], writes=[Pp_b[tt]], dma=True)
            if tt % 4 == 3:
                q = tt // 4
                S.op("pool", lambda e: e.collective_compute(
                    "AllReduce", ALU.add, replica_groups=RG,
                    ins=[Pp[q * 512:(q + 1) * 512, :]], outs=[Ps[q * 512:(q + 1) * 512, :]]),
                    reads=Pp_b[q * 4:(q + 1) * 4], writes=[Ps_b[q]], cc=True)

        cu, cu_b = Fv[0][:, 0:2 + T], F_b[0]
        gate, gate_b = Fv[1][:, 0:T], F_b[1]
        acc, acc_b = Fv[2][:, 0:T], F_b[2]
        sccw = sb("sccw", [128, 12, 3])
        sccw_b = Buf("sccw")

        def sc_layer(li, layer, h_src, h_dst, delta):
            norm_phase(h_src, h_dst, delta, layer)
            S.op("sp", lambda e: e.dma_start(out=sccw[:], in_=sc_cw[li][:, :, :]), writes=[sccw_b], dma=True)
            for ct in range(12):
                wb, wbb = load_wblk(sc_w[li][ct])
                S.op("dve", lambda e: e.memset(cu[:, 0:2], 0.0), writes=[cu_b])
                for tg in range(4):
                    pp = []
                    for part in range(4):
                        pt, ptb = psum()
                        for dc in range(16):
                            S.op("pe", lambda e, pt=pt, dc=dc, part=part, tg=tg, wb=wb: e.matmul(
                                pt[:, :], wb[:, dc, part * 128:(part + 1) * 128], hnT[:, dc, tg * 512:(tg + 1) * 512],
                                start=(dc == 0), stop=(dc == 15)),
                                reads=[wbb, tokA] + hnT_b[tg * 4:(tg + 1) * 4], writes=[ptb])
                        pp.append((pt, ptb))
                    (pb_, pbb), (pc_, pcb), (pu_, pub), (pz_, pzb) = pp
                    ua, uab = SM[0], SM_b[0]
                    za, zab = SM[1], SM_b[1]
                    S.op("act", lambda e, pu_=pu_: e.copy(ua[:], pu_[:, :]), reads=[pub], writes=[uab])
                    S.op("dve", lambda e, pc_=pc_, tg=tg: e.tensor_tensor(
                        out=cu[:, 2 + tg * 512: 2 + (tg + 1) * 512], in0=pc_[:, :], in1=ua[:], op=ALU.mult),
                        reads=[pcb, uab], writes=[cu_b])
                    S.op("act", lambda e, pz_=pz_: e.activation(out=za[:], in_=pz_[:, :], func=AF.Silu),
                         reads=[pzb], writes=[zab])
                    S.op("dve", lambda e, pb_=pb_, tg=tg: e.tensor_tensor(
                        out=gate[:, tg * 512:(tg + 1) * 512], in0=pb_[:, :], in1=za[:], op=ALU.mult),
                        reads=[pbb, zab], writes=[gate_b])
                S.op("act", lambda e, ct=ct: e.mul(acc, cu[:, 2:2 + T], sccw[:, ct, 2:3]),
                     reads=[cu_b, sccw_b], writes=[acc_b])
                for i in (1, 0):
                    S.op("dve", lambda e, ct=ct, i=i: e.scalar_tensor_tensor(
                        out=acc, in0=cu[:, i:i + T], scalar=sccw[:, ct, i:i + 1], in1=acc,
                        op0=ALU.mult, op1=ALU.add), reads=[cu_b, sccw_b, acc_b], writes=[acc_b])
                yb, ybb = BT[ct % 2], BT_b[ct % 2]
                S.op("pool", lambda e, yb=yb: e.tensor_tensor(out=yb[:], in0=acc, in1=gate, op=ALU.mult),
                     reads=[acc_b, gate_b], writes=[ybb])
                S.op("sp", lambda e, yb=yb, ct=ct: e.dma_start(out=YT[ct * 128:(ct + 1) * 128, :], in_=yb[:]),
                     reads=[ybb], writes=[YT_b[ct]], dma=True)
            load_wo(sc_wo[li])
            for tt in range(NT):
                S.op("sp", lambda e, tt=tt: e.dma_start(
                    out=ytile[:], in_=YT[:, tt * 128:(tt + 1) * 128].rearrange("(c p) t -> p c t", p=128)),
                    reads=YT_b, writes=[ytile_b], dma=True)
                outproj_tile(tt)

        def final_phase(h_src, delta):
            fw, fw_b = Fv[0][:, 0:D], F_b[0]
            S.op("sp", lambda e: e.dma_start(out=fw, in_=fnw_d[0:1, :].broadcast_to([128, D])),
                 writes=[fw_b], dma=True)
            for tt in range(NT):
                load_h(h_src, None, tt, delta)
                c0, c0b = col[:, 0:1], col_b[0]
                c1, c1b = col[:, 1:2], col_b[1]
                S.op("act", lambda e: e.activation(out=junk[:], in_=hbuf[:], func=AF.Square, accum_out=c0),
                     reads=[hbuf_b], writes=[junk_b, c0b])
                S.op("act", lambda e: e.activation(out=c1, in_=c0, func=AF.Sqrt, bias=EPS, scale=1.0 / D),
                     reads=[c0b], writes=[c1b])
                S.op("dve", lambda e: e.reciprocal(c1, c1), reads=[c1b], writes=[c1b])
                S.op("dve", lambda e: e.scalar_tensor_tensor(
                    out=hs, in0=hbuf[:], scalar=c1, in1=fw, op0=ALU.mult, op1=ALU.mult),
                    reads=[hbuf_b, c1b, fw_b], writes=[hs_b])
                S.op("sp", lambda e, tt=tt: e.dma_start(out=out_d[tt * 128:(tt + 1) * 128, :], in_=hs),
                     reads=[hs_b], writes=[out_b[tt]], dma=True)

        HYB_TILES = {}

        def hyb_tiles():
            if HYB_TILES:
                return HYB_TILES
            d = HYB_TILES
            d["cw"] = sb("hcw_s", [128, 16, 4])
            d["wab"] = sb("wab_s", [128, 16, 16], BF16)
            d["bet"] = sb("bet", [128, 16, 8])
            d["gr"] = sb("gr", [128, 16, 8])
            d["bc16"] = sb("bc16", [128, 3, 8])
            d["dnw"] = sb("dnw", [128, 128])
            d["dn"] = sb("dnwork", [128, 12, 256])
            d["rb"] = sb("rbwork", [128, 22, 128], BF16)
            d["S32"] = sb("S32", [128, 2, 128])
            d["Sb"] = sb("Sbf", [128, 2, 128], BF16)
            d["qbt"] = sb("qbt", [128, 512], BF16)
            d["VA"] = sb("VA", [128, 3, 132], BF16)
            d["oev"] = sb("oev", [128, 2, 132])
            for k in list(d.keys()):
                d[k + "_b"] = Buf(k)
            d["dn_bs"] = bufs("dnw", 12)
            d["nn_bs"] = bufs("nn", 4)
            d["qo_bs"] = bufs("qo", 4)
            d["rb_bs"] = bufs("rb", 22)
            d["PTa"] = [sb(f"PTa{i}", [128, 256], BF16) for i in range(3)]
            d["PTa_bs"] = bufs("PTa", 3)
            d["OBw"] = []
            d["oc_b"] = Buf("oc")
            d["VA_bs"] = bufs("VA", 3)
            d["oev_bs"] = bufs("oev", 2)
            d["S_bs"] = bufs("S", 2)
            return d

        def hyb_layer(li, layer, h_src, h_dst, delta):
            d = hyb_tiles()
            norm_phase(h_src, h_dst, delta, layer)
            cw, cw_b = d["cw"], d["cw_b"]
            S.op("sp", lambda e: e.dma_start(out=cw[:], in_=hyb_cw[li][:, :, :]), writes=[cw_b], dma=True)
            wab, wab_b = d["wab"], d["wab_b"]
            S.op("pool", lambda e: e.dma_start(out=wab[:], in_=hyb_ab[li].rearrange("(c p) n -> p c n", p=128)),
                 writes=[wab_b], dma=True)
            bc16, bc16_b = d["bc16"], d["bc16_b"]
            S.op("sp", lambda e: e.dma_start(out=bc16[:, 0, :], in_=hyb_dt[li][0:1, :].broadcast_to([128, 8])),
                 writes=[bc16_b], dma=True)
            S.op("sp", lambda e: e.dma_start(out=bc16[:, 1, :], in_=hyb_al[li][0:1, :].broadcast_to([128, 8])),
                 reads=[bc16_b], writes=[bc16_b], dma=True)
            dnw, dnw_b = d["dnw"], d["dnw_b"]
            S.op("sp", lambda e: e.dma_start(out=dnw[:], in_=hyb_nw[li][0:1, :].broadcast_to([128, 128])),
                 writes=[dnw_b], dma=True)
            S.op("act", lambda e: e.activation(out=bc16[:, 1, :], in_=bc16[:, 1, :], func=AF.Exp),
                 reads=[bc16_b], writes=[bc16_b])
            S.op("dve", lambda e: e.tensor_scalar_mul(bc16[:, 1, :], bc16[:, 1, :], -1.0),
                 reads=[bc16_b], writes=[bc16_b])
            bet, bet_b, gr, gr_b = d["bet"], d["bet_b"], d["gr"], d["gr_b"]
            t1, t1b = SM[2], SM_b[2]
            t2, t2b = SM[3], SM_b[3]
            for tt in range(NT):
                pt, ptb = psum()
                for dc in range(16):
                    S.op("pe", lambda e, pt=pt, dc=dc, tt=tt: e.matmul(
                        pt[:, 0:16], hnT[:, dc, tt * 128:(tt + 1) * 128], wab[:, dc, :],
                        start=(dc == 0), stop=(dc == 15)), reads=[wab_b, hnT_b[tt], tokA], writes=[ptb])
                S.op("act", lambda e, pt=pt, tt=tt: e.activation(out=bet[:, tt, :], in_=pt[:, 0:8], func=AF.Sigmoid),
                     reads=[ptb], writes=[bet_b])
                S.op("dve", lambda e, pt=pt: e.tensor_tensor(out=t1[:, 0:8], in0=pt[:, 8:16], in1=bc16[:, 0, :],
                                                             op=ALU.add), reads=[ptb, bc16_b], writes=[t1b])
                S.op("act", lambda e: e.activation(out=t2[:, 0:8], in_=t1[:, 0:8], func=AF.Abs),
                     reads=[t1b], writes=[t2b])
                S.op("act", lambda e: e.activation(out=t2[:, 0:8], in_=t2[:, 0:8], func=AF.Exp, scale=-1.0),
                     reads=[t2b], writes=[t2b])
                S.op("act", lambda e: e.activation(out=t2[:, 0:8], in_=t2[:, 0:8], func=AF.Ln, bias=1.0),
                     reads=[t2b], writes=[t2b])
                S.op("dve", lambda e: e.scalar_tensor_tensor(out=t1[:, 0:8], in0=t1[:, 0:8], scalar=0.0,
                                                             in1=t2[:, 0:8], op0=ALU.max, op1=ALU.add),
                     reads=[t1b, t2b], writes=[t1b])
                S.op("dve", lambda e, tt=tt: e.tensor_tensor(out=gr[:, tt, :], in0=t1[:, 0:8], in1=bc16[:, 1, :],
                                                             op=ALU.mult), reads=[t1b, bc16_b], writes=[gr_b])
            for zb in range(3):
                wb, wbb = load_wblk(hyb_w[li][9 + zb])
                for tt in range(NT):
                    pt, ptb = psum()
                    for dc in range(16):
                        S.op("pe", lambda e, pt=pt, dc=dc, tt=tt, wb=wb: e.matmul(
                            pt[:, :], hnT[:, dc, tt * 128:(tt + 1) * 128], wb[:, dc, :],
                            start=(dc == 0), stop=(dc == 15)), reads=[wbb, hnT_b[tt], tokA], writes=[ptb])
                    zt, ztb = SM[tt % 2], SM_b[tt % 2]
                    S.op("act", lambda e, pt=pt, zt=zt: e.activation(out=zt[:], in_=pt[:, :], func=AF.Silu),
                         reads=[ptb], writes=[ztb])
                    S.op("sp", lambda e, zt=zt, tt=tt, zb=zb: e.dma_start(
                        out=ZS[tt * 128:(tt + 1) * 128, zb * 512:(zb + 1) * 512], in_=zt[:]),
                        reads=[ztb], writes=[ZS_b[tt]], dma=True)
            for g in range(NQK):
                dn_head(d, li, g)
            S.op("dve", lambda e: e.memset(d["VA"][:, :, 128:132], 1.0), writes=d["VA_bs"])
            for h in range(NSW):
                sw_head(d, li, h)
            load_wo(hyb_wo[li])
            for tt in range(NT):
                combine_tile(d, tt)
                if tt == NT - 1:
                    d["OBw"] = []
                outproj_tile(tt)

        def proj_cm(wb, wbb, part, evac):
            for tg in range(4):
                pt, ptb = psum()
                for dc in range(16):
                    S.op("pe", lambda e, pt=pt, dc=dc, tg=tg: e.matmul(
                        pt[:, :], wb[:, dc, part * 128:(part + 1) * 128], hnT[:, dc, tg * 512:(tg + 1) * 512],
                        start=(dc == 0), stop=(dc == 15)),
                        reads=[wbb, tokA] + hnT_b[tg * 4:(tg + 1) * 4], writes=[ptb])
                evac(tg, pt, ptb)

        def dn_head(d, li, g):
            cw, cw_b = d["cw"], d["cw_b"]
            wb, wbb = load_wblk(hyb_w[li][g])
            qs, qs_b = Fv[3][:, 0:T], F_b[3]
            accv, accv_b = Fv[2][:, 0:T], F_b[2]
            sq, sq_b = BT[5], BT_b[5]
            for part in range(4):
                raw, raw_b = Fv[part % 2], F_b[part % 2]
                S.op("dve", lambda e, raw=raw: e.memset(raw[:, 0:3], 0.0), writes=[raw_b])

                def ev(tg, pt, ptb, raw=raw, raw_b=raw_b):
                    S.op("act", lambda e: e.copy(raw[:, 3 + tg * 512: 3 + (tg + 1) * 512], pt[:, :]),
                         reads=[ptb], writes=[raw_b])
                proj_cm(wb, wbb, part, ev)
                tl = g * 4 + part
                S.op("act", lambda e, raw=raw, tl=tl: e.mul(accv, raw[:, 3:3 + T], cw[:, tl, 3:4]),
                     reads=[raw_b, cw_b], writes=[accv_b])
                for i in (2, 1, 0):
                    S.op("dve", lambda e, raw=raw, tl=tl, i=i: e.scalar_tensor_tensor(
                        out=accv, in0=raw[:, i:i + T], scalar=cw[:, tl, i:i + 1], in1=accv,
                        op0=ALU.mult, op1=ALU.add), reads=[raw_b, cw_b, accv_b], writes=[accv_b])
                if part < 2:
                    S.op("act", lambda e: e.activation(out=qs, in_=accv, func=AF.Silu), reads=[accv_b], writes=[qs_b])
                    S.op("act", lambda e: e.activation(out=sq[:], in_=qs, func=AF.Square), reads=[qs_b], writes=[sq_b])
                    for tg in range(4):
                        pt, ptb = psum()
                        S.op("pe", lambda e, pt=pt, tg=tg: e.matmul(pt[:, :], ones_b, sq[:, tg * 512:(tg + 1) * 512],
                                                                    start=True, stop=True),
                             reads=[sq_b, cstb_b], writes=[ptb])
                        rn, rnb = SM[tg % 2], SM_b[tg % 2]
                        S.op("act", lambda e, pt=pt, rn=rn: e.activation(out=rn[:], in_=pt[:, :], func=AF.Sqrt,
                                                                         bias=EPS, scale=1.0),
                             reads=[ptb], writes=[rnb])
                        S.op("dve", lambda e, rn=rn: e.reciprocal(rn[:], rn[:]), reads=[rnb], writes=[rnb])
                        sc_ = (128.0 ** -0.5) if part == 0 else 1.0
                        S.op("dve", lambda e, rn=rn, tg=tg, part=part, sc_=sc_: e.scalar_tensor_tensor(
                            out=BT[part][:, tg * 512:(tg + 1) * 512], in0=qs[:, tg * 512:(tg + 1) * 512], scalar=sc_,
                            in1=rn[:], op0=ALU.mult, op1=ALU.mult), reads=[qs_b, rnb], writes=[BT_b[part]])
                else:
                    S.op("act", lambda e, part=part: e.activation(out=BT[part][:], in_=accv, func=AF.Silu),
                         reads=[accv_b], writes=[BT_b[part]])
            qn, kn = BT[0], BT[1]
            qkb = [BT_b[0], BT_b[1]]
            dn, dn_bs, rb, rb_bs = d["dn"], d["dn_bs"], d["rb"], d["rb_bs"]
            S32, Sb, S_bs = d["S32"], d["Sb"], d["S_bs"]
            bet, bet_b, gr, gr_b = d["bet"], d["bet_b"], d["gr"], d["gr_b"]
            dnw, dnw_b = d["dnw"], d["dnw_b"]
            for e_ in range(2):
                S.op("dve", lambda e, e_=e_: e.memset(S32[:, e_, :], 0.0), writes=[S_bs[e_]])
                S.op("dve", lambda e, e_=e_: e.memset(Sb[:, e_, :], 0.0), reads=[S_bs[e_]], writes=[S_bs[e_]])
            def shared_pre(tt):
                par = tt % 2
                ts = slice(tt * 128, (tt + 1) * 128)
                ktok, ktok_b = rb[:, par, :], rb_bs[par]
                p16, p16b = psum16()
                S.op("pe", lambda e: e.transpose(p16, kn[:, ts], ident_b), reads=[qkb[1], cstb_b], writes=[p16b])
                S.op("act", lambda e: e.copy(ktok, p16), reads=[p16b], writes=[ktok_b])
                kkqk, kkqk_b = dn[:, 10 + par, :], dn_bs[10 + par]
                pt, ptb = psum()
                S.op("pe", lambda e: e.matmul(pt[:, 0:128], kn[:, ts], kn[:, ts], start=True, stop=True),
                     reads=[qkb[1]], writes=[ptb])
                S.op("pe", lambda e: e.matmul(pt[:, 128:256], kn[:, ts], qn[:, ts], start=True, stop=True),
                     reads=qkb, writes=[ptb])
                S.op("act", lambda e: e.copy(kkqk, pt[:, 0:256]), reads=[ptb], writes=[kkqk_b])
                yield

            def tiles(e_, tt):
                par = tt % 2
                hv = 2 * g + e_
                t = {}
                t["ts"] = slice(tt * 128, (tt + 1) * 128)
                t["ktok"], t["ktok_b"] = rb[:, par, :], rb_bs[par]
                t["kkqk"], t["kkqk_b"] = dn[:, 10 + par, :], dn_bs[10 + par]
                i = 2 + e_ * 2 + par
                t["vtok"], t["vtok_b"] = rb[:, i, :], rb_bs[i]
                for j, nm in enumerate(("PTt", "Kd", "TT")):
                    i = 6 + (e_ * 2 + par) * 3 + j
                    t[nm], t[nm + "_b"] = rb[:, i, :], rb_bs[i]
                for j, nm in enumerate(("Rt", "vnew")):
                    i = 18 + e_ * 2 + j
                    t[nm], t[nm + "_b"] = rb[:, i, :], rb_bs[i]
                c0 = 40 + (e_ * 2 + par) * 4
                for j, nm in enumerate(("ngc", "eg", "neg", "egl")):
                    t[nm], t[nm + "_b"] = col[:, c0 + j:c0 + j + 1], col_b[c0 + j]
                c0 = 56 + e_ * 2
                for j, nm in enumerate(("ss", "rs")):
                    t[nm], t[nm + "_b"] = col[:, c0 + j:c0 + j + 1], col_b[c0 + j]
                t["gcol"] = gr[:, tt, hv:hv + 1]
                t["bcol"] = bet[:, tt, hv:hv + 1]
                w0 = e_ * 5
                t["DT"], t["DTs"], t["DT_b"] = dn[:, w0, 0:128], dn[:, w0, 128:256], dn_bs[w0]
                t["AB"] = [dn[:, w0 + 1, :], dn[:, w0 + 2, :]]
                t["AB_b"] = [dn_bs[w0 + 1], dn_bs[w0 + 2]]
                t["Nn"] = [dn[:, w0 + 3, 0:128], dn[:, w0 + 3, 128:256]]
                t["Nn_b"] = [d["nn_bs"][e_ * 2], d["nn_bs"][e_ * 2 + 1]]
                t["QSs"], t["QSs_b"] = dn[:, w0 + 4, 0:128], d["qo_bs"][e_ * 2]
                t["osb"], t["osb_b"] = dn[:, w0 + 4, 128:256], d["qo_bs"][e_ * 2 + 1]
                t["hv"] = hv
                return t

            def pre(e_, tt):
                t = tiles(e_, tt)
                ts, hv = t["ts"], t["hv"]
                vT, vT_b = BT[2 + e_], BT_b[2 + e_]
                vtok, vtok_b = t["vtok"], t["vtok_b"]
                gcol, bcol = t["gcol"], t["bcol"]
                ngc, eg, neg, egl = t["ngc"], t["eg"], t["neg"], t["egl"]
                ngcb, egb, negb, eglb = t["ngc_b"], t["eg_b"], t["neg_b"], t["egl_b"]
                DT, DTs, DT_b, AB, AB_b, Nn, Nn_b = t["DT"], t["DTs"], t["DT_b"], t["AB"], t["AB_b"], t["Nn"], t["Nn_b"]
                kkqk, kkqk_b, ktok, ktok_b = t["kkqk"], t["kkqk_b"], t["ktok"], t["ktok_b"]
                PTt, PTt_b, Kd, Kd_b, TT, TT_b = t["PTt"], t["PTt_b"], t["Kd"], t["Kd_b"], t["TT"], t["TT_b"]
                tmp, tmp_b = AB[1][:, 0:128], AB_b[1]
                p16, p16b = psum16()
                S.op("pe", lambda e: e.transpose(p16, vT[:, ts], ident_b), reads=[vT_b, cstb_b], writes=[p16b])
                S.op("act", lambda e: e.copy(vtok, p16), reads=[p16b], writes=[vtok_b])
                pg, pgb = psum()
                S.op("pe", lambda e: e.matmul(pg[:, 0:128], gcol.broadcast_to([128, 128]), UT, start=True, stop=True),
                     reads=[gr_b, cst_b], writes=[pgb])
                S.op("pe", lambda e: e.matmul(pg[:, 128:129], UT, gcol, start=True, stop=True),
                     reads=[gr_b, cst_b], writes=[pgb])
                S.op("act", lambda e: e.mul(ngc, pg[:, 128:129], -1.0), reads=[pgb], writes=[ngcb])
                S.op("act", lambda e: e.activation(out=eg, in_=pg[:, 128:129], func=AF.Exp), reads=[pgb], writes=[egb])
                S.op("act", lambda e: e.activation(out=egl, in_=pg[:, 127:128], func=AF.Exp), reads=[pgb], writes=[eglb])
                S.op("dve", lambda e: e.tensor_tensor(out=tmp, in0=pg[:, 0:128], in1=MBT, op=ALU.add),
                     reads=[pgb, cst_b], writes=[tmp_b])
                S.op("dve", lambda e: e.tensor_scalar_mul(neg, eg, -1.0), reads=[egb], writes=[negb])
                yield
                S.op("act", lambda e: e.activation(out=DT, in_=tmp, func=AF.Exp, bias=ngc, scale=1.0),
                     reads=[tmp_b, ngcb], writes=[DT_b])
                S.op("pool", lambda e: e.tensor_tensor(out=DTs, in0=DT, in1=SMT, op=ALU.mult),
                     reads=[DT_b, cst_b], writes=[DT_b])
                S.op("dve", lambda e: e.scalar_tensor_tensor(
                    out=AB[0][:, 128:256], in0=kkqk[:, 0:128], scalar=bcol, in1=DTs, op0=ALU.mult, op1=ALU.mult),
                    reads=[kkqk_b, bet_b, DT_b], writes=[AB_b[0]])
                S.op("pool", lambda e: e.tensor_tensor(out=PTt, in0=kkqk[:, 128:256], in1=DT, op=ALU.mult),
                     reads=[kkqk_b, DT_b], writes=[PTt_b])
                S.op("act", lambda e: e.mul(Kd, ktok, DT[:, 127:128]), reads=[ktok_b, DT_b], writes=[Kd_b])
                yield
                pa0, pa0b = psum()
                S.op("pe", lambda e: e.transpose(pa0[:, 0:128], AB[0][:, 128:256], ident_f),
                     reads=[AB_b[0], cst_b], writes=[pa0b])
                S.op("act", lambda e: e.copy(AB[0][:, 0:128], pa0[:, 0:128]), reads=[pa0b], writes=[AB_b[0]])
                S.op("dve", lambda e: e.tensor_tensor(out=Nn[0], in0=ident_f, in1=AB[0][:, 128:256], op=ALU.subtract),
                     reads=[AB_b[0], cst_b], writes=[Nn_b[0]])
                yield
                for k in range(1, 7):
                    prv, nxt_ = AB[(k - 1) % 2], AB[k % 2]
                    prvb, nxtb = AB_b[(k - 1) % 2], AB_b[k % 2]
                    pa, pab = psum()
                    S.op("pe", lambda e, pa=pa, prv=prv: e.matmul(pa[:, 0:128], prv[:, 128:256], prv[:, 0:128],
                                                                  start=True, stop=True), reads=[prvb], writes=[pab])
                    if k < 6:
                        S.op("pe", lambda e, pa=pa, prv=prv: e.matmul(pa[:, 128:256], prv[:, 0:128], prv[:, 128:256],
                                                                      start=True, stop=True), reads=[prvb], writes=[pab])
                    S.op("act", lambda e, pa=pa, nxt_=nxt_: e.copy(nxt_[:, 0:256], pa[:, 0:256]), reads=[pab],
                         writes=[nxtb])
                    yield
                    npv, npvb = Nn[(k - 1) % 2], Nn_b[(k - 1) % 2]
                    pn, pnb = psum()
                    S.op("pe", lambda e, pn=pn, nxt_=nxt_, npv=npv: e.matmul(pn[:, 0:128], nxt_[:, 0:128], npv,
                                                                             start=True, stop=True),
                         reads=[nxtb, npvb], writes=[pnb])
                    if k < 6:
                        nnx, nnxb = Nn[k % 2], Nn_b[k % 2]
                    else:
                        nnx, nnxb = TT, TT_b
                    S.op("dve", lambda e, pn=pn, npv=npv, nnx=nnx: e.tensor_tensor(out=nnx, in0=pn[:, 0:128],
                                                                                   in1=npv, op=ALU.add),
                         reads=[pnb, npvb], writes=[nnxb])
                    yield

            def scan(e_, tt):
                t = tiles(e_, tt)
                ts, hv = t["ts"], t["hv"]
                vtok, vtok_b, bcol = t["vtok"], t["vtok_b"], t["bcol"]
                eg, neg, egl, ss, rs = t["eg"], t["neg"], t["egl"], t["ss"], t["rs"]
                egb, negb, eglb, ssb, rsb = t["eg_b"], t["neg_b"], t["egl_b"], t["ss_b"], t["rs_b"]
                PTt, PTt_b, Kd, Kd_b, TT, TT_b = t["PTt"], t["PTt_b"], t["Kd"], t["Kd_b"], t["TT"], t["TT_b"]
                Rt, Rt_b, vnew, vnew_b = t["Rt"], t["Rt_b"], t["vnew"], t["vnew_b"]
                QSs, QSs_b, osb, osb_b = t["QSs"], t["QSs_b"], t["osb"], t["osb_b"]
                Sbe = Sb[:, e_, :]
                S32e = S32[:, e_, :]
                p1, p1b = psum()
                S.op("pe", lambda e: e.matmul(p1[:, 0:128], kn[:, ts], Sbe, start=True, stop=True),
                     reads=[qkb[1], S_bs[e_]], writes=[p1b])
                S.op("pe", lambda e: e.matmul(p1[:, 128:256], qn[:, ts], Sbe, start=True, stop=True),
                     reads=[qkb[0], S_bs[e_]], writes=[p1b])
                S.op("dve", lambda e: e.scalar_tensor_tensor(
                    out=Rt, in0=p1[:, 0:128], scalar=neg, in1=vtok, op0=ALU.mult, op1=ALU.add),
                    reads=[p1b, negb, vtok_b], writes=[Rt_b])
                S.op("act", lambda e: e.mul(QSs, p1[:, 128:256], eg), reads=[p1b, egb], writes=[QSs_b])
                yield
                p2, p2b = psum()
                S.op("pe", lambda e: e.matmul(p2[:, 0:128], TT, Rt, start=True, stop=True),
                     reads=[TT_b, Rt_b], writes=[p2b])
                S.op("act", lambda e: e.mul(vnew, p2[:, 0:128], bcol), reads=[p2b, bet_b], writes=[vnew_b])
                yield
                p3, p3b = psum()
                S.op("pe", lambda e: e.matmul(p3[:, 0:128], PTt, vnew, start=True, stop=True),
                     reads=[PTt_b, vnew_b], writes=[p3b])
                S.op("pe", lambda e: e.matmul(p3[:, 128:256], Kd, vnew, start=True, stop=True),
                     reads=[Kd_b, vnew_b], writes=[p3b])
                S.op("dve", lambda e: e.scalar_tensor_tensor(
                    out=S32e, in0=S32e, scalar=egl, in1=p3[:, 128:256], op0=ALU.mult, op1=ALU.add),
                    reads=[p3b, eglb, S_bs[e_]], writes=[S_bs[e_]])
                S.op("act", lambda e: e.copy(Sbe, S32e), reads=[S_bs[e_]], writes=[S_bs[e_]])
                S.op("dve", lambda e: e.tensor_tensor(out=osb, in0=p3[:, 0:128], in1=QSs, op=ALU.add),
                     reads=[p3b, QSs_b], writes=[osb_b])
                yield
                jk, jk_b = SM[2 + e_][:, 0:128], SM_b[2 + e_]
                S.op("act", lambda e: e.activation(out=jk, in_=osb, func=AF.Square, accum_out=ss),
                     reads=[osb_b], writes=[jk_b, ssb])
                S.op("act", lambda e: e.activation(out=rs, in_=ss, func=AF.Sqrt, bias=EPS, scale=1.0 / 128),
                     reads=[ssb], writes=[rsb])
                S.op("dve", lambda e: e.reciprocal(rs, rs), reads=[rsb], writes=[rsb])
                S.op("dve", lambda e: e.scalar_tensor_tensor(
                    out=osb, in0=osb, scalar=rs, in1=dnw[:], op0=ALU.mult, op1=ALU.mult),
                    reads=[osb_b, rsb, dnw_b], writes=[osb_b])
                S.op("sp", lambda e: e.dma_start(out=OA[tt * 128:(tt + 1) * 128, hv * 128:(hv + 1) * 128], in_=osb),
                     reads=[osb_b], writes=[OA_b[tt]], dma=True)
                yield

            def interleave(gens):
                gens = list(gens)
                while gens:
                    for gen in list(gens):
                        try:
                            next(gen)
                        except StopIteration:
                            gens.remove(gen)

            interleave([shared_pre(0)])
            interleave([pre(0, 0), pre(1, 0)])
            for tt in range(NT):
                gl = [scan(0, tt), scan(1, tt)]
                if tt + 1 < NT:
                    interleave([shared_pre(tt + 1)])
                    gl += [pre(0, tt + 1), pre(1, tt + 1)]
                interleave(gl)

        def sw_head(d, li, h):
            qbt, qbt_b = d["qbt"], d["qbt_b"]
            wb, wbb = load_wblk(hyb_w[li][4 + h])
            for tg in range(4):
                S.op("sp", lambda e, tg=tg: e.dma_start(out=SM[2][:], in_=rope_d[:, 0, tg * 512:(tg + 1) * 512]),
                     writes=[SM_b[2]], dma=True)
                S.op("sp", lambda e, tg=tg: e.dma_start(out=SM[3][:], in_=rope_d[:, 1, tg * 512:(tg + 1) * 512]),
                     writes=[SM_b[3]], dma=True)
                for part in range(4):
                    pt, ptb = psum()
                    for dc in range(16):
                        S.op("pe", lambda e, pt=pt, dc=dc, tg=tg, part=part: e.matmul(
                            pt[:, :], wb[:, dc, part * 128:(part + 1) * 128], hnT[:, dc, tg * 512:(tg + 1) * 512],
                            start=(dc == 0), stop=(dc == 15)),
                            reads=[wbb, tokA] + hnT_b[tg * 4:(tg + 1) * 4], writes=[ptb])
                    S.op("act", lambda e, pt=pt: e.copy(qbt[:], pt[:, :]), reads=[ptb], writes=[qbt_b])
                    pp, ppb = psum()
                    S.op("pe", lambda e, pp=pp: e.matmul(pp[:, :], permT_b, qbt[:], start=True, stop=True),
                         reads=[qbt_b, cstb_b], writes=[ppb])
                    S.op("dve", lambda e, pt=pt: e.tensor_tensor(out=SM[0][:], in0=pt[:, :], in1=SM[2][:],
                                                                 op=ALU.mult),
                         reads=[ptb, SM_b[2]], writes=[SM_b[0]])
                    S.op("dve", lambda e, pp=pp: e.tensor_tensor(out=SM[1][:], in0=pp[:, :], in1=SM[3][:],
                                                                 op=ALU.mult),
                         reads=[ppb, SM_b[3]], writes=[SM_b[1]])
                    S.op("pool", lambda e, part=part, tg=tg: e.tensor_tensor(
                        out=BT[part][:, tg * 512:(tg + 1) * 512], in0=SM[0][:], in1=SM[1][:], op=ALU.add),
                        reads=[SM_b[0], SM_b[1]], writes=[BT_b[part]])
            wv, wvb = load_wblk(hyb_w[li][8])

            def evv(tg, pt, ptb):
                S.op("act", lambda e: e.copy(BT[4][:, tg * 512:(tg + 1) * 512], pt[:, :]), reads=[ptb], writes=[BT_b[4]])
            proj_cm(wv, wvb, h % 4, evv)
            sq, sq_b = BT[5], BT_b[5]
            kcol, kcol_b = col[0:1, 24:28], col_b[24]
            kmx, kmx_b = col[0:1, 28:29], col_b[28]
            S.op("act", lambda e: e.activation(out=sq[:], in_=BT[3][:], func=AF.Square), reads=[BT_b[3]], writes=[sq_b])
            for tg in range(4):
                pk, pkb = psum()
                S.op("pe", lambda e, pk=pk, tg=tg: e.matmul(pk[0:1, :], ones_b[:, 0:1], sq[:, tg * 512:(tg + 1) * 512],
                                                            start=True, stop=True), reads=[sq_b, cstb_b], writes=[pkb])
                S.op("dve", lambda e, pk=pk, tg=tg: e.reduce_max(out=col[0:1, 24 + tg:25 + tg], in_=pk[0:1, :],
                                                                 axis=mybir.AxisListType.X),
                     reads=[pkb], writes=[kcol_b])
            S.op("dve", lambda e: e.reduce_max(out=kmx, in_=kcol, axis=mybir.AxisListType.X), reads=[kcol_b],
                 writes=[kmx_b])
            rowf, rowf_b = SM[2], SM_b[2]
            for tg in range(4):
                pr, prb = psum()
                for gi in range(3):
                    S.op("act", lambda e, gi=gi, tg=tg: e.activation(out=qbt[:], in_=BT[gi][:, tg * 512:(tg + 1) * 512],
                                                                      func=AF.Square), reads=[BT_b[gi]], writes=[qbt_b])
                    S.op("pe", lambda e, pr=pr, gi=gi: e.matmul(pr[0:1, :], ones_b[:, 0:1], qbt[:], start=(gi == 0),
                                                                stop=(gi == 2)), reads=[qbt_b, cstb_b], writes=[prb])
                S.op("act", lambda e, pr=pr: e.activation(out=rowf[0:1, :], in_=pr[0:1, :], func=AF.Sqrt, scale=kmx),
                     reads=[prb, kmx_b], writes=[rowf_b])
                S.op("dve", lambda e, tg=tg: e.tensor_scalar_mul(sq[0:1, tg * 512:(tg + 1) * 512], rowf[0:1, :], -1.0),
                     reads=[rowf_b, sq_b], writes=[sq_b])
            negc = sq
            VA, VA_bs, oev, oev_bs, rb, rb_bs = d["VA"], d["VA_bs"], d["oev"], d["oev_bs"], d["rb"], d["rb_bs"]
            it = 0
            for (dil, gi) in SWG:
                L = T // dil
                nb = L // 128
                Qg, Qg_b = BT[gi], BT_b[gi]
                for r in range(dil):
                    acc_ps = [None, None]
                    for m in range(nb):
                        k0 = r + dil * 128 * m
                        ksl = slice(k0, k0 + dil * 127 + 1, dil)
                        nq = 2 if m + 1 < nb else 1
                        qsl = slice(k0, k0 + dil * (128 * nq - 1) + 1, dil)
                        N = 128 * nq
                        psc, pscb = psum()
                        S.op("pe", lambda e, psc=psc, ksl=ksl, qsl=qsl, N=N, Qg=Qg: e.matmul(
                            psc[:, 0:N], BT[3][:, ksl], Qg[:, qsl], start=True, stop=False),
                            reads=[BT_b[3], Qg_b], writes=[pscb])
                        S.op("pe", lambda e, psc=psc, qsl=qsl, N=N: e.matmul(
                            psc[:, 0:N], ones_b[0:1, :], negc[0:1, qsl], start=False, stop=False),
                            reads=[sq_b, cstb_b], writes=[pscb])
                        S.op("pe", lambda e, psc=psc, N=N: e.matmul(
                            psc[:, 0:N], ident_b, cstb[:, 3:3 + N // 128, :], start=False, stop=True),
                            reads=[cstb_b], writes=[pscb])
                        PTa = d["PTa"][it % 3]
                        PTa_b = d["PTa_bs"][it % 3]
                        S.op("act", lambda e, psc=psc, N=N, PTa=PTa: e.activation(
                            out=PTa[:, 0:N], in_=psc[:, 0:N], func=AF.Exp, scale=128.0 ** -0.5),
                            reads=[pscb], writes=[PTa_b])
                        va, va_b = VA[:, it % 3, :], VA_bs[it % 3]
                        p16, p16b = psum16()
                        S.op("pe", lambda e, p16=p16, ksl=ksl: e.transpose(p16, BT[4][:, ksl], ident_b),
                             reads=[BT_b[4], cstb_b], writes=[p16b])
                        S.op("dve", lambda e, p16=p16, va=va: e.tensor_copy(va[:, 0:128], p16), reads=[p16b],
                             writes=[va_b])
                        pa, pab = ps_f[ACC0 + m % 2], ps_b[ACC0 + m % 2]
                        S.op("pe", lambda e, pa=pa, PTa=PTa, va=va, m=m: e.matmul(
                            pa[:, 0:129], PTa[:, 0:128], va[:, 0:129], start=(m == 0), stop=True),
                            reads=[PTa_b, va_b], writes=[pab])
                        if nq == 2:
                            pn_, pnb_ = ps_f[ACC0 + (m + 1) % 2], ps_b[ACC0 + (m + 1) % 2]
                        ov, ov_b = oev[:, it % 2, :], oev_bs[it % 2]
                        S.op("act", lambda e, pa=pa, ov=ov: e.copy(ov[:, 0:129], pa[:, 0:129]), reads=[pab],
                             writes=[ov_b])
                        tsl = slice(k0, k0 + dil * 127 + 1, dil)
                        obw = Buf("obw")
                        d["OBw"].append(obw)
                        S.op("sp", lambda e, ov=ov, tsl=tsl, gi=gi: e.dma_start(
                            out=OB[gi, tsl, h, 0:129], in_=ov[:, 0:129]),
                            reads=[ov_b], writes=[obw], dma=True)
                        if nq == 2:
                            S.op("pe", lambda e, pn_=pn_, PTa=PTa, va=va: e.matmul(
                                pn_[:, 0:129], PTa[:, 128:256], va[:, 0:129], start=True, stop=False),
                                reads=[PTa_b, va_b], writes=[pnb_])
                        it += 1

        def combine_tile(d, tt):
            ts = slice(tt * 128, (tt + 1) * 128)
            ych, ych_b = BT[5][:, 0:512], BT_b[5]
            for ck in range(3):
                zc, zc_b = Fv[0][:, 0:512], F_b[0]
                S.op("sp", lambda e, ck=ck: e.dma_start(out=zc, in_=ZS[ts, ck * 512:(ck + 1) * 512]),
                     reads=[ZS_b[tt]], writes=[zc_b], dma=True)
                oc, oc_b = Fv[0][:, 512:1024], d["oc_b"]
                if ck < 2:
                    S.op("sp", lambda e, ck=ck: e.dma_start(out=oc, in_=OA[ts, ck * 512:(ck + 1) * 512]),
                         reads=[OA_b[tt]], writes=[oc_b], dma=True)
                else:
                    ob, ob_b = FW[:, 4104:4104 + 3 * 516].rearrange("p (g x) -> p g x", g=3), F_b[2]
                    S.op("sp", lambda e: e.dma_start(out=ob, in_=OB[:, ts, :, :].rearrange("g t h x -> t g (h x)")),
                         reads=list(d["OBw"]), writes=[ob_b], dma=True)
                    S.op("dve", lambda e: e.tensor_tensor(out=ob[:, 0, :], in0=ob[:, 0, :], in1=ob[:, 1, :], op=ALU.add),
                         reads=[ob_b], writes=[ob_b])
                    S.op("dve", lambda e: e.tensor_tensor(out=ob[:, 0, :], in0=ob[:, 0, :], in1=ob[:, 2, :], op=ALU.add),
                         reads=[ob_b], writes=[ob_b])
                    o3 = ob[:, 0, :].rearrange("p (h x) -> p h x", h=NSW)
                    rd, rd_b = col[:, 32:32 + NSW], col_b[32]
                    S.op("dve", lambda e: e.reciprocal(rd, o3[:, :, 128]), reads=[ob_b], writes=[rd_b])
                    S.op("dve", lambda e: e.tensor_tensor(
                        out=oc.rearrange("p (h x) -> p h x", h=NSW), in0=o3[:, :, 0:128],
                        in1=rd.unsqueeze(2).broadcast_to([128, NSW, 128]), op=ALU.mult),
                        reads=[ob_b, rd_b], writes=[oc_b])
                S.op("dve", lambda e: e.tensor_tensor(out=ych, in0=oc, in1=zc, op=ALU.mult),
                     reads=[oc_b, zc_b], writes=[ych_b])
                p16, p16b = psum16(full=True)
                for j in range(4):
                    S.op("pe", lambda e, p16=p16, j=j: e.transpose(p16[:, j * 128:(j + 1) * 128],
                                                                   ych[:, j * 128:(j + 1) * 128], ident_b),
                         reads=[ych_b, cstb_b], writes=[p16b])
                S.op("act", lambda e, p16=p16, ck=ck: e.copy(
                    ytile[:, ck * 4:(ck + 1) * 4, :], p16[:, 0:512].rearrange("p (a b) -> p a b", a=4)),
                    reads=[p16b], writes=[ytile_b])

        for hh in H:
            Hb[id(hh)] = bufs("H", NT)
        Hb[id(x_d)] = bufs("x", NT)
        cur = x_d
        nxt = 0
        delta = False
        for layer in layers:
            dst = H[nxt] if delta else None
            if layer % 2 == 0:
                hyb_layer(layer // 2, layer, cur, dst, delta)
            else:
                sc_layer(layer // 2, layer, cur, dst, delta)
            if delta:
                cur = dst
                nxt ^= 1
            delta = True
        if final_norm:
            final_phase(cur, delta)
        else:
            for tt in range(NT):
                load_h(cur, None, tt, delta)
                S.op("sp", lambda e, tt=tt: e.dma_start(out=out_d[tt * 128:(tt + 1) * 128, :], in_=hbuf[:]),
                     reads=[hbuf_b], writes=[out_b[tt]], dma=True)
        S.op("sp", None, reads=out_b)
        S.emit(nc, st)
    return nc


def _consts():
    idx = np.arange(128)
    ident = np.eye(128, dtype=np.float32)
    UT = (idx[:, None] <= idx[None, :]).astype(np.float32)
    MBT = np.where(idx[None, :] >= idx[:, None], 0.0, NEG).astype(np.float32)
    SMT = (idx[None, :] > idx[:, None]).astype(np.float32)
    permT = np.zeros((128, 128), np.float32)
    for m in range(16):
        permT[m + 16, m] = 1.0
        permT[m, m + 16] = 1.0
    ones = np.ones((128, 128), np.float32)
    mcur = np.where(idx[:, None] <= idx[None, :], 0.0, NEG).astype(np.float32)
    mnext = np.where(idx[:, None] >= idx[None, :], 0.0, NEG).astype(np.float32)
    c = np.stack([ident, UT, MBT, SMT, ident, permT, ones, mcur, mnext], axis=1)
    half = 16
    inv = np.power(np.float32(500000.0), -np.arange(half, dtype=np.float32) * np.float32(2.0) / np.float32(32)).astype(np.float32)
    ang = np.arange(T, dtype=np.float32)[None, :] * inv[:, None]
    cos = np.cos(ang).astype(np.float32)
    sin = np.sin(ang).astype(np.float32)
    C = np.ones((128, T), np.float32)
    Sg = np.zeros((128, T), np.float32)
    C[0:16] = cos
    C[16:32] = cos
    Sg[0:16] = -sin
    Sg[16:32] = sin
    rope = np.stack([C, Sg], axis=1)
    return np.ascontiguousarray(c), np.ascontiguousarray(rope)


def _pack_hyb(w_in, j):
    blocks = []
    for g in range(4):
        gq = 4 * j + g
        cols = np.concatenate([np.arange(gq * 128, (gq + 1) * 128), 1024 + np.arange(gq * 128, (gq + 1) * 128),
                               2048 + np.arange(2 * gq * 128, (2 * gq + 2) * 128)])
        blocks.append(w_in[:, cols])
    for h in range(4):
        hq = 4 * j + h
        cols = np.concatenate([6176 + (gi * 8 + hq) * 128 + np.arange(128) for gi in range(3)] +
                              [9248 + hq * 128 + np.arange(128)])
        blocks.append(w_in[:, cols])
    blocks.append(w_in[:, 10272 + j * 512: 10272 + (j + 1) * 512])
    for k in range(2):
        blocks.append(w_in[:, 4096 + j * 1024 + k * 512: 4096 + j * 1024 + (k + 1) * 512])
    blocks.append(w_in[:, 11296 + j * 512: 11296 + (j + 1) * 512])
    return np.ascontiguousarray(np.stack(blocks, axis=0))


def _pack_hcw(cw, j):
    out = np.zeros((128, 16, 4), np.float32)
    for g in range(4):
        gq = 4 * j + g
        out[:, g * 4 + 0] = cw[gq * 128:(gq + 1) * 128]
        out[:, g * 4 + 1] = cw[1024 + gq * 128: 1024 + (gq + 1) * 128]
        out[:, g * 4 + 2] = cw[2048 + 2 * gq * 128: 2048 + (2 * gq + 1) * 128]
        out[:, g * 4 + 3] = cw[2048 + (2 * gq + 1) * 128: 2048 + (2 * gq + 2) * 128]
    return out


def _pack_sc(w_in, j):
    blocks = []
    for ct in range(12):
        cg = 12 * j + ct
        cols = np.concatenate([p * 3072 + cg * 128 + np.arange(128) for p in range(4)])
        blocks.append(w_in[:, cols])
    return np.ascontiguousarray(np.stack(blocks, axis=0))


_NC_CACHE = {}


def make_in_maps(x, norm_w, hyb_w_in, dn_conv_w, dn_a_log, dn_dt_bias, dn_norm_w, hyb_w_out,
                 sc_w_in, sc_conv_w, sc_w_out, final_norm_w):
    f = lambda a: np.ascontiguousarray(np.asarray(a, dtype=np.float32))
    consts, rope = _consts()
    shared = {
        "normw": f(np.asarray(norm_w).reshape(4, 16, 128).transpose(2, 0, 1).reshape(128, 64)),
        "fnw": f(np.asarray(final_norm_w).reshape(1, D)),
        "consts": consts, "rope": rope,
    }
    halves = []
    for j in range(2):
        m = dict(shared)
        for i in range(2):
            w_in = np.asarray(hyb_w_in[i])
            m[f"hw{i}"] = _pack_hyb(w_in, j)
            m[f"hab{i}"] = f(np.concatenate([w_in[:, 6144 + 8 * j: 6144 + 8 * j + 8],
                                             w_in[:, 6160 + 8 * j: 6160 + 8 * j + 8]], axis=1))
            m[f"hcw{i}"] = _pack_hcw(np.asarray(dn_conv_w[i]), j)
            m[f"hal{i}"] = f(np.asarray(dn_a_log[i])[8 * j: 8 * j + 8].reshape(1, 8))
            m[f"hdt{i}"] = f(np.asarray(dn_dt_bias[i])[8 * j: 8 * j + 8].reshape(1, 8))
            m[f"hnw{i}"] = f(np.asarray(dn_norm_w[i]).reshape(1, 128))
            wo_ = np.asarray(hyb_w_out[i])
            m[f"hwo{i}"] = f(np.concatenate([wo_[1024 * j: 1024 * (j + 1)], wo_[2048 + 512 * j: 2048 + 512 * (j + 1)]], axis=0))
            m[f"sw{i}"] = _pack_sc(np.asarray(sc_w_in[i]), j)
            m[f"scw{i}"] = f(np.asarray(sc_conv_w[i])[1536 * j: 1536 * (j + 1)].reshape(12, 128, 3).transpose(1, 0, 2))
            m[f"swo{i}"] = f(np.asarray(sc_w_out[i])[1536 * j: 1536 * (j + 1)])
        halves.append(m)
    maps = []
    for c in range(8):
        m = dict(halves[c % 2])
        m["x"] = f(np.asarray(x)[c // 2])
        maps.append(m)
    return maps


def kernel(x, norm_w, hyb_w_in, dn_conv_w, dn_a_log, dn_dt_bias, dn_norm_w, hyb_w_out,
           sc_w_in, sc_conv_w, sc_w_out, final_norm_w):
    maps = make_in_maps(x, norm_w, hyb_w_in, dn_conv_w, dn_a_log, dn_dt_bias, dn_norm_w, hyb_w_out,
                        sc_w_in, sc_conv_w, sc_w_out, final_norm_w)
    if "nc" not in _NC_CACHE:
        _NC_CACHE["nc"] = build_program()
    res = run_bass_kernel_spmd(_NC_CACHE["nc"], maps, core_ids=list(range(8)))
    out = np.stack([np.asarray(res.results[2 * b]["out"], dtype=np.float32) for b in range(4)], axis=0)
    return out
```

```python
import numpy as np
from contextlib import ExitStack
import concourse.bass as bass
import concourse.mybir as mybir
from concourse.bass_utils import run_bass_kernel_spmd

F32 = mybir.dt.float32
BF16 = mybir.dt.bfloat16
AF = mybir.ActivationFunctionType
ALU = mybir.AluOpType

T = 2048
D = 2048
NT = 16
EPS = 1e-6
NEG = -30000.0

EPOCH = 12000
DMA_SLOTS = 8


class Buf:
    __slots__ = ("name", "w", "rs", "excl")

    def __init__(self, name="", excl=False):
        self.name = name
        self.w = None
        self.rs = []
        self.excl = excl


class Op:
    __slots__ = ("stream", "fn", "deps", "dma", "flagged", "fidx", "slot", "use", "n", "cc")

    def __init__(self, stream, fn, dma):
        self.stream = stream
        self.fn = fn
        self.dma = dma
        self.deps = []
        self.flagged = False
        self.fidx = 0
        self.slot = 0
        self.use = 0
        self.n = 0
        self.cc = False


class Sched:
    STREAMS = ("pe", "dve", "act", "pool", "sp")

    def __init__(self):
        self.ops = {s: [] for s in self.STREAMS}
        self.ndma = {s: 0 for s in self.STREAMS}
        self.nops = 0

    def op(self, stream, fn, reads=(), writes=(), dma=False, cc=False):
        o = Op(stream, fn, dma or cc)
        o.cc = cc
        o.n = self.nops
        self.nops += 1
        ex = [b for b in reads if b.excl]
        if ex:
            reads = [b for b in reads if not b.excl]
            writes = list(writes) + ex
        deps = {}
        for b in reads:
            if b.w is not None:
                deps[id(b.w)] = b.w
        for b in writes:
            if b.w is not None:
                deps[id(b.w)] = b.w
            for r in b.rs:
                deps[id(r)] = r
        best = {}
        for d in deps.values():
            if d is o:
                continue
            if d.dma:
                o.deps.append(d)
                continue
            if d.stream == stream and stream == "pe" and not dma:
                continue
            cur = best.get(d.stream)
            if cur is None or d.n > cur.n:
                best[d.stream] = d
        for d in best.values():
            o.deps.append(d)
            d.flagged = True
        for b in reads:
            b.rs.append(o)
        for b in writes:
            b.w = o
            b.rs = []
        if cc:
            self.ncc = getattr(self, "ncc", 0) + 1
            o.use = self.ncc
        elif dma:
            n = self.ndma[stream]
            self.ndma[stream] = n + 1
            o.slot = n % DMA_SLOTS
            o.use = n // DMA_SLOTS + 1
        self.ops[stream].append(o)
        return o

    def emit(self, nc, stack):
        nsem = {}
        for s in self.STREAMS:
            c = 0
            for o in self.ops[s]:
                if o.flagged and not o.dma:
                    c += 1
                    o.fidx = c
            nsem[s] = (max(c - 1, 0) // EPOCH) + 1
        csem = {}
        for s in self.STREAMS:
            for e in range(nsem[s]):
                csem[(s, e)] = stack.enter_context(nc.semaphore(f"c_{s}_{e}"))
        dsem = {}
        for s in self.STREAMS:
            if self.ndma[s] > 0:
                for k in range(DMA_SLOTS):
                    dsem[(s, k)] = stack.enter_context(nc.semaphore(f"d_{s}_{k}"))
        ccsem = stack.enter_context(nc.semaphore("ccsem")) if getattr(self, "ncc", 0) else None
        block = stack.enter_context(nc.Block())

        def run_stream(s, eng):
            waited = {}
            for o in self.ops[s]:
                need = {}
                for d in o.deps:
                    if d.cc:
                        key = ("cc", 0, 0)
                        val = d.use
                    elif d.dma:
                        key = ("d", d.stream, d.slot)
                        val = 16 * d.use
                    else:
                        e = (d.fidx - 1) // EPOCH
                        key = ("c", d.stream, e)
                        val = d.fidx - e * EPOCH
                    if need.get(key, 0) < val:
                        need[key] = val
                if o.cc:
                    if o.use > 1:
                        need[("cc", 0, 0)] = max(need.get(("cc", 0, 0), 0), o.use - 1)
                elif o.dma and o.use > 1:
                    key = ("d", s, o.slot)
                    val = 16 * (o.use - 1)
                    if need.get(key, 0) < val:
                        need[key] = val
                for key, val in need.items():
                    if waited.get(key, 0) >= val:
                        continue
                    waited[key] = val
                    sem = ccsem if key[0] == "cc" else (dsem[(key[1], key[2])] if key[0] == "d" else csem[(key[1], key[2])])
                    eng.wait_ge(sem, val)
                if o.fn is None:
                    continue
                ins = o.fn(eng)
                if o.cc:
                    ins.then_inc(ccsem, 1)
                elif o.dma:
                    ins.then_inc(dsem[(s, o.slot)], 16)
                elif o.flagged:
                    e = (o.fidx - 1) // EPOCH
                    ins.then_inc(csem[(s, e)], 1)

        if self.ops["pe"]:
            block.tensor(lambda eng: run_stream("pe", eng))
        if self.ops["dve"]:
            block.vector(lambda eng: run_stream("dve", eng))
        if self.ops["act"]:
            block.scalar(lambda eng: run_stream("act", eng))
        if self.ops["pool"]:
            block.gpsimd(lambda eng: run_stream("pool", eng))
        if self.ops["sp"]:
            block.sync(lambda eng: run_stream("sp", eng))


NQK = 4
NV = 8
NSW = 4
SWG = ((1, 0), (4, 1), (16, 2))


def build_program(layers=(0, 1, 2, 3), final_norm=True):
    nc = bass.Bass("TRN2", target_bir_lowering=False)

    def din(name, shape, dt=F32):
        return nc.dram_tensor(name, list(shape), dt, kind="ExternalInput").ap()

    def dscr(name, shape, dt=F32):
        return nc.dram_tensor(name, list(shape), dt).ap()

    x_d = din("x", [T, D])
    normw_d = din("normw", [128, 64])
    fnw_d = din("fnw", [1, D])
    consts_d = din("consts", [128, 9, 128])
    rope_d = din("rope", [128, 2, T])
    hyb_w = [din(f"hw{i}", [12, 128, 16, 512]) for i in range(2)]
    hyb_ab = [din(f"hab{i}", [D, 16]) for i in range(2)]
    hyb_cw = [din(f"hcw{i}", [128, 16, 4]) for i in range(2)]
    hyb_al = [din(f"hal{i}", [1, 8]) for i in range(2)]
    hyb_dt = [din(f"hdt{i}", [1, 8]) for i in range(2)]
    hyb_nw = [din(f"hnw{i}", [1, 128]) for i in range(2)]
    hyb_wo = [din(f"hwo{i}", [1536, D]) for i in range(2)]
    sc_w = [din(f"sw{i}", [12, 128, 16, 512]) for i in range(2)]
    sc_cw = [din(f"scw{i}", [128, 12, 3]) for i in range(2)]
    sc_wo = [din(f"swo{i}", [1536, D]) for i in range(2)]
    out_d = nc.dram_tensor("out", [T, D], F32, kind="ExternalOutput").ap()

    H = [dscr("Hs0", [T, D]), dscr("Hs1", [T, D])]
    YT = dscr("YT", [1536, T], BF16)
    ZS = dscr("ZS", [T, 1536])
    OA = dscr("OA", [T, 1024])
    Pp = dscr("Pp", [T, D])
    Ps = dscr("Ps", [T, D])
    OB = dscr("OB", [3, T, NSW, 129])

    S = Sched()
    with ExitStack() as st:
        def sb(name, shape, dt=F32):
            return st.enter_context(nc.sbuf_tensor(name, list(shape), dt))

        def bufs(name, n):
            return [Buf(f"{name}{i}") for i in range(n)]

        ps_f = [st.enter_context(nc.psum_tensor(f"ps{i}", [128, 512], F32)) for i in range(6)]
        ps_b = [Buf(f"ps{i}", excl=True) for i in range(6)]
        ps16 = [st.enter_context(nc.psum_tensor(f"ps16_{i}", [128, 1024], BF16)) for i in range(2)]
        ps16_b = [Buf(f"ps16_{i}", excl=True) for i in range(2)]
        ps_ctr = [0, 0]
        ACC0 = 4

        ps_nrot = [6]

        def psum():
            i = ps_ctr[0] % ps_nrot[0]
            ps_ctr[0] += 1
            return ps_f[i], ps_b[i]

        def psum16(full=False):
            i = ps_ctr[1] % 2
            ps_ctr[1] += 1
            if full:
                return ps16[i], ps16_b[i]
            return ps16[i][:, 0:128], ps16_b[i]

        cst = sb("cst", [128, 4, 128])
        cst_b = Buf("cst")
        S.op("sp", lambda e: e.dma_start(out=cst[:], in_=consts_d[:, 0:4, :]), writes=[cst_b], dma=True)
        cstb = sb("cstb", [128, 5, 128], BF16)
        cstb_b = Buf("cstb")
        S.op("pool", lambda e: e.dma_start(out=cstb[:], in_=consts_d[:, 4:9, :]), writes=[cstb_b], dma=True)
        ident_f, UT, MBT, SMT = cst[:, 0, :], cst[:, 1, :], cst[:, 2, :], cst[:, 3, :]
        ident_b, permT_b, ones_b = cstb[:, 0, :], cstb[:, 1, :], cstb[:, 2, :]
        normw = sb("normw_s", [128, 64])
        normw_b = Buf("normw")
        S.op("sp", lambda e: e.dma_start(out=normw[:], in_=normw_d[:, :]), writes=[normw_b], dma=True)

        arena = sb("arena", [128, 49152], BF16)
        hnT = arena[:, 0:32768].rearrange("p (c t) -> p c t", c=16)
        wblk = [arena[:, 32768 + i * 8192: 32768 + (i + 1) * 8192].rearrange("p (c n) -> p c n", c=16)
                for i in range(2)]
        wo = arena[:, 0:24576].rearrange("p (c n) -> p c n", c=12)
        tokA = Buf("tokA")
        hnT_b = bufs("hnT", NT)
        wblk_b = bufs("wblk", 2)
        wo_b = bufs("wo", 12)
        fence = sb("fence", [128, 4])
        wctr = [0]

        def load_wblk(src_ap, ncols=512):
            i = wctr[0] % 2
            wctr[0] += 1
            S.op("pool", lambda e: e.dma_start(out=wblk[i][:, :, 0:ncols], in_=src_ap),
                 reads=[tokA], writes=[wblk_b[i]], dma=True)
            return wblk[i], wblk_b[i]

        FW = sb("FW", [128, 4 * 2052])
        Fv = [FW[:, i * 2052:(i + 1) * 2052] for i in range(4)]
        F_b = bufs("F", 4)
        BT = [sb(f"BT{i}", [128, T], BF16) for i in range(6)]
        BT_b = bufs("BT", 6)
        hbuf = sb("hbuf", [128, D])
        hbuf_b = Buf("hbuf")
        SM = [sb(f"SM{i}", [128, 512]) for i in range(4)]
        SM_b = bufs("SM", 4)
        col = sb("colstat", [128, 64])
        col_b = bufs("col", 64)
        ytile = sb("ytile", [128, 12, 128], BF16)
        ytile_b = Buf("ytile")
        Hb = {}
        YT_b = bufs("YT", 12)
        Pp_b = bufs("Pp", NT)
        Ps_b = bufs("Ps", 4)
        ZS_b = bufs("ZS", NT)
        OA_b = bufs("OA", NT)
        OB_b = bufs("OB", NT)
        out_b = bufs("out", NT)
        hs, hs_b = Fv[3][:, 0:D], F_b[3]
        junk, junk_b = BT[5], BT_b[5]

        def load_h(h_src, h_dst, tt, delta):
            S.op("sp", lambda e: e.dma_start(out=hbuf[:], in_=h_src[tt * 128:(tt + 1) * 128, :]),
                 reads=[Hb[id(h_src)][tt]], writes=[hbuf_b], dma=True)
            if delta:
                S.op("sp", lambda e: e.dma_start(out=hs, in_=Ps[tt * 128:(tt + 1) * 128, :]),
                     reads=[Ps_b[tt // 4]], writes=[hs_b], dma=True)
                S.op("dve", lambda e: e.tensor_tensor(out=hbuf[:], in0=hbuf[:], in1=hs, op=ALU.add),
                     reads=[hbuf_b, hs_b], writes=[hbuf_b])
                if h_dst is not None:
                    S.op("sp", lambda e: e.dma_start(out=h_dst[tt * 128:(tt + 1) * 128, :], in_=hbuf[:]),
                         reads=[hbuf_b], writes=[Hb[id(h_dst)][tt]], dma=True)

        def norm_phase(h_src, h_dst, delta, layer):
            S.op("dve", lambda e: e.memset(fence[:, 0:1], 0.0), writes=[tokA])
            for tt in range(NT):
                load_h(h_src, h_dst, tt, delta)
                c0, c0b = col[:, 0:1], col_b[0]
                c1, c1b = col[:, 1:2], col_b[1]
                S.op("act", lambda e: e.activation(out=junk[:], in_=hbuf[:], func=AF.Square, accum_out=c0),
                     reads=[hbuf_b], writes=[junk_b, c0b])
                S.op("act", lambda e: e.activation(out=c1, in_=c0, func=AF.Sqrt, bias=EPS, scale=1.0 / D),
                     reads=[c0b], writes=[c1b])
                S.op("dve", lambda e: e.reciprocal(c1, c1), reads=[c1b], writes=[c1b])
                S.op("act", lambda e: e.mul(hs, hbuf[:], c1), reads=[hbuf_b, c1b], writes=[hs_b])
                for q in range(4):
                    pt, ptb = psum()
                    for j in range(4):
                        dc = q * 4 + j
                        S.op("pe", lambda e, pt=pt, j=j, dc=dc: e.transpose(
                            pt[:, j * 128:(j + 1) * 128], hs[:, dc * 128:(dc + 1) * 128], ident_f),
                            reads=[hs_b, cst_b], writes=[ptb])
                    nwv = normw[:, layer * 16 + q * 4: layer * 16 + q * 4 + 4].unsqueeze(2).broadcast_to([128, 4, 128])
                    S.op("dve", lambda e, pt=pt, q=q, tt=tt, nwv=nwv: e.tensor_tensor(
                        out=hnT[:, q * 4:(q + 1) * 4, tt * 128:(tt + 1) * 128],
                        in0=pt[:, :].rearrange("p (a b) -> p a b", a=4), in1=nwv, op=ALU.mult),
                        reads=[ptb, normw_b, tokA], writes=[hnT_b[tt]])

        def load_wo(wo_d):
            for c in range(12):
                S.op("pool", lambda e, c=c: e.dma_start(out=wo[:, c, :], in_=wo_d[c * 128:(c + 1) * 128, :]),
                     writes=[tokA, wo_b[c]] if c == 0 else [wo_b[c]], reads=[] if c == 0 else [tokA], dma=True)

        RG = [[0, 1], [2, 3], [4, 5], [6, 7]]

        def outproj_tile(tt):
            po, po_b = Fv[1][:, 0:D], F_b[1]
            for cb in range(4):
                pt, ptb = psum()
                for c in range(12):
                    S.op("pe", lambda e, pt=pt, c=c, cb=cb: e.matmul(
                        pt[:, :], ytile[:, c, :], wo[:, c, cb * 512:(cb + 1) * 512], start=(c == 0), stop=(c == 11)),
                        reads=[ytile_b, wo_b[c], tokA], writes=[ptb])
                if cb % 2 == 0:
                    S.op("act", lambda e, pt=pt, cb=cb: e.copy(po[:, cb * 512:(cb + 1) * 512], pt[:, :]),
                         reads=[ptb], writes=[po_b])
                else:
                    S.op("dve", lambda e, pt=pt, cb=cb: e.tensor_copy(po[:, cb * 512:(cb + 1) * 512], pt[:, :]),
                         reads=[ptb], writes=[po_b])
            S.op("sp", lambda e: e.dma_start(out=Pp[tt * 128:(tt + 1) * 128, :], in_=po),
                 reads=[po_b], writes=[Pp_b[tt]], dma=True)
            if tt % 4 == 3:
                q = tt // 4
                S.op("pool", lambda e: e.collective_compute(
                    "AllReduce", ALU.add, replica_groups=RG,
                    ins=[Pp[q * 512:(q + 1) * 512, :]], outs=[Ps[q * 512:(q + 1) * 512, :]]),
                    reads=Pp_b[q * 4:(q + 1) * 4], writes=[Ps_b[q]], cc=True)

        cu, cu_b = Fv[0][:, 0:2 + T], F_b[0]
        gate, gate_b = Fv[1][:, 0:T], F_b[1]
        acc, acc_b = Fv[2][:, 0:T], F_b[2]
        sccw = sb("sccw", [128, 12, 3])
        sccw_b = Buf("sccw")

        def sc_layer(li, layer, h_src, h_dst, delta):
            norm_phase(h_src, h_dst, delta, layer)
            S.op("sp", lambda e: e.dma_start(out=sccw[:], in_=sc_cw[li][:, :, :]), writes=[sccw_b], dma=True)
            for ct in range(12):
                wb, wbb = load_wblk(sc_w[li][ct])
                S.op("dve", lambda e: e.memset(cu[:, 0:2], 0.0), writes=[cu_b])
                for tg in range(4):
                    pp = []
                    for part in range(4):
                        pt, ptb = psum()
                        for dc in range(16):
                            S.op("pe", lambda e, pt=pt, dc=dc, part=part, tg=tg, wb=wb: e.matmul(
                                pt[:, :], wb[:, dc, part * 128:(part + 1) * 128], hnT[:, dc, tg * 512:(tg + 1) * 512],
                                start=(dc == 0), stop=(dc == 15)),
                                reads=[wbb, tokA] + hnT_b[tg * 4:(tg + 1) * 4], writes=[ptb])
                        pp.append((pt, ptb))
                    (pb_, pbb), (pc_, pcb), (pu_, pub), (pz_, pzb) = pp
                    ua, uab = SM[0], SM_b[0]
                    za, zab = SM[1], SM_b[1]
                    S.op("act", lambda e, pu_=pu_: e.copy(ua[:], pu_[:, :]), reads=[pub], writes=[uab])
                    S.op("dve", lambda e, pc_=pc_, tg=tg: e.tensor_tensor(
                        out=cu[:, 2 + tg * 512: 2 + (tg + 1) * 512], in0=pc_[:, :], in1=ua[:], op=ALU.mult),
                        reads=[pcb, uab], writes=[cu_b])
                    S.op("act", lambda e, pz_=pz_: e.activation(out=za[:], in_=pz_[:, :], func=AF.Silu),
                         reads=[pzb], writes=[zab])
                    S.op("dve", lambda e, pb_=pb_, tg=tg: e.tensor_tensor(
                        out=gate[:, tg * 512:(tg + 1) * 512], in0=pb_[:, :], in1=za[:], op=ALU.mult),
                        reads=[pbb, zab], writes=[gate_b])
                S.op("act", lambda e, ct=ct: e.mul(acc, cu[:, 2:2 + T], sccw[:, ct, 2:3]),
                     reads=[cu_b, sccw_b], writes=[acc_b])
                for i in (1, 0):
                    S.op("dve", lambda e, ct=ct, i=i: e.scalar_tensor_tensor(
                        out=acc, in0=cu[:, i:i + T], scalar=sccw[:, ct, i:i + 1], in1=acc,
                        op0=ALU.mult, op1=ALU.add), reads=[cu_b, sccw_b, acc_b], writes=[acc_b])
                yb, ybb = BT[ct % 2], BT_b[ct % 2]
                S.op("pool", lambda e, yb=yb: e.tensor_tensor(out=yb[:], in0=acc, in1=gate, op=ALU.mult),
                     reads=[acc_b, gate_b], writes=[ybb])
                S.op("sp", lambda e, yb=yb, ct=ct: e.dma_start(out=YT[ct * 128:(ct + 1) * 128, :], in_=yb[:]),
                     reads=[ybb], writes=[YT_b[ct]], dma=True)
            load_wo(sc_wo[li])
            for tt in range(NT):
                S.op("sp", lambda e, tt=tt: e.dma_start(
                    out=ytile[:], in_=YT[:, tt * 128:(tt + 1) * 128].rearrange("(c p) t -> p c t", p=128)),
                    reads=YT_b, writes=[ytile_b], dma=True)
                outproj_tile(tt)

        def final_phase(h_src, delta):
            fw, fw_b = Fv[0][:, 0:D], F_b[0]
            S.op("sp", lambda e: e.dma_start(out=fw, in_=fnw_d[0:1, :].broadcast_to([128, D])),
                 writes=[fw_b], dma=True)
            for tt in range(NT):
                load_h(h_src, None, tt, delta)
                c0, c0b = col[:, 0:1], col_b[0]
                c1, c1b = col[:, 1:2], col_b[1]
                S.op("act", lambda e: e.activation(out=junk[:], in_=hbuf[:], func=AF.Square, accum_out=c0),
                     reads=[hbuf_b], writes=[junk_b, c0b])
                S.op("act", lambda e: e.activation(out=c1, in_=c0, func=AF.Sqrt, bias=EPS, scale=1.0 / D),
                     reads=[c0b], writes=[c1b])
                S.op("dve", lambda e: e.reciprocal(c1, c1), reads=[c1b], writes=[c1b])
                S.op("dve", lambda e: e.scalar_tensor_tensor(
                    out=hs, in0=hbuf[:], scalar=c1, in1=fw, op0=ALU.mult, op1=ALU.mult),
                    reads=[hbuf_b, c1b, fw_b], writes=[hs_b])
                S.op("sp", lambda e, tt=tt: e.dma_start(out=out_d[tt * 128:(tt + 1) * 128, :], in_=hs),
                     reads=[hs_b], writes=[out_b[tt]], dma=True)

        HYB_TILES = {}

        def hyb_tiles():
            if HYB_TILES:
                return HYB_TILES
            d = HYB_TILES
            d["cw"] = sb("hcw_s", [128, 16, 4])
            d["wab"] = sb("wab_s", [128, 16, 16], BF16)
            d["bet"] = sb("bet", [128, 16, 8])
            d["gr"] = sb("gr", [128, 16, 8])
            d["bc16"] = sb("bc16", [128, 3, 8])
            d["dnw"] = sb("dnw", [128, 128])
            d["dn"] = sb("dnwork", [128, 12, 256])
            d["rb"] = sb("rbwork", [128, 22, 128], BF16)
            d["S32"] = sb("S32", [128, 2, 128])
            d["Sb"] = sb("Sbf", [128, 2, 128], BF16)
            d["qbt"] = sb("qbt", [128, 512], BF16)
            d["VA"] = sb("VA", [128, 3, 132], BF16)
            d["oev"] = sb("oev", [128, 2, 132])
            for k in list(d.keys()):
                d[k + "_b"] = Buf(k)
            d["dn_bs"] = bufs("dnw", 12)
            d["nn_bs"] = bufs("nn", 4)
            d["qo_bs"] = bufs("qo", 4)
            d["rb_bs"] = bufs("rb", 22)
            d["PTa"] = [sb(f"PTa{i}", [128, 256], BF16) for i in range(3)]
            d["PTa_bs"] = bufs("PTa", 3)
            d["OBw"] = []
            d["oc_b"] = Buf("oc")
            d["VA_bs"] = bufs("VA", 3)
            d["oev_bs"] = bufs("oev", 2)
            d["S_bs"] = bufs("S", 2)
            return d

        def hyb_layer(li, layer, h_src, h_dst, delta):
            d = hyb_tiles()
            norm_phase(h_src, h_dst, delta, layer)
            cw, cw_b = d["cw"], d["cw_b"]
            S.op("sp", lambda e: e.dma_start(out=cw[:], in_=hyb_cw[li][:, :, :]), writes=[cw_b], dma=True)
            wab, wab_b = d["wab"], d["wab_b"]
            S.op("pool", lambda e: e.dma_start(out=wab[:], in_=hyb_ab[li].rearrange("(c p) n -> p c n", p=128)),
                 writes=[wab_b], dma=True)
            bc16, bc16_b = d["bc16"], d["bc16_b"]
            S.op("sp", lambda e: e.dma_start(out=bc16[:, 0, :], in_=hyb_dt[li][0:1, :].broadcast_to([128, 8])),
                 writes=[bc16_b], dma=True)
            S.op("sp", lambda e: e.dma_start(out=bc16[:, 1, :], in_=hyb_al[li][0:1, :].broadcast_to([128, 8])),
                 reads=[bc16_b], writes=[bc16_b], dma=True)
            dnw, dnw_b = d["dnw"], d["dnw_b"]
            S.op("sp", lambda e: e.dma_start(out=dnw[:], in_=hyb_nw[li][0:1, :].broadcast_to([128, 128])),
                 writes=[dnw_b], dma=True)
            S.op("act", lambda e: e.activation(out=bc16[:, 1, :], in_=bc16[:, 1, :], func=AF.Exp),
                 reads=[bc16_b], writes=[bc16_b])
            S.op("dve", lambda e: e.tensor_scalar_mul(bc16[:, 1, :], bc16[:, 1, :], -1.0),
                 reads=[bc16_b], writes=[bc16_b])
            bet, bet_b, gr, gr_b = d["bet"], d["bet_b"], d["gr"], d["gr_b"]
            t1, t1b = SM[2], SM_b[2]
            t2, t2b = SM[3], SM_b[3]
            for tt in range(NT):
                pt, ptb = psum()
                for dc in range(16):
                    S.op("pe", lambda e, pt=pt, dc=dc, tt=tt: e.matmul(
                        pt[:, 0:16], hnT[:, dc, tt * 128:(tt + 1) * 128], wab[:, dc, :],
                        start=(dc == 0), stop=(dc == 15)), reads=[wab_b, hnT_b[tt], tokA], writes=[ptb])
                S.op("act", lambda e, pt=pt, tt=tt: e.activation(out=bet[:, tt, :], in_=pt[:, 0:8], func=AF.Sigmoid),
                     reads=[ptb], writes=[bet_b])
                S.op("dve", lambda e, pt=pt: e.tensor_tensor(out=t1[:, 0:8], in0=pt[:, 8:16], in1=bc16[:, 0, :],
                                                             op=ALU.add), reads=[ptb, bc16_b], writes=[t1b])
                S.op("act", lambda e: e.activation(out=t2[:, 0:8], in_=t1[:, 0:8], func=AF.Abs),
                     reads=[t1b], writes=[t2b])
                S.op("act", lambda e: e.activation(out=t2[:, 0:8], in_=t2[:, 0:8], func=AF.Exp, scale=-1.0),
                     reads=[t2b], writes=[t2b])
                S.op("act", lambda e: e.activation(out=t2[:, 0:8], in_=t2[:, 0:8], func=AF.Ln, bias=1.0),
                     reads=[t2b], writes=[t2b])
                S.op("dve", lambda e: e.scalar_tensor_tensor(out=t1[:, 0:8], in0=t1[:, 0:8], scalar=0.0,
                                                             in1=t2[:, 0:8], op0=ALU.max, op1=ALU.add),
                     reads=[t1b, t2b], writes=[t1b])
                S.op("dve", lambda e, tt=tt: e.tensor_tensor(out=gr[:, tt, :], in0=t1[:, 0:8], in1=bc16[:, 1, :],
                                                             op=ALU.mult), reads=[t1b, bc16_b], writes=[gr_b])
            for zb in range(3):
                wb, wbb = load_wblk(hyb_w[li][9 + zb])
                for tt in range(NT):
                    pt, ptb = psum()
                    for dc in range(16):
                        S.op("pe", lambda e, pt=pt, dc=dc, tt=tt, wb=wb: e.matmul(
                            pt[:, :], hnT[:, dc, tt * 128:(tt + 1) * 128], wb[:, dc, :],
                            start=(dc == 0), stop=(dc == 15)), reads=[wbb, hnT_b[tt], tokA], writes=[ptb])
                    zt, ztb = SM[tt % 2], SM_b[tt % 2]
                    S.op("act", lambda e, pt=pt, zt=zt: e.activation(out=zt[:], in_=pt[:, :], func=AF.Silu),
                         reads=[ptb], writes=[ztb])
                    S.op("sp", lambda e, zt=zt, tt=tt, zb=zb: e.dma_start(
                        out=ZS[tt * 128:(tt + 1) * 128, zb * 512:(zb + 1) * 512], in_=zt[:]),
                        reads=[ztb], writes=[ZS_b[tt]], dma=True)
            for g in range(NQK):
                dn_head(d, li, g)
            S.op("dve", lambda e: e.memset(d["VA"][:, :, 128:132], 1.0), writes=d["VA_bs"])
            ps_nrot[0] = 4
            for h in range(NSW):
                sw_head(d, li, h)
            ps_nrot[0] = 6
            load_wo(hyb_wo[li])
            for tt in range(NT):
                combine_tile(d, tt)
                if tt == NT - 1:
                    d["OBw"] = []
                outproj_tile(tt)

        def proj_cm(wb, wbb, part, evac):
            for tg in range(4):
                pt, ptb = psum()
                for dc in range(16):
                    S.op("pe", lambda e, pt=pt, dc=dc, tg=tg: e.matmul(
                        pt[:, :], wb[:, dc, part * 128:(part + 1) * 128], hnT[:, dc, tg * 512:(tg + 1) * 512],
                        start=(dc == 0), stop=(dc == 15)),
                        reads=[wbb, tokA] + hnT_b[tg * 4:(tg + 1) * 4], writes=[ptb])
                evac(tg, pt, ptb)

        def dn_head(d, li, g):
            cw, cw_b = d["cw"], d["cw_b"]
            wb, wbb = load_wblk(hyb_w[li][g])
            qs, qs_b = Fv[3][:, 0:T], F_b[3]
            accv, accv_b = Fv[2][:, 0:T], F_b[2]
            sq, sq_b = BT[5], BT_b[5]
            for part in range(4):
                raw, raw_b = Fv[part % 2], F_b[part % 2]
                S.op("dve", lambda e, raw=raw: e.memset(raw[:, 0:3], 0.0), writes=[raw_b])

                def ev(tg, pt, ptb, raw=raw, raw_b=raw_b):
                    S.op("act", lambda e: e.copy(raw[:, 3 + tg * 512: 3 + (tg + 1) * 512], pt[:, :]),
                         reads=[ptb], writes=[raw_b])
                proj_cm(wb, wbb, part, ev)
                tl = g * 4 + part
                S.op("act", lambda e, raw=raw, tl=tl: e.mul(accv, raw[:, 3:3 + T], cw[:, tl, 3:4]),
                     reads=[raw_b, cw_b], writes=[accv_b])
                for i in (2, 1, 0):
                    S.op("dve", lambda e, raw=raw, tl=tl, i=i: e.scalar_tensor_tensor(
                        out=accv, in0=raw[:, i:i + T], scalar=cw[:, tl, i:i + 1], in1=accv,
                        op0=ALU.mult, op1=ALU.add), reads=[raw_b, cw_b, accv_b], writes=[accv_b])
                if part < 2:
                    S.op("act", lambda e: e.activation(out=qs, in_=accv, func=AF.Silu), reads=[accv_b], writes=[qs_b])
                    S.op("act", lambda e: e.activation(out=sq[:], in_=qs, func=AF.Square), reads=[qs_b], writes=[sq_b])
                    for tg in range(4):
                        pt, ptb = psum()
                        S.op("pe", lambda e, pt=pt, tg=tg: e.matmul(pt[:, :], ones_b, sq[:, tg * 512:(tg + 1) * 512],
                                                                    start=True, stop=True),
                             reads=[sq_b, cstb_b], writes=[ptb])
                        rn, rnb = SM[tg % 2], SM_b[tg % 2]
                        S.op("act", lambda e, pt=pt, rn=rn: e.activation(out=rn[:], in_=pt[:, :], func=AF.Sqrt,
                                                                         bias=EPS, scale=1.0),
                             reads=[ptb], writes=[rnb])
                        S.op("dve", lambda e, rn=rn: e.reciprocal(rn[:], rn[:]), reads=[rnb], writes=[rnb])
                        sc_ = (128.0 ** -0.5) if part == 0 else 1.0
                        S.op("dve", lambda e, rn=rn, tg=tg, part=part, sc_=sc_: e.scalar_tensor_tensor(
                            out=BT[part][:, tg * 512:(tg + 1) * 512], in0=qs[:, tg * 512:(tg + 1) * 512], scalar=sc_,
                            in1=rn[:], op0=ALU.mult, op1=ALU.mult), reads=[qs_b, rnb], writes=[BT_b[part]])
                else:
                    S.op("act", lambda e, part=part: e.activation(out=BT[part][:], in_=accv, func=AF.Silu),
                         reads=[accv_b], writes=[BT_b[part]])
            qn, kn = BT[0], BT[1]
            qkb = [BT_b[0], BT_b[1]]
            dn, dn_bs, rb, rb_bs = d["dn"], d["dn_bs"], d["rb"], d["rb_bs"]
            S32, Sb, S_bs = d["S32"], d["Sb"], d["S_bs"]
            bet, bet_b, gr, gr_b = d["bet"], d["bet_b"], d["gr"], d["gr_b"]
            dnw, dnw_b = d["dnw"], d["dnw_b"]
            for e_ in range(2):
                S.op("dve", lambda e, e_=e_: e.memset(S32[:, e_, :], 0.0), writes=[S_bs[e_]])
                S.op("dve", lambda e, e_=e_: e.memset(Sb[:, e_, :], 0.0), reads=[S_bs[e_]], writes=[S_bs[e_]])
            def shared_pre(tt):
                par = tt % 2
                ts = slice(tt * 128, (tt + 1) * 128)
                ktok, ktok_b = rb[:, par, :], rb_bs[par]
                p16, p16b = psum16()
                S.op("pe", lambda e: e.transpose(p16, kn[:, ts], ident_b), reads=[qkb[1], cstb_b], writes=[p16b])
                S.op("act", lambda e: e.copy(ktok, p16), reads=[p16b], writes=[ktok_b])
                kkqk, kkqk_b = dn[:, 10 + par, :], dn_bs[10 + par]
                pt, ptb = psum()
                S.op("pe", lambda e: e.matmul(pt[:, 0:128], kn[:, ts], kn[:, ts], start=True, stop=True),
                     reads=[qkb[1]], writes=[ptb])
                S.op("pe", lambda e: e.matmul(pt[:, 128:256], kn[:, ts], qn[:, ts], start=True, stop=True),
                     reads=qkb, writes=[ptb])
                S.op("act", lambda e: e.copy(kkqk, pt[:, 0:256]), reads=[ptb], writes=[kkqk_b])
                yield

            def tiles(e_, tt):
                par = tt % 2
                hv = 2 * g + e_
                t = {}
                t["ts"] = slice(tt * 128, (tt + 1) * 128)
                t["ktok"], t["ktok_b"] = rb[:, par, :], rb_bs[par]
                t["kkqk"], t["kkqk_b"] = dn[:, 10 + par, :], dn_bs[10 + par]
                i = 2 + e_ * 2 + par
                t["vtok"], t["vtok_b"] = rb[:, i, :], rb_bs[i]
                for j, nm in enumerate(("PTt", "Kd", "TT")):
                    i = 6 + (e_ * 2 + par) * 3 + j
                    t[nm], t[nm + "_b"] = rb[:, i, :], rb_bs[i]
                for j, nm in enumerate(("Rt", "vnew")):
                    i = 18 + e_ * 2 + j
                    t[nm], t[nm + "_b"] = rb[:, i, :], rb_bs[i]
                c0 = 40 + (e_ * 2 + par) * 4
                for j, nm in enumerate(("ngc", "eg", "neg", "egl")):
                    t[nm], t[nm + "_b"] = col[:, c0 + j:c0 + j + 1], col_b[c0 + j]
                c0 = 56 + e_ * 2
                for j, nm in enumerate(("ss", "rs")):
                    t[nm], t[nm + "_b"] = col[:, c0 + j:c0 + j + 1], col_b[c0 + j]
                t["gcol"] = gr[:, tt, hv:hv + 1]
                t["bcol"] = bet[:, tt, hv:hv + 1]
                w0 = e_ * 5
                t["DT"], t["DTs"], t["DT_b"] = dn[:, w0, 0:128], dn[:, w0, 128:256], dn_bs[w0]
                t["AB"] = [dn[:, w0 + 1, :], dn[:, w0 + 2, :]]
                t["AB_b"] = [dn_bs[w0 + 1], dn_bs[w0 + 2]]
                t["Nn"] = [dn[:, w0 + 3, 0:128], dn[:, w0 + 3, 128:256]]
                t["Nn_b"] = [d["nn_bs"][e_ * 2], d["nn_bs"][e_ * 2 + 1]]
                t["QSs"], t["QSs_b"] = dn[:, w0 + 4, 0:128], d["qo_bs"][e_ * 2]
                t["osb"], t["osb_b"] = dn[:, w0 + 4, 128:256], d["qo_bs"][e_ * 2 + 1]
                t["hv"] = hv
                return t

            def pre(e_, tt):
                t = tiles(e_, tt)
                ts, hv = t["ts"], t["hv"]
                vT, vT_b = BT[2 + e_], BT_b[2 + e_]
                vtok, vtok_b = t["vtok"], t["vtok_b"]
                gcol, bcol = t["gcol"], t["bcol"]
                ngc, eg, neg, egl = t["ngc"], t["eg"], t["neg"], t["egl"]
                ngcb, egb, negb, eglb = t["ngc_b"], t["eg_b"], t["neg_b"], t["egl_b"]
                DT, DTs, DT_b, AB, AB_b, Nn, Nn_b = t["DT"], t["DTs"], t["DT_b"], t["AB"], t["AB_b"], t["Nn"], t["Nn_b"]
                kkqk, kkqk_b, ktok, ktok_b = t["kkqk"], t["kkqk_b"], t["ktok"], t["ktok_b"]
                PTt, PTt_b, Kd, Kd_b, TT, TT_b = t["PTt"], t["PTt_b"], t["Kd"], t["Kd_b"], t["TT"], t["TT_b"]
                tmp, tmp_b = AB[1][:, 0:128], AB_b[1]
                p16, p16b = psum16()
                S.op("pe", lambda e: e.transpose(p16, vT[:, ts], ident_b), reads=[vT_b, cstb_b], writes=[p16b])
                S.op("act", lambda e: e.copy(vtok, p16), reads=[p16b], writes=[vtok_b])
                pg, pgb = psum()
                S.op("pe", lambda e: e.matmul(pg[:, 0:128], gcol.broadcast_to([128, 128]), UT, start=True, stop=True),
                     reads=[gr_b, cst_b], writes=[pgb])
                S.op("pe", lambda e: e.matmul(pg[:, 128:129], UT, gcol, start=True, stop=True),
                     reads=[gr_b, cst_b], writes=[pgb])
                S.op("act", lambda e: e.mul(ngc, pg[:, 128:129], -1.0), reads=[pgb], writes=[ngcb])
                S.op("act", lambda e: e.activation(out=eg, in_=pg[:, 128:129], func=AF.Exp), reads=[pgb], writes=[egb])
                S.op("act", lambda e: e.activation(out=egl, in_=pg[:, 127:128], func=AF.Exp), reads=[pgb], writes=[eglb])
                S.op("dve", lambda e: e.tensor_tensor(out=tmp, in0=pg[:, 0:128], in1=MBT, op=ALU.add),
                     reads=[pgb, cst_b], writes=[tmp_b])
                S.op("dve", lambda e: e.tensor_scalar_mul(neg, eg, -1.0), reads=[egb], writes=[negb])
                yield
                S.op("act", lambda e: e.activation(out=DT, in_=tmp, func=AF.Exp, bias=ngc, scale=1.0),
                     reads=[tmp_b, ngcb], writes=[DT_b])
                S.op("pool", lambda e: e.tensor_tensor(out=DTs, in0=DT, in1=SMT, op=ALU.mult),
                     reads=[DT_b, cst_b], writes=[DT_b])
                S.op("dve", lambda e: e.scalar_tensor_tensor(
                    out=AB[0][:, 128:256], in0=kkqk[:, 0:128], scalar=bcol, in1=DTs, op0=ALU.mult, op1=ALU.mult),
                    reads=[kkqk_b, bet_b, DT_b], writes=[AB_b[0]])
                S.op("pool", lambda e: e.tensor_tensor(out=PTt, in0=kkqk[:, 128:256], in1=DT, op=ALU.mult),
                     reads=[kkqk_b, DT_b], writes=[PTt_b])
                S.op("act", lambda e: e.mul(Kd, ktok, DT[:, 127:128]), reads=[ktok_b, DT_b], writes=[Kd_b])
                yield
                pa0, pa0b = psum()
                S.op("pe", lambda e: e.transpose(pa0[:, 0:128], AB[0][:, 128:256], ident_f),
                     reads=[AB_b[0], cst_b], writes=[pa0b])
                S.op("act", lambda e: e.copy(AB[0][:, 0:128], pa0[:, 0:128]), reads=[pa0b], writes=[AB_b[0]])
                S.op("dve", lambda e: e.tensor_tensor(out=Nn[0], in0=ident_f, in1=AB[0][:, 128:256], op=ALU.subtract),
                     reads=[AB_b[0], cst_b], writes=[Nn_b[0]])
                yield
                for k in range(1, 7):
                    prv, nxt_ = AB[(k - 1) % 2], AB[k % 2]
                    prvb, nxtb = AB_b[(k - 1) % 2], AB_b[k % 2]
                    pa, pab = psum()
                    S.op("pe", lambda e, pa=pa, prv=prv: e.matmul(pa[:, 0:128], prv[:, 128:256], prv[:, 0:128],
                                                                  start=True, stop=True), reads=[prvb], writes=[pab])
                    if k < 6:
                        S.op("pe", lambda e, pa=pa, prv=prv: e.matmul(pa[:, 128:256], prv[:, 0:128], prv[:, 128:256],
                                                                      start=True, stop=True), reads=[prvb], writes=[pab])
                    S.op("act", lambda e, pa=pa, nxt_=nxt_: e.copy(nxt_[:, 0:256], pa[:, 0:256]), reads=[pab],
                         writes=[nxtb])
                    yield
                    npv, npvb = Nn[(k - 1) % 2], Nn_b[(k - 1) % 2]
                    pn, pnb = psum()
                    S.op("pe", lambda e, pn=pn, nxt_=nxt_, npv=npv: e.matmul(pn[:, 0:128], nxt_[:, 0:128], npv,
                                                                             start=True, stop=True),
                         reads=[nxtb, npvb], writes=[pnb])
                    if k < 6:
                        nnx, nnxb = Nn[k % 2], Nn_b[k % 2]
                    else:
                        nnx, nnxb = TT, TT_b
                    S.op("dve", lambda e, pn=pn, npv=npv, nnx=nnx: e.tensor_tensor(out=nnx, in0=pn[:, 0:128],
                                                                                   in1=npv, op=ALU.add),
                         reads=[pnb, npvb], writes=[nnxb])
                    yield

            def scan(e_, tt):
                t = tiles(e_, tt)
                ts, hv = t["ts"], t["hv"]
                vtok, vtok_b, bcol = t["vtok"], t["vtok_b"], t["bcol"]
                eg, neg, egl, ss, rs = t["eg"], t["neg"], t["egl"], t["ss"], t["rs"]
                egb, negb, eglb, ssb, rsb = t["eg_b"], t["neg_b"], t["egl_b"], t["ss_b"], t["rs_b"]
                PTt, PTt_b, Kd, Kd_b, TT, TT_b = t["PTt"], t["PTt_b"], t["Kd"], t["Kd_b"], t["TT"], t["TT_b"]
                Rt, Rt_b, vnew, vnew_b = t["Rt"], t["Rt_b"], t["vnew"], t["vnew_b"]
                QSs, QSs_b, osb, osb_b = t["QSs"], t["QSs_b"], t["osb"], t["osb_b"]
                Sbe = Sb[:, e_, :]
                S32e = S32[:, e_, :]
                p1, p1b = psum()
                S.op("pe", lambda e: e.matmul(p1[:, 0:128], kn[:, ts], Sbe, start=True, stop=True),
                     reads=[qkb[1], S_bs[e_]], writes=[p1b])
                S.op("pe", lambda e: e.matmul(p1[:, 128:256], qn[:, ts], Sbe, start=True, stop=True),
                     reads=[qkb[0], S_bs[e_]], writes=[p1b])
                S.op("dve", lambda e: e.scalar_tensor_tensor(
                    out=Rt, in0=p1[:, 0:128], scalar=neg, in1=vtok, op0=ALU.mult, op1=ALU.add),
                    reads=[p1b, negb, vtok_b], writes=[Rt_b])
                S.op("act", lambda e: e.mul(QSs, p1[:, 128:256], eg), reads=[p1b, egb], writes=[QSs_b])
                yield
                p2, p2b = psum()
                S.op("pe", lambda e: e.matmul(p2[:, 0:128], TT, Rt, start=True, stop=True),
                     reads=[TT_b, Rt_b], writes=[p2b])
                S.op("act", lambda e: e.mul(vnew, p2[:, 0:128], bcol), reads=[p2b, bet_b], writes=[vnew_b])
                yield
                p3, p3b = psum()
                S.op("pe", lambda e: e.matmul(p3[:, 0:128], PTt, vnew, start=True, stop=True),
                     reads=[PTt_b, vnew_b], writes=[p3b])
                S.op("pe", lambda e: e.matmul(p3[:, 128:256], Kd, vnew, start=True, stop=True),
                     reads=[Kd_b, vnew_b], writes=[p3b])
                S.op("dve", lambda e: e.scalar_tensor_tensor(
                    out=S32e, in0=S32e, scalar=egl, in1=p3[:, 128:256], op0=ALU.mult, op1=ALU.add),
                    reads=[p3b, eglb, S_bs[e_]], writes=[S_bs[e_]])
                S.op("act", lambda e: e.copy(Sbe, S32e), reads=[S_bs[e_]], writes=[S_bs[e_]])
                S.op("dve", lambda e: e.tensor_tensor(out=osb, in0=p3[:, 0:128], in1=QSs, op=ALU.add),
                     reads=[p3b, QSs_b], writes=[osb_b])
                yield
                jk, jk_b = SM[2 + e_][:, 0:128], SM_b[2 + e_]
                S.op("act", lambda e: e.activation(out=jk, in_=osb, func=AF.Square, accum_out=ss),
                     reads=[osb_b], writes=[jk_b, ssb])
                S.op("act", lambda e: e.activation(out=rs, in_=ss, func=AF.Sqrt, bias=EPS, scale=1.0 / 128),
                     reads=[ssb], writes=[rsb])
                S.op("dve", lambda e: e.reciprocal(rs, rs), reads=[rsb], writes=[rsb])
                S.op("dve", lambda e: e.scalar_tensor_tensor(
                    out=osb, in0=osb, scalar=rs, in1=dnw[:], op0=ALU.mult, op1=ALU.mult),
                    reads=[osb_b, rsb, dnw_b], writes=[osb_b])
                S.op("sp", lambda e: e.dma_start(out=OA[tt * 128:(tt + 1) * 128, hv * 128:(hv + 1) * 128], in_=osb),
                     reads=[osb_b], writes=[OA_b[tt]], dma=True)
                yield

            def interleave(gens):
                gens = list(gens)
                while gens:
                    for gen in list(gens):
                        try:
                            next(gen)
                        except StopIteration:
                            gens.remove(gen)

            interleave([shared_pre(0)])
            interleave([pre(0, 0), pre(1, 0)])
            for tt in range(NT):
                gl = [scan(0, tt), scan(1, tt)]
                if tt + 1 < NT:
                    interleave([shared_pre(tt + 1)])
                    gl += [pre(0, tt + 1), pre(1, tt + 1)]
                interleave(gl)

        def sw_head(d, li, h):
            qbt, qbt_b = d["qbt"], d["qbt_b"]
            wb, wbb = load_wblk(hyb_w[li][4 + h])
            for tg in range(4):
                S.op("sp", lambda e, tg=tg: e.dma_start(out=SM[2][:], in_=rope_d[:, 0, tg * 512:(tg + 1) * 512]),
                     writes=[SM_b[2]], dma=True)
                S.op("sp", lambda e, tg=tg: e.dma_start(out=SM[3][:], in_=rope_d[:, 1, tg * 512:(tg + 1) * 512]),
                     writes=[SM_b[3]], dma=True)
                for part in range(4):
                    pt, ptb = psum()
                    for dc in range(16):
                        S.op("pe", lambda e, pt=pt, dc=dc, tg=tg, part=part: e.matmul(
                            pt[:, :], wb[:, dc, part * 128:(part + 1) * 128], hnT[:, dc, tg * 512:(tg + 1) * 512],
                            start=(dc == 0), stop=(dc == 15)),
                            reads=[wbb, tokA] + hnT_b[tg * 4:(tg + 1) * 4], writes=[ptb])
                    S.op("act", lambda e, pt=pt: e.copy(qbt[:], pt[:, :]), reads=[ptb], writes=[qbt_b])
                    pp, ppb = psum()
                    S.op("pe", lambda e, pp=pp: e.matmul(pp[:, :], permT_b, qbt[:], start=True, stop=True),
                         reads=[qbt_b, cstb_b], writes=[ppb])
                    S.op("dve", lambda e, pt=pt: e.tensor_tensor(out=SM[0][:], in0=pt[:, :], in1=SM[2][:],
                                                                 op=ALU.mult),
                         reads=[ptb, SM_b[2]], writes=[SM_b[0]])
                    S.op("dve", lambda e, pp=pp: e.tensor_tensor(out=SM[1][:], in0=pp[:, :], in1=SM[3][:],
                                                                 op=ALU.mult),
                         reads=[ppb, SM_b[3]], writes=[SM_b[1]])
                    S.op("pool", lambda e, part=part, tg=tg: e.tensor_tensor(
                        out=BT[part][:, tg * 512:(tg + 1) * 512], in0=SM[0][:], in1=SM[1][:], op=ALU.add),
                        reads=[SM_b[0], SM_b[1]], writes=[BT_b[part]])
            wv, wvb = load_wblk(hyb_w[li][8])

            def evv(tg, pt, ptb):
                S.op("act", lambda e: e.copy(BT[4][:, tg * 512:(tg + 1) * 512], pt[:, :]), reads=[ptb], writes=[BT_b[4]])
            proj_cm(wv, wvb, h % 4, evv)
            sq, sq_b = BT[5], BT_b[5]
            kcol, kcol_b = col[0:1, 24:28], col_b[24]
            kmx, kmx_b = col[0:1, 28:29], col_b[28]
            S.op("act", lambda e: e.activation(out=sq[:], in_=BT[3][:], func=AF.Square), reads=[BT_b[3]], writes=[sq_b])
            for tg in range(4):
                pk, pkb = psum()
                S.op("pe", lambda e, pk=pk, tg=tg: e.matmul(pk[0:1, :], ones_b[:, 0:1], sq[:, tg * 512:(tg + 1) * 512],
                                                            start=True, stop=True), reads=[sq_b, cstb_b], writes=[pkb])
                S.op("dve", lambda e, pk=pk, tg=tg: e.reduce_max(out=col[0:1, 24 + tg:25 + tg], in_=pk[0:1, :],
                                                                 axis=mybir.AxisListType.X),
                     reads=[pkb], writes=[kcol_b])
            S.op("dve", lambda e: e.reduce_max(out=kmx, in_=kcol, axis=mybir.AxisListType.X), reads=[kcol_b],
                 writes=[kmx_b])
            rowf, rowf_b = SM[2], SM_b[2]
            for tg in range(4):
                pr, prb = psum()
                for gi in range(3):
                    S.op("act", lambda e, gi=gi, tg=tg: e.activation(out=qbt[:], in_=BT[gi][:, tg * 512:(tg + 1) * 512],
                                                                      func=AF.Square), reads=[BT_b[gi]], writes=[qbt_b])
                    S.op("pe", lambda e, pr=pr, gi=gi: e.matmul(pr[0:1, :], ones_b[:, 0:1], qbt[:], start=(gi == 0),
                                                                stop=(gi == 2)), reads=[qbt_b, cstb_b], writes=[prb])
                S.op("act", lambda e, pr=pr: e.activation(out=rowf[0:1, :], in_=pr[0:1, :], func=AF.Sqrt, scale=kmx),
                     reads=[prb, kmx_b], writes=[rowf_b])
                S.op("dve", lambda e, tg=tg: e.tensor_scalar_mul(sq[0:1, tg * 512:(tg + 1) * 512], rowf[0:1, :], -1.0),
                     reads=[rowf_b, sq_b], writes=[sq_b])
            negc = sq
            VA, VA_bs, oev, oev_bs, rb, rb_bs = d["VA"], d["VA_bs"], d["oev"], d["oev_bs"], d["rb"], d["rb_bs"]
            it = 0
            for (dil, gi) in SWG:
                L = T // dil
                nb = L // 128
                Qg, Qg_b = BT[gi], BT_b[gi]
                for r in range(dil):
                    acc_ps = [None, None]
                    for m in range(nb):
                        k0 = r + dil * 128 * m
                        ksl = slice(k0, k0 + dil * 127 + 1, dil)
                        nq = 2 if m + 1 < nb else 1
                        qsl = slice(k0, k0 + dil * (128 * nq - 1) + 1, dil)
                        N = 128 * nq
                        psc, pscb = psum()
                        S.op("pe", lambda e, psc=psc, ksl=ksl, qsl=qsl, N=N, Qg=Qg: e.matmul(
                            psc[:, 0:N], BT[3][:, ksl], Qg[:, qsl], start=True, stop=False),
                            reads=[BT_b[3], Qg_b], writes=[pscb])
                        S.op("pe", lambda e, psc=psc, qsl=qsl, N=N: e.matmul(
                            psc[:, 0:N], ones_b[0:1, :], negc[0:1, qsl], start=False, stop=False),
                            reads=[sq_b, cstb_b], writes=[pscb])
                        S.op("pe", lambda e, psc=psc, N=N: e.matmul(
                            psc[:, 0:N], ident_b, cstb[:, 3:3 + N // 128, :], start=False, stop=True),
                            reads=[cstb_b], writes=[pscb])
                        PTa = d["PTa"][it % 3]
                        PTa_b = d["PTa_bs"][it % 3]
                        S.op("act", lambda e, psc=psc, N=N, PTa=PTa: e.activation(
                            out=PTa[:, 0:N], in_=psc[:, 0:N], func=AF.Exp, scale=128.0 ** -0.5),
                            reads=[pscb], writes=[PTa_b])
                        va, va_b = VA[:, it % 3, :], VA_bs[it % 3]
                        p16, p16b = psum16()
                        S.op("pe", lambda e, p16=p16, ksl=ksl: e.transpose(p16, BT[4][:, ksl], ident_b),
                             reads=[BT_b[4], cstb_b], writes=[p16b])
                        S.op("dve", lambda e, p16=p16, va=va: e.tensor_copy(va[:, 0:128], p16), reads=[p16b],
                             writes=[va_b])
                        pa, pab = ps_f[ACC0 + m % 2], ps_b[ACC0 + m % 2]
                        S.op("pe", lambda e, pa=pa, PTa=PTa, va=va, m=m: e.matmul(
                            pa[:, 0:129], PTa[:, 0:128], va[:, 0:129], start=(m == 0), stop=True),
                            reads=[PTa_b, va_b], writes=[pab])
                        if nq == 2:
                            pn_, pnb_ = ps_f[ACC0 + (m + 1) % 2], ps_b[ACC0 + (m + 1) % 2]
                        ov, ov_b = oev[:, it % 2, :], oev_bs[it % 2]
                        S.op("act", lambda e, pa=pa, ov=ov: e.copy(ov[:, 0:129], pa[:, 0:129]), reads=[pab],
                             writes=[ov_b])
                        tsl = slice(k0, k0 + dil * 127 + 1, dil)
                        obw = Buf("obw")
                        d["OBw"].append(obw)
                        S.op("sp", lambda e, ov=ov, tsl=tsl, gi=gi: e.dma_start(
                            out=OB[gi, tsl, h, 0:129], in_=ov[:, 0:129]),
                            reads=[ov_b], writes=[obw], dma=True)
                        if nq == 2:
                            S.op("pe", lambda e, pn_=pn_, PTa=PTa, va=va: e.matmul(
                                pn_[:, 0:129], PTa[:, 128:256], va[:, 0:129], start=True, stop=False),
                                reads=[PTa_b, va_b], writes=[pnb_])
                        it += 1

        def combine_tile(d, tt):
            ts = slice(tt * 128, (tt + 1) * 128)
            ych, ych_b = BT[5][:, 0:512], BT_b[5]
            for ck in range(3):
                zc, zc_b = Fv[0][:, 0:512], F_b[0]
                S.op("sp", lambda e, ck=ck: e.dma_start(out=zc, in_=ZS[ts, ck * 512:(ck + 1) * 512]),
                     reads=[ZS_b[tt]], writes=[zc_b], dma=True)
                oc, oc_b = Fv[0][:, 512:1024], d["oc_b"]
                if ck < 2:
                    S.op("sp", lambda e, ck=ck: e.dma_start(out=oc, in_=OA[ts, ck * 512:(ck + 1) * 512]),
                         reads=[OA_b[tt]], writes=[oc_b], dma=True)
                else:
                    ob, ob_b = FW[:, 4104:4104 + 3 * 516].rearrange("p (g x) -> p g x", g=3), F_b[2]
                    S.op("sp", lambda e: e.dma_start(out=ob, in_=OB[:, ts, :, :].rearrange("g t h x -> t g (h x)")),
                         reads=list(d["OBw"]), writes=[ob_b], dma=True)
                    S.op("dve", lambda e: e.tensor_tensor(out=ob[:, 0, :], in0=ob[:, 0, :], in1=ob[:, 1, :], op=ALU.add),
                         reads=[ob_b], writes=[ob_b])
                    S.op("dve", lambda e: e.tensor_tensor(out=ob[:, 0, :], in0=ob[:, 0, :], in1=ob[:, 2, :], op=ALU.add),
                         reads=[ob_b], writes=[ob_b])
                    o3 = ob[:, 0, :].rearrange("p (h x) -> p h x", h=NSW)
                    rd, rd_b = col[:, 32:32 + NSW], col_b[32]
                    S.op("dve", lambda e: e.reciprocal(rd, o3[:, :, 128]), reads=[ob_b], writes=[rd_b])
                    S.op("dve", lambda e: e.tensor_tensor(
                        out=oc.rearrange("p (h x) -> p h x", h=NSW), in0=o3[:, :, 0:128],
                        in1=rd.unsqueeze(2).broadcast_to([128, NSW, 128]), op=ALU.mult),
                        reads=[ob_b, rd_b], writes=[oc_b])
                S.op("dve", lambda e: e.tensor_tensor(out=ych, in0=oc, in1=zc, op=ALU.mult),
                     reads=[oc_b, zc_b], writes=[ych_b])
                p16, p16b = psum16(full=True)
                for j in range(4):
                    S.op("pe", lambda e, p16=p16, j=j: e.transpose(p16[:, j * 128:(j + 1) * 128],
                                                                   ych[:, j * 128:(j + 1) * 128], ident_b),
                         reads=[ych_b, cstb_b], writes=[p16b])
                S.op("act", lambda e, p16=p16, ck=ck: e.copy(
                    ytile[:, ck * 4:(ck + 1) * 4, :], p16[:, 0:512].rearrange("p (a b) -> p a b", a=4)),
                    reads=[p16b], writes=[ytile_b])

        for hh in H:
            Hb[id(hh)] = bufs("H", NT)
        Hb[id(x_d)] = bufs("x", NT)
        cur = x_d
        nxt = 0
        delta = False
        for layer in layers:
            dst = H[nxt] if delta else None
            if layer % 2 == 0:
                hyb_layer(layer // 2, layer, cur, dst, delta)
            else:
                sc_layer(layer // 2, layer, cur, dst, delta)
            if delta:
                cur = dst
                nxt ^= 1
            delta = True
        if final_norm:
            final_phase(cur, delta)
        else:
            for tt in range(NT):
                load_h(cur, None, tt, delta)
                S.op("sp", lambda e, tt=tt: e.dma_start(out=out_d[tt * 128:(tt + 1) * 128, :], in_=hbuf[:]),
                     reads=[hbuf_b], writes=[out_b[tt]], dma=True)
        S.op("sp", None, reads=out_b)
        S.emit(nc, st)
    return nc


def _consts():
    idx = np.arange(128)
    ident = np.eye(128, dtype=np.float32)
    UT = (idx[:, None] <= idx[None, :]).astype(np.float32)
    MBT = np.where(idx[None, :] >= idx[:, None], 0.0, NEG).astype(np.float32)
    SMT = (idx[None, :] > idx[:, None]).astype(np.float32)
    permT = np.zeros((128, 128), np.float32)
    for m in range(16):
        permT[m + 16, m] = 1.0
        permT[m, m + 16] = 1.0
    ones = np.ones((128, 128), np.float32)
    mcur = np.where(idx[:, None] <= idx[None, :], 0.0, NEG).astype(np.float32)
    mnext = np.where(idx[:, None] >= idx[None, :], 0.0, NEG).astype(np.float32)
    c = np.stack([ident, UT, MBT, SMT, ident, permT, ones, mcur, mnext], axis=1)
    half = 16
    inv = np.power(np.float32(500000.0), -np.arange(half, dtype=np.float32) * np.float32(2.0) / np.float32(32)).astype(np.float32)
    ang = np.arange(T, dtype=np.float32)[None, :] * inv[:, None]
    cos = np.cos(ang).astype(np.float32)
    sin = np.sin(ang).astype(np.float32)
    C = np.ones((128, T), np.float32)
    Sg = np.zeros((128, T), np.float32)
    C[0:16] = cos
    C[16:32] = cos
    Sg[0:16] = -sin
    Sg[16:32] = sin
    rope = np.stack([C, Sg], axis=1)
    return np.ascontiguousarray(c), np.ascontiguousarray(rope)


def _pcn(blk):
    nb = blk.shape[0]
    return np.ascontiguousarray(blk.reshape(nb, 16, 128, 512).transpose(0, 2, 1, 3))


def _pack_hyb(w_in, j):
    blocks = []
    for g in range(4):
        gq = 4 * j + g
        cols = np.concatenate([np.arange(gq * 128, (gq + 1) * 128), 1024 + np.arange(gq * 128, (gq + 1) * 128),
                               2048 + np.arange(2 * gq * 128, (2 * gq + 2) * 128)])
        blocks.append(w_in[:, cols])
    for h in range(4):
        hq = 4 * j + h
        cols = np.concatenate([6176 + (gi * 8 + hq) * 128 + np.arange(128) for gi in range(3)] +
                              [9248 + hq * 128 + np.arange(128)])
        blocks.append(w_in[:, cols])
    blocks.append(w_in[:, 10272 + j * 512: 10272 + (j + 1) * 512])
    for k in range(2):
        blocks.append(w_in[:, 4096 + j * 1024 + k * 512: 4096 + j * 1024 + (k + 1) * 512])
    blocks.append(w_in[:, 11296 + j * 512: 11296 + (j + 1) * 512])
    return _pcn(np.stack(blocks, axis=0))


def _pack_hcw(cw, j):
    out = np.zeros((128, 16, 4), np.float32)
    for g in range(4):
        gq = 4 * j + g
        out[:, g * 4 + 0] = cw[gq * 128:(gq + 1) * 128]
        out[:, g * 4 + 1] = cw[1024 + gq * 128: 1024 + (gq + 1) * 128]
        out[:, g * 4 + 2] = cw[2048 + 2 * gq * 128: 2048 + (2 * gq + 1) * 128]
        out[:, g * 4 + 3] = cw[2048 + (2 * gq + 1) * 128: 2048 + (2 * gq + 2) * 128]
    return out


def _pack_sc(w_in, j):
    blocks = []
    for ct in range(12):
        cg = 12 * j + ct
        cols = np.concatenate([p * 3072 + cg * 128 + np.arange(128) for p in range(4)])
        blocks.append(w_in[:, cols])
    return _pcn(np.stack(blocks, axis=0))


_NC_CACHE = {}


def make_in_maps(x, norm_w, hyb_w_in, dn_conv_w, dn_a_log, dn_dt_bias, dn_norm_w, hyb_w_out,
                 sc_w_in, sc_conv_w, sc_w_out, final_norm_w):
    f = lambda a: np.ascontiguousarray(np.asarray(a, dtype=np.float32))
    consts, rope = _consts()
    shared = {
        "normw": f(np.asarray(norm_w).reshape(4, 16, 128).transpose(2, 0, 1).reshape(128, 64)),
        "fnw": f(np.asarray(final_norm_w).reshape(1, D)),
        "consts": consts, "rope": rope,
    }
    halves = []
    for j in range(2):
        m = dict(shared)
        for i in range(2):
            w_in = np.asarray(hyb_w_in[i])
            m[f"hw{i}"] = _pack_hyb(w_in, j)
            m[f"hab{i}"] = f(np.concatenate([w_in[:, 6144 + 8 * j: 6144 + 8 * j + 8],
                                             w_in[:, 6160 + 8 * j: 6160 + 8 * j + 8]], axis=1))
            m[f"hcw{i}"] = _pack_hcw(np.asarray(dn_conv_w[i]), j)
            m[f"hal{i}"] = f(np.asarray(dn_a_log[i])[8 * j: 8 * j + 8].reshape(1, 8))
            m[f"hdt{i}"] = f(np.asarray(dn_dt_bias[i])[8 * j: 8 * j + 8].reshape(1, 8))
            m[f"hnw{i}"] = f(np.asarray(dn_norm_w[i]).reshape(1, 128))
            wo_ = np.asarray(hyb_w_out[i])
            m[f"hwo{i}"] = f(np.concatenate([wo_[1024 * j: 1024 * (j + 1)], wo_[2048 + 512 * j: 2048 + 512 * (j + 1)]], axis=0))
            m[f"sw{i}"] = _pack_sc(np.asarray(sc_w_in[i]), j)
            m[f"scw{i}"] = f(np.asarray(sc_conv_w[i])[1536 * j: 1536 * (j + 1)].reshape(12, 128, 3).transpose(1, 0, 2))
            m[f"swo{i}"] = f(np.asarray(sc_w_out[i])[1536 * j: 1536 * (j + 1)])
        halves.append(m)
    maps = []
    for c in range(8):
        m = dict(halves[c % 2])
        m["x"] = f(np.asarray(x)[c // 2])
        maps.append(m)
    return maps


def kernel(x, norm_w, hyb_w_in, dn_conv_w, dn_a_log, dn_dt_bias, dn_norm_w, hyb_w_out,
           sc_w_in, sc_conv_w, sc_w_out, final_norm_w):
    maps = make_in_maps(x, norm_w, hyb_w_in, dn_conv_w, dn_a_log, dn_dt_bias, dn_norm_w, hyb_w_out,
                        sc_w_in, sc_conv_w, sc_w_out, final_norm_w)
    if "nc" not in _NC_CACHE:
        _NC_CACHE["nc"] = build_program()
    res = run_bass_kernel_spmd(_NC_CACHE["nc"], maps, core_ids=list(range(8)))
    out = np.stack([np.asarray(res.results[2 * b]["out"], dtype=np.float32) for b in range(4)], axis=0)
    return out
```

```python
import numpy as np
from contextlib import ExitStack
import concourse.bass as bass
import concourse.mybir as mybir
from concourse.bass_utils import run_bass_kernel_spmd

F32 = mybir.dt.float32
BF16 = mybir.dt.bfloat16
AF = mybir.ActivationFunctionType
ALU = mybir.AluOpType

T = 2048
D = 2048
NT = 16
EPS = 1e-6
NEG = -30000.0

EPOCH = 12000
DMA_SLOTS = 8


class Buf:
    __slots__ = ("name", "w", "rs", "excl")

    def __init__(self, name="", excl=False):
        self.name = name
        self.w = None
        self.rs = []
        self.excl = excl


class Op:
    __slots__ = ("stream", "fn", "deps", "dma", "flagged", "fidx", "slot", "use", "n", "cc")

    def __init__(self, stream, fn, dma):
        self.stream = stream
        self.fn = fn
        self.dma = dma
        self.deps = []
        self.flagged = False
        self.fidx = 0
        self.slot = 0
        self.use = 0
        self.n = 0
        self.cc = False


class Sched:
    STREAMS = ("pe", "dve", "act", "pool", "sp")

    def __init__(self):
        self.ops = {s: [] for s in self.STREAMS}
        self.ndma = {s: 0 for s in self.STREAMS}
        self.nops = 0

    def op(self, stream, fn, reads=(), writes=(), dma=False, cc=False):
        o = Op(stream, fn, dma or cc)
        o.cc = cc
        o.n = self.nops
        self.nops += 1
        ex = [b for b in reads if b.excl]
        if ex:
            reads = [b for b in reads if not b.excl]
            writes = list(writes) + ex
        deps = {}
        for b in reads:
            if b.w is not None:
                deps[id(b.w)] = b.w
        for b in writes:
            if b.w is not None:
                deps[id(b.w)] = b.w
            for r in b.rs:
                deps[id(r)] = r
        best = {}
        for d in deps.values():
            if d is o:
                continue
            if d.dma:
                o.deps.append(d)
                continue
            if d.stream == stream and stream == "pe" and not dma:
                continue
            cur = best.get(d.stream)
            if cur is None or d.n > cur.n:
                best[d.stream] = d
        for d in best.values():
            o.deps.append(d)
            d.flagged = True
        for b in reads:
            b.rs.append(o)
        for b in writes:
            b.w = o
            b.rs = []
        if cc:
            self.ncc = getattr(self, "ncc", 0) + 1
            o.use = self.ncc
        elif dma:
            n = self.ndma[stream]
            self.ndma[stream] = n + 1
            o.slot = n % DMA_SLOTS
            o.use = n // DMA_SLOTS + 1
        self.ops[stream].append(o)
        return o

    def emit(self, nc, stack):
        nsem = {}
        for s in self.STREAMS:
            c = 0
            for o in self.ops[s]:
                if o.flagged and not o.dma:
                    c += 1
                    o.fidx = c
            nsem[s] = (max(c - 1, 0) // EPOCH) + 1
        csem = {}
        for s in self.STREAMS:
            for e in range(nsem[s]):
                csem[(s, e)] = stack.enter_context(nc.semaphore(f"c_{s}_{e}"))
        dsem = {}
        for s in self.STREAMS:
            if self.ndma[s] > 0:
                for k in range(DMA_SLOTS):
                    dsem[(s, k)] = stack.enter_context(nc.semaphore(f"d_{s}_{k}"))
        ccsem = stack.enter_context(nc.semaphore("ccsem")) if getattr(self, "ncc", 0) else None
        block = stack.enter_context(nc.Block())

        def run_stream(s, eng):
            waited = {}
            for o in self.ops[s]:
                need = {}
                for d in o.deps:
                    if d.cc:
                        key = ("cc", 0, 0)
                        val = d.use
                    elif d.dma:
                        key = ("d", d.stream, d.slot)
                        val = 16 * d.use
                    else:
                        e = (d.fidx - 1) // EPOCH
                        key = ("c", d.stream, e)
                        val = d.fidx - e * EPOCH
                    if need.get(key, 0) < val:
                        need[key] = val
                if o.cc:
                    if o.use > 1:
                        need[("cc", 0, 0)] = max(need.get(("cc", 0, 0), 0), o.use - 1)
                elif o.dma and o.use > 1:
                    key = ("d", s, o.slot)
                    val = 16 * (o.use - 1)
                    if need.get(key, 0) < val:
                        need[key] = val
                for key, val in need.items():
                    if waited.get(key, 0) >= val:
                        continue
                    waited[key] = val
                    sem = ccsem if key[0] == "cc" else (dsem[(key[1], key[2])] if key[0] == "d" else csem[(key[1], key[2])])
                    eng.wait_ge(sem, val)
                if o.fn is None:
                    continue
                ins = o.fn(eng)
                if o.cc:
                    ins.then_inc(ccsem, 1)
                elif o.dma:
                    ins.then_inc(dsem[(s, o.slot)], 16)
                elif o.flagged:
                    e = (o.fidx - 1) // EPOCH
                    ins.then_inc(csem[(s, e)], 1)

        if self.ops["pe"]:
            block.tensor(lambda eng: run_stream("pe", eng))
        if self.ops["dve"]:
            block.vector(lambda eng: run_stream("dve", eng))
        if self.ops["act"]:
            block.scalar(lambda eng: run_stream("act", eng))
        if self.ops["pool"]:
            block.gpsimd(lambda eng: run_stream("pool", eng))
        if self.ops["sp"]:
            block.sync(lambda eng: run_stream("sp", eng))


NQK = 4
NV = 8
NSW = 4
SWG = ((1, 0), (4, 1), (16, 2))


def build_program(layers=(0, 1, 2, 3), final_norm=True):
    nc = bass.Bass("TRN2", target_bir_lowering=False)

    def din(name, shape, dt=F32):
        return nc.dram_tensor(name, list(shape), dt, kind="ExternalInput").ap()

    def dscr(name, shape, dt=F32):
        return nc.dram_tensor(name, list(shape), dt).ap()

    x_d = din("x", [T, D])
    normw_d = din("normw", [128, 64])
    fnw_d = din("fnw", [1, D])
    consts_d = din("consts", [128, 9, 128])
    rope_d = din("rope", [128, 2, T])
    hyb_w = [din(f"hw{i}", [12, 128, 16, 512]) for i in range(2)]
    hyb_ab = [din(f"hab{i}", [D, 16]) for i in range(2)]
    hyb_cw = [din(f"hcw{i}", [128, 16, 4]) for i in range(2)]
    hyb_al = [din(f"hal{i}", [1, 8]) for i in range(2)]
    hyb_dt = [din(f"hdt{i}", [1, 8]) for i in range(2)]
    hyb_nw = [din(f"hnw{i}", [1, 128]) for i in range(2)]
    hyb_wo = [din(f"hwo{i}", [1536, D]) for i in range(2)]
    sc_w = [din(f"sw{i}", [12, 128, 16, 512]) for i in range(2)]
    sc_cw = [din(f"scw{i}", [128, 12, 3]) for i in range(2)]
    sc_wo = [din(f"swo{i}", [1536, D]) for i in range(2)]
    out_d = nc.dram_tensor("out", [T, D], F32, kind="ExternalOutput").ap()

    H = [dscr("Hs0", [T, D]), dscr("Hs1", [T, D])]
    YT = dscr("YT", [1536, T], BF16)
    ZS = dscr("ZS", [T, 1536])
    OA = dscr("OA", [T, 1024])
    Pp = dscr("Pp", [T, D])
    Ps = dscr("Ps", [T, D])
    OB = dscr("OB", [3, T, NSW, 129])

    S = Sched()
    with ExitStack() as st:
        def sb(name, shape, dt=F32):
            return st.enter_context(nc.sbuf_tensor(name, list(shape), dt))

        def bufs(name, n):
            return [Buf(f"{name}{i}") for i in range(n)]

        ps_f = [st.enter_context(nc.psum_tensor(f"ps{i}", [128, 512], F32)) for i in range(6)]
        ps_b = [Buf(f"ps{i}", excl=True) for i in range(6)]
        ps16 = [st.enter_context(nc.psum_tensor(f"ps16_{i}", [128, 1024], BF16)) for i in range(2)]
        ps16_b = [Buf(f"ps16_{i}", excl=True) for i in range(2)]
        ps_ctr = [0, 0]
        ACC0 = 4

        ps_nrot = [6]

        def psum():
            i = ps_ctr[0] % ps_nrot[0]
            ps_ctr[0] += 1
            return ps_f[i], ps_b[i]

        def psum16(full=False):
            i = ps_ctr[1] % 2
            ps_ctr[1] += 1
            if full:
                return ps16[i], ps16_b[i]
            return ps16[i][:, 0:128], ps16_b[i]

        cst = sb("cst", [128, 4, 128])
        cst_b = Buf("cst")
        S.op("sp", lambda e: e.dma_start(out=cst[:], in_=consts_d[:, 0:4, :]), writes=[cst_b], dma=True)
        cstb = sb("cstb", [128, 5, 128], BF16)
        cstb_b = Buf("cstb")
        S.op("pool", lambda e: e.dma_start(out=cstb[:], in_=consts_d[:, 4:9, :]), writes=[cstb_b], dma=True)
        ident_f, UT, MBT, SMT = cst[:, 0, :], cst[:, 1, :], cst[:, 2, :], cst[:, 3, :]
        ident_b, permT_b, ones_b = cstb[:, 0, :], cstb[:, 1, :], cstb[:, 2, :]
        normw = sb("normw_s", [128, 64])
        normw_b = Buf("normw")
        S.op("sp", lambda e: e.dma_start(out=normw[:], in_=normw_d[:, :]), writes=[normw_b], dma=True)

        arena = sb("arena", [128, 49152], BF16)
        hnT = arena[:, 0:32768].rearrange("p (c t) -> p c t", c=16)
        wblk = [arena[:, 32768 + i * 8192: 32768 + (i + 1) * 8192].rearrange("p (c n) -> p c n", c=16)
                for i in range(2)]
        wo = arena[:, 0:24576].rearrange("p (c n) -> p c n", c=12)
        tokA = Buf("tokA")
        hnT_b = bufs("hnT", NT)
        wblk_b = bufs("wblk", 2)
        wo_b = bufs("wo", 12)
        fence = sb("fence", [128, 4])
        wctr = [0]

        def load_wblk(src_ap, ncols=512):
            i = wctr[0] % 2
            wctr[0] += 1
            S.op("pool", lambda e: e.dma_start(out=wblk[i][:, :, 0:ncols], in_=src_ap),
                 reads=[tokA], writes=[wblk_b[i]], dma=True)
            return wblk[i], wblk_b[i]

        class WStream:
            def __init__(self, srcs):
                self.srcs, self.pos, self.q = list(srcs), 0, []

            def _pf(self):
                if self.pos < len(self.srcs):
                    self.q.append(load_wblk(self.srcs[self.pos]))
                    self.pos += 1

            def get(self):
                if not self.q:
                    self._pf()
                r = self.q.pop(0)
                self._pf()
                return r

        FW = sb("FW", [128, 4 * 2052])
        Fv = [FW[:, i * 2052:(i + 1) * 2052] for i in range(4)]
        F_b = bufs("F", 4)
        BT = [sb(f"BT{i}", [128, T], BF16) for i in range(6)]
        BT_b = bufs("BT", 6)
        hbuf = sb("hbuf", [128, D])
        hbuf_b = Buf("hbuf")
        SM = [sb(f"SM{i}", [128, 512]) for i in range(4)]
        SM_b = bufs("SM", 4)
        col = sb("colstat", [128, 64])
        col_b = bufs("col", 64)
        ytile = sb("ytile", [128, 12, 128], BF16)
        ytile_b = Buf("ytile")
        Hb = {}
        YT_b = bufs("YT", 12)
        Pp_b = bufs("Pp", NT)
        Ps_b = bufs("Ps", 4)
        ZS_b = bufs("ZS", NT)
        OA_b = bufs("OA", NT)
        OB_b = bufs("OB", NT)
        out_b = bufs("out", NT)
        hs, hs_b = Fv[3][:, 0:D], F_b[3]
        junk, junk_b = BT[5], BT_b[5]

        def load_h(h_src, h_dst, tt, delta):
            S.op("sp", lambda e: e.dma_start(out=hbuf[:], in_=h_src[tt * 128:(tt + 1) * 128, :]),
                 reads=[Hb[id(h_src)][tt]], writes=[hbuf_b], dma=True)
            if delta:
                S.op("sp", lambda e: e.dma_start(out=hs, in_=Ps[tt * 128:(tt + 1) * 128, :]),
                     reads=[Ps_b[tt // 4]], writes=[hs_b], dma=True)
                S.op("dve", lambda e: e.tensor_tensor(out=hbuf[:], in0=hbuf[:], in1=hs, op=ALU.add),
                     reads=[hbuf_b, hs_b], writes=[hbuf_b])
                if h_dst is not None:
                    S.op("sp", lambda e: e.dma_start(out=h_dst[tt * 128:(tt + 1) * 128, :], in_=hbuf[:]),
                         reads=[hbuf_b], writes=[Hb[id(h_dst)][tt]], dma=True)

        def norm_phase(h_src, h_dst, delta, layer):
            S.op("dve", lambda e: e.memset(fence[:, 0:1], 0.0), writes=[tokA])
            for tt in range(NT):
                load_h(h_src, h_dst, tt, delta)
                c0, c0b = col[:, 0:1], col_b[0]
                c1, c1b = col[:, 1:2], col_b[1]
                S.op("act", lambda e: e.activation(out=junk[:], in_=hbuf[:], func=AF.Square, accum_out=c0),
                     reads=[hbuf_b], writes=[junk_b, c0b])
                S.op("act", lambda e: e.activation(out=c1, in_=c0, func=AF.Sqrt, bias=EPS, scale=1.0 / D),
                     reads=[c0b], writes=[c1b])
                S.op("dve", lambda e: e.reciprocal(c1, c1), reads=[c1b], writes=[c1b])
                S.op("act", lambda e: e.mul(hs, hbuf[:], c1), reads=[hbuf_b, c1b], writes=[hs_b])
                for q in range(4):
                    pt, ptb = psum()
                    for j in range(4):
                        dc = q * 4 + j
                        S.op("pe", lambda e, pt=pt, j=j, dc=dc: e.transpose(
                            pt[:, j * 128:(j + 1) * 128], hs[:, dc * 128:(dc + 1) * 128], ident_f),
                            reads=[hs_b, cst_b], writes=[ptb])
                    nwv = normw[:, layer * 16 + q * 4: layer * 16 + q * 4 + 4].unsqueeze(2).broadcast_to([128, 4, 128])
                    S.op("dve", lambda e, pt=pt, q=q, tt=tt, nwv=nwv: e.tensor_tensor(
                        out=hnT[:, q * 4:(q + 1) * 4, tt * 128:(tt + 1) * 128],
                        in0=pt[:, :].rearrange("p (a b) -> p a b", a=4), in1=nwv, op=ALU.mult),
                        reads=[ptb, normw_b, tokA], writes=[hnT_b[tt]])

        def load_wo(wo_d):
            for c in range(12):
                S.op("pool", lambda e, c=c: e.dma_start(out=wo[:, c, :], in_=wo_d[c * 128:(c + 1) * 128, :]),
                     writes=[tokA, wo_b[c]] if c == 0 else [wo_b[c]], reads=[] if c == 0 else [tokA], dma=True)

        RG = [[0, 1], [2, 3], [4, 5], [6, 7]]

        def outproj_tile(tt):
            po, po_b = Fv[1][:, 0:D], F_b[1]
            for cb in range(4):
                pt, ptb = psum()
                for c in range(12):
                    S.op("pe", lambda e, pt=pt, c=c, cb=cb: e.matmul(
                        pt[:, :], ytile[:, c, :], wo[:, c, cb * 512:(cb + 1) * 512], start=(c == 0), stop=(c == 11)),
                        reads=[ytile_b, wo_b[c], tokA], writes=[ptb])
                if cb % 2 == 0:
                    S.op("act", lambda e, pt=pt, cb=cb: e.copy(po[:, cb * 512:(cb + 1) * 512], pt[:, :]),
                         reads=[ptb], writes=[po_b])
                else:
                    S.op("dve", lambda e, pt=pt, cb=cb: e.tensor_copy(po[:, cb * 512:(cb + 1) * 512], pt[:, :]),
                         reads=[ptb], writes=[po_b])
            S.op("sp", lambda e: e.dma_start(out=Pp[tt * 128:(tt + 1) * 128, :], in_=po),
                 reads=[po_b], writes=[Pp_b[tt]], dma=True)
            if tt % 4 == 3:
                q = tt // 4
                S.op("pool", lambda e: e.collective_compute(
                    "AllReduce", ALU.add, replica_groups=RG,
                    ins=[Pp[q * 512:(q + 1) * 512, :]], outs=[Ps[q * 512:(q + 1) * 512, :]]),
                    reads=Pp_b[q * 4:(q + 1) * 4], writes=[Ps_b[q]], cc=True)

        cu, cu_b = Fv[0][:, 0:2 + T], F_b[0]
        gate, gate_b = Fv[1][:, 0:T], F_b[1]
        acc, acc_b = Fv[2][:, 0:T], F_b[2]
        sccw = sb("sccw", [128, 12, 3])
        sccw_b = Buf("sccw")

        def sc_layer(li, layer, h_src, h_dst, delta):
            norm_phase(h_src, h_dst, delta, layer)
            S.op("sp", lambda e: e.dma_start(out=sccw[:], in_=sc_cw[li][:, :, :]), writes=[sccw_b], dma=True)
            ws = WStream([sc_w[li][ct] for ct in range(12)])
            for ct in range(12):
                wb, wbb = ws.get()
                S.op("dve", lambda e: e.memset(cu[:, 0:2], 0.0), writes=[cu_b])
                for tg in range(4):
                    pp = []
                    for part in range(4):
                        pt, ptb = psum()
                        for dc in range(16):
                            S.op("pe", lambda e, pt=pt, dc=dc, part=part, tg=tg, wb=wb: e.matmul(
                                pt[:, :], wb[:, dc, part * 128:(part + 1) * 128], hnT[:, dc, tg * 512:(tg + 1) * 512],
                                start=(dc == 0), stop=(dc == 15)),
                                reads=[wbb, tokA] + hnT_b[tg * 4:(tg + 1) * 4], writes=[ptb])
                        pp.append((pt, ptb))
                    (pb_, pbb), (pc_, pcb), (pu_, pub), (pz_, pzb) = pp
                    ua, uab = SM[0], SM_b[0]
                    za, zab = SM[1], SM_b[1]
                    S.op("act", lambda e, pu_=pu_: e.copy(ua[:], pu_[:, :]), reads=[pub], writes=[uab])
                    S.op("dve", lambda e, pc_=pc_, tg=tg: e.tensor_tensor(
                        out=cu[:, 2 + tg * 512: 2 + (tg + 1) * 512], in0=pc_[:, :], in1=ua[:], op=ALU.mult),
                        reads=[pcb, uab], writes=[cu_b])
                    S.op("act", lambda e, pz_=pz_: e.activation(out=za[:], in_=pz_[:, :], func=AF.Silu),
                         reads=[pzb], writes=[zab])
                    S.op("dve", lambda e, pb_=pb_, tg=tg: e.tensor_tensor(
                        out=gate[:, tg * 512:(tg + 1) * 512], in0=pb_[:, :], in1=za[:], op=ALU.mult),
                        reads=[pbb, zab], writes=[gate_b])
                S.op("act", lambda e, ct=ct: e.mul(acc, cu[:, 2:2 + T], sccw[:, ct, 2:3]),
                     reads=[cu_b, sccw_b], writes=[acc_b])
                for i in (1, 0):
                    S.op("dve", lambda e, ct=ct, i=i: e.scalar_tensor_tensor(
                        out=acc, in0=cu[:, i:i + T], scalar=sccw[:, ct, i:i + 1], in1=acc,
                        op0=ALU.mult, op1=ALU.add), reads=[cu_b, sccw_b, acc_b], writes=[acc_b])
                yb, ybb = BT[ct % 2], BT_b[ct % 2]
                S.op("dve", lambda e, yb=yb: e.tensor_tensor(out=yb[:], in0=acc, in1=gate, op=ALU.mult),
                     reads=[acc_b, gate_b], writes=[ybb])
                S.op("sp", lambda e, yb=yb, ct=ct: e.dma_start(out=YT[ct * 128:(ct + 1) * 128, :], in_=yb[:]),
                     reads=[ybb], writes=[YT_b[ct]], dma=True)
            load_wo(sc_wo[li])
            for tt in range(NT):
                S.op("sp", lambda e, tt=tt: e.dma_start(
                    out=ytile[:], in_=YT[:, tt * 128:(tt + 1) * 128].rearrange("(c p) t -> p c t", p=128)),
                    reads=YT_b, writes=[ytile_b], dma=True)
                outproj_tile(tt)

        def final_phase(h_src, delta):
            fw, fw_b = Fv[0][:, 0:D], F_b[0]
            S.op("sp", lambda e: e.dma_start(out=fw, in_=fnw_d[0:1, :].broadcast_to([128, D])),
                 writes=[fw_b], dma=True)
            for tt in range(NT):
                load_h(h_src, None, tt, delta)
                c0, c0b = col[:, 0:1], col_b[0]
                c1, c1b = col[:, 1:2], col_b[1]
                S.op("act", lambda e: e.activation(out=junk[:], in_=hbuf[:], func=AF.Square, accum_out=c0),
                     reads=[hbuf_b], writes=[junk_b, c0b])
                S.op("act", lambda e: e.activation(out=c1, in_=c0, func=AF.Sqrt, bias=EPS, scale=1.0 / D),
                     reads=[c0b], writes=[c1b])
                S.op("dve", lambda e: e.reciprocal(c1, c1), reads=[c1b], writes=[c1b])
                S.op("dve", lambda e: e.scalar_tensor_tensor(
                    out=hs, in0=hbuf[:], scalar=c1, in1=fw, op0=ALU.mult, op1=ALU.mult),
                    reads=[hbuf_b, c1b, fw_b], writes=[hs_b])
                S.op("sp", lambda e, tt=tt: e.dma_start(out=out_d[tt * 128:(tt + 1) * 128, :], in_=hs),
                     reads=[hs_b], writes=[out_b[tt]], dma=True)

        HYB_TILES = {}

        def hyb_tiles():
            if HYB_TILES:
                return HYB_TILES
            d = HYB_TILES
            d["cw"] = sb("hcw_s", [128, 16, 4])
            d["wab"] = sb("wab_s", [128, 16, 16], BF16)
            d["bet"] = sb("bet", [128, 16, 8])
            d["gr"] = sb("gr", [128, 16, 8])
            d["bc16"] = sb("bc16", [128, 3, 8])
            d["dnw"] = sb("dnw", [128, 128])
            d["dn"] = sb("dnwork", [128, 12, 256])
            d["rb"] = sb("rbwork", [128, 22, 128], BF16)
            d["S32"] = sb("S32", [128, 2, 128])
            d["Sb"] = sb("Sbf", [128, 2, 128], BF16)
            d["qbt"] = sb("qbt", [128, 512], BF16)
            d["VA"] = sb("VA", [128, 3, 132], BF16)
            d["oev"] = sb("oev", [128, 2, 132])
            for k in list(d.keys()):
                d[k + "_b"] = Buf(k)
            d["dn_bs"] = bufs("dnw", 12)
            d["nn_bs"] = bufs("nn", 4)
            d["qo_bs"] = bufs("qo", 4)
            d["rb_bs"] = bufs("rb", 22)
            d["PTa"] = [sb(f"PTa{i}", [128, 256], BF16) for i in range(3)]
            d["PTa_bs"] = bufs("PTa", 3)
            d["OBw"] = []
            d["oc_b"] = Buf("oc")
            d["VA_bs"] = bufs("VA", 3)
            d["oev_bs"] = bufs("oev", 2)
            d["S_bs"] = bufs("S", 2)
            return d

        def hyb_layer(li, layer, h_src, h_dst, delta):
            d = hyb_tiles()
            norm_phase(h_src, h_dst, delta, layer)
            cw, cw_b = d["cw"], d["cw_b"]
            S.op("sp", lambda e: e.dma_start(out=cw[:], in_=hyb_cw[li][:, :, :]), writes=[cw_b], dma=True)
            wab, wab_b = d["wab"], d["wab_b"]
            S.op("pool", lambda e: e.dma_start(out=wab[:], in_=hyb_ab[li].rearrange("(c p) n -> p c n", p=128)),
                 writes=[wab_b], dma=True)
            bc16, bc16_b = d["bc16"], d["bc16_b"]
            S.op("sp", lambda e: e.dma_start(out=bc16[:, 0, :], in_=hyb_dt[li][0:1, :].broadcast_to([128, 8])),
                 writes=[bc16_b], dma=True)
            S.op("sp", lambda e: e.dma_start(out=bc16[:, 1, :], in_=hyb_al[li][0:1, :].broadcast_to([128, 8])),
                 reads=[bc16_b], writes=[bc16_b], dma=True)
            dnw, dnw_b = d["dnw"], d["dnw_b"]
            S.op("sp", lambda e: e.dma_start(out=dnw[:], in_=hyb_nw[li][0:1, :].broadcast_to([128, 128])),
                 writes=[dnw_b], dma=True)
            S.op("act", lambda e: e.activation(out=bc16[:, 1, :], in_=bc16[:, 1, :], func=AF.Exp),
                 reads=[bc16_b], writes=[bc16_b])
            S.op("dve", lambda e: e.tensor_scalar_mul(bc16[:, 1, :], bc16[:, 1, :], -1.0),
                 reads=[bc16_b], writes=[bc16_b])
            bet, bet_b, gr, gr_b = d["bet"], d["bet_b"], d["gr"], d["gr_b"]
            t1, t1b = SM[2], SM_b[2]
            t2, t2b = SM[3], SM_b[3]
            for tt in range(NT):
                pt, ptb = psum()
                for dc in range(16):
                    S.op("pe", lambda e, pt=pt, dc=dc, tt=tt: e.matmul(
                        pt[:, 0:16], hnT[:, dc, tt * 128:(tt + 1) * 128], wab[:, dc, :],
                        start=(dc == 0), stop=(dc == 15)), reads=[wab_b, hnT_b[tt], tokA], writes=[ptb])
                S.op("act", lambda e, pt=pt, tt=tt: e.activation(out=bet[:, tt, :], in_=pt[:, 0:8], func=AF.Sigmoid),
                     reads=[ptb], writes=[bet_b])
                S.op("dve", lambda e, pt=pt: e.tensor_tensor(out=t1[:, 0:8], in0=pt[:, 8:16], in1=bc16[:, 0, :],
                                                             op=ALU.add), reads=[ptb, bc16_b], writes=[t1b])
                S.op("act", lambda e: e.activation(out=t2[:, 0:8], in_=t1[:, 0:8], func=AF.Abs),
                     reads=[t1b], writes=[t2b])
                S.op("act", lambda e: e.activation(out=t2[:, 0:8], in_=t2[:, 0:8], func=AF.Exp, scale=-1.0),
                     reads=[t2b], writes=[t2b])
                S.op("act", lambda e: e.activation(out=t2[:, 0:8], in_=t2[:, 0:8], func=AF.Ln, bias=1.0),
                     reads=[t2b], writes=[t2b])
                S.op("dve", lambda e: e.scalar_tensor_tensor(out=t1[:, 0:8], in0=t1[:, 0:8], scalar=0.0,
                                                             in1=t2[:, 0:8], op0=ALU.max, op1=ALU.add),
                     reads=[t1b, t2b], writes=[t1b])
                S.op("dve", lambda e, tt=tt: e.tensor_tensor(out=gr[:, tt, :], in0=t1[:, 0:8], in1=bc16[:, 1, :],
                                                             op=ALU.mult), reads=[t1b, bc16_b], writes=[gr_b])
            seq = [9, 10, 11, 0, 1, 2, 3]
            for h in range(NSW):
                seq += [4 + h, 8]
            d["ws"] = WStream([hyb_w[li][k] for k in seq])
            for zb in range(3):
                wb, wbb = d["ws"].get()
                for tt in range(NT):
                    pt, ptb = psum()
                    for dc in range(16):
                        S.op("pe", lambda e, pt=pt, dc=dc, tt=tt, wb=wb: e.matmul(
                            pt[:, :], hnT[:, dc, tt * 128:(tt + 1) * 128], wb[:, dc, :],
                            start=(dc == 0), stop=(dc == 15)), reads=[wbb, hnT_b[tt], tokA], writes=[ptb])
                    zt, ztb = SM[tt % 2], SM_b[tt % 2]
                    S.op("act", lambda e, pt=pt, zt=zt: e.activation(out=zt[:], in_=pt[:, :], func=AF.Silu),
                         reads=[ptb], writes=[ztb])
                    S.op("sp", lambda e, zt=zt, tt=tt, zb=zb: e.dma_start(
                        out=ZS[tt * 128:(tt + 1) * 128, zb * 512:(zb + 1) * 512], in_=zt[:]),
                        reads=[ztb], writes=[ZS_b[tt]], dma=True)
            for g in range(NQK):
                dn_head(d, li, g)
            S.op("dve", lambda e: e.memset(d["VA"][:, :, 128:132], 1.0), writes=d["VA_bs"])
            ps_nrot[0] = 4
            for h in range(NSW):
                sw_head(d, li, h)
            ps_nrot[0] = 6
            load_wo(hyb_wo[li])
            for tt in range(NT):
                combine_tile(d, tt)
                if tt == NT - 1:
                    d["OBw"] = []
                outproj_tile(tt)

        def proj_cm(wb, wbb, part, evac):
            for tg in range(4):
                pt, ptb = psum()
                for dc in range(16):
                    S.op("pe", lambda e, pt=pt, dc=dc, tg=tg: e.matmul(
                        pt[:, :], wb[:, dc, part * 128:(part + 1) * 128], hnT[:, dc, tg * 512:(tg + 1) * 512],
                        start=(dc == 0), stop=(dc == 15)),
                        reads=[wbb, tokA] + hnT_b[tg * 4:(tg + 1) * 4], writes=[ptb])
                evac(tg, pt, ptb)

        def dn_head(d, li, g):
            cw, cw_b = d["cw"], d["cw_b"]
            wb, wbb = d["ws"].get()
            qs, qs_b = Fv[3][:, 0:T], F_b[3]
            accv, accv_b = Fv[2][:, 0:T], F_b[2]
            sq, sq_b = BT[5], BT_b[5]
            for part in range(4):
                raw, raw_b = Fv[part % 2], F_b[part % 2]
                S.op("dve", lambda e, raw=raw: e.memset(raw[:, 0:3], 0.0), writes=[raw_b])

                def ev(tg, pt, ptb, raw=raw, raw_b=raw_b):
                    S.op("act", lambda e: e.copy(raw[:, 3 + tg * 512: 3 + (tg + 1) * 512], pt[:, :]),
                         reads=[ptb], writes=[raw_b])
                proj_cm(wb, wbb, part, ev)
                tl = g * 4 + part
                S.op("act", lambda e, raw=raw, tl=tl: e.mul(accv, raw[:, 3:3 + T], cw[:, tl, 3:4]),
                     reads=[raw_b, cw_b], writes=[accv_b])
                for i in (2, 1, 0):
                    S.op("dve", lambda e, raw=raw, tl=tl, i=i: e.scalar_tensor_tensor(
                        out=accv, in0=raw[:, i:i + T], scalar=cw[:, tl, i:i + 1], in1=accv,
                        op0=ALU.mult, op1=ALU.add), reads=[raw_b, cw_b, accv_b], writes=[accv_b])
                if part < 2:
                    S.op("act", lambda e: e.activation(out=qs, in_=accv, func=AF.Silu), reads=[accv_b], writes=[qs_b])
                    S.op("act", lambda e: e.activation(out=sq[:], in_=qs, func=AF.Square), reads=[qs_b], writes=[sq_b])
                    for tg in range(4):
                        pt, ptb = psum()
                        S.op("pe", lambda e, pt=pt, tg=tg: e.matmul(pt[:, :], ones_b, sq[:, tg * 512:(tg + 1) * 512],
                                                                    start=True, stop=True),
                             reads=[sq_b, cstb_b], writes=[ptb])
                        rn, rnb = SM[tg % 2], SM_b[tg % 2]
                        S.op("act", lambda e, pt=pt, rn=rn: e.activation(out=rn[:], in_=pt[:, :], func=AF.Sqrt,
                                                                         bias=EPS, scale=1.0),
                             reads=[ptb], writes=[rnb])
                        S.op("dve", lambda e, rn=rn: e.reciprocal(rn[:], rn[:]), reads=[rnb], writes=[rnb])
                        sc_ = (128.0 ** -0.5) if part == 0 else 1.0
                        S.op("dve", lambda e, rn=rn, tg=tg, part=part, sc_=sc_: e.scalar_tensor_tensor(
                            out=BT[part][:, tg * 512:(tg + 1) * 512], in0=qs[:, tg * 512:(tg + 1) * 512], scalar=sc_,
                            in1=rn[:], op0=ALU.mult, op1=ALU.mult), reads=[qs_b, rnb], writes=[BT_b[part]])
                else:
                    S.op("act", lambda e, part=part: e.activation(out=BT[part][:], in_=accv, func=AF.Silu),
                         reads=[accv_b], writes=[BT_b[part]])
            qn, kn = BT[0], BT[1]
            qkb = [BT_b[0], BT_b[1]]
            dn, dn_bs, rb, rb_bs = d["dn"], d["dn_bs"], d["rb"], d["rb_bs"]
            S32, Sb, S_bs = d["S32"], d["Sb"], d["S_bs"]
            bet, bet_b, gr, gr_b = d["bet"], d["bet_b"], d["gr"], d["gr_b"]
            dnw, dnw_b = d["dnw"], d["dnw_b"]
            for e_ in range(2):
                S.op("dve", lambda e, e_=e_: e.memset(S32[:, e_, :], 0.0), writes=[S_bs[e_]])
                S.op("dve", lambda e, e_=e_: e.memset(Sb[:, e_, :], 0.0), reads=[S_bs[e_]], writes=[S_bs[e_]])
            def shared_pre(tt):
                par = tt % 2
                ts = slice(tt * 128, (tt + 1) * 128)
                ktok, ktok_b = rb[:, par, :], rb_bs[par]
                p16, p16b = psum16()
                S.op("pe", lambda e: e.transpose(p16, kn[:, ts], ident_b), reads=[qkb[1], cstb_b], writes=[p16b])
                S.op("act", lambda e: e.copy(ktok, p16), reads=[p16b], writes=[ktok_b])
                kkqk, kkqk_b = dn[:, 10 + par, :], dn_bs[10 + par]
                pt, ptb = psum()
                S.op("pe", lambda e: e.matmul(pt[:, 0:128], kn[:, ts], kn[:, ts], start=True, stop=True),
                     reads=[qkb[1]], writes=[ptb])
                S.op("pe", lambda e: e.matmul(pt[:, 128:256], kn[:, ts], qn[:, ts], start=True, stop=True),
                     reads=qkb, writes=[ptb])
                S.op("act", lambda e: e.copy(kkqk, pt[:, 0:256]), reads=[ptb], writes=[kkqk_b])
                yield

            def tiles(e_, tt):
                par = tt % 2
                hv = 2 * g + e_
                t = {}
                t["ts"] = slice(tt * 128, (tt + 1) * 128)
                t["ktok"], t["ktok_b"] = rb[:, par, :], rb_bs[par]
                t["kkqk"], t["kkqk_b"] = dn[:, 10 + par, :], dn_bs[10 + par]
                i = 2 + e_ * 2 + par
                t["vtok"], t["vtok_b"] = rb[:, i, :], rb_bs[i]
                for j, nm in enumerate(("PTt", "Kd", "TT")):
                    i = 6 + (e_ * 2 + par) * 3 + j
                    t[nm], t[nm + "_b"] = rb[:, i, :], rb_bs[i]
                for j, nm in enumerate(("Rt", "vnew")):
                    i = 18 + e_ * 2 + j
                    t[nm], t[nm + "_b"] = rb[:, i, :], rb_bs[i]
                c0 = 40 + (e_ * 2 + par) * 4
                for j, nm in enumerate(("ngc", "eg", "neg", "egl")):
                    t[nm], t[nm + "_b"] = col[:, c0 + j:c0 + j + 1], col_b[c0 + j]
                c0 = 56 + e_ * 2
                for j, nm in enumerate(("ss", "rs")):
                    t[nm], t[nm + "_b"] = col[:, c0 + j:c0 + j + 1], col_b[c0 + j]
                t["gcol"] = gr[:, tt, hv:hv + 1]
                t["bcol"] = bet[:, tt, hv:hv + 1]
                w0 = e_ * 5
                t["DT"], t["DTs"], t["DT_b"] = dn[:, w0, 0:128], dn[:, w0, 128:256], dn_bs[w0]
                t["AB"] = [dn[:, w0 + 1, :], dn[:, w0 + 2, :]]
                t["AB_b"] = [dn_bs[w0 + 1], dn_bs[w0 + 2]]
                t["Nn"] = [dn[:, w0 + 3, 0:128], dn[:, w0 + 3, 128:256]]
                t["Nn_b"] = [d["nn_bs"][e_ * 2], d["nn_bs"][e_ * 2 + 1]]
                t["QSs"], t["QSs_b"] = dn[:, w0 + 4, 0:128], d["qo_bs"][e_ * 2]
                t["osb"], t["osb_b"] = dn[:, w0 + 4, 128:256], d["qo_bs"][e_ * 2 + 1]
                t["hv"] = hv
                return t

            def pre(e_, tt):
                t = tiles(e_, tt)
                ts, hv = t["ts"], t["hv"]
                vT, vT_b = BT[2 + e_], BT_b[2 + e_]
                vtok, vtok_b = t["vtok"], t["vtok_b"]
                gcol, bcol = t["gcol"], t["bcol"]
                ngc, eg, neg, egl = t["ngc"], t["eg"], t["neg"], t["egl"]
                ngcb, egb, negb, eglb = t["ngc_b"], t["eg_b"], t["neg_b"], t["egl_b"]
                DT, DTs, DT_b, AB, AB_b, Nn, Nn_b = t["DT"], t["DTs"], t["DT_b"], t["AB"], t["AB_b"], t["Nn"], t["Nn_b"]
                kkqk, kkqk_b, ktok, ktok_b = t["kkqk"], t["kkqk_b"], t["ktok"], t["ktok_b"]
                PTt, PTt_b, Kd, Kd_b, TT, TT_b = t["PTt"], t["PTt_b"], t["Kd"], t["Kd_b"], t["TT"], t["TT_b"]
                tmp, tmp_b = AB[1][:, 0:128], AB_b[1]
                p16, p16b = psum16()
                S.op("pe", lambda e: e.transpose(p16, vT[:, ts], ident_b), reads=[vT_b, cstb_b], writes=[p16b])
                S.op("act", lambda e: e.copy(vtok, p16), reads=[p16b], writes=[vtok_b])
                pg, pgb = psum()
                S.op("pe", lambda e: e.matmul(pg[:, 0:128], gcol.broadcast_to([128, 128]), UT, start=True, stop=True),
                     reads=[gr_b, cst_b], writes=[pgb])
                S.op("pe", lambda e: e.matmul(pg[:, 128:129], UT, gcol, start=True, stop=True),
                     reads=[gr_b, cst_b], writes=[pgb])
                S.op("act", lambda e: e.mul(ngc, pg[:, 128:129], -1.0), reads=[pgb], writes=[ngcb])
                S.op("act", lambda e: e.activation(out=eg, in_=pg[:, 128:129], func=AF.Exp), reads=[pgb], writes=[egb])
                S.op("act", lambda e: e.activation(out=egl, in_=pg[:, 127:128], func=AF.Exp), reads=[pgb], writes=[eglb])
                S.op("dve", lambda e: e.tensor_tensor(out=tmp, in0=pg[:, 0:128], in1=MBT, op=ALU.add),
                     reads=[pgb, cst_b], writes=[tmp_b])
                S.op("dve", lambda e: e.tensor_scalar_mul(neg, eg, -1.0), reads=[egb], writes=[negb])
                yield
                S.op("act", lambda e: e.activation(out=DT, in_=tmp, func=AF.Exp, bias=ngc, scale=1.0),
                     reads=[tmp_b, ngcb], writes=[DT_b])
                S.op("pool", lambda e: e.tensor_tensor(out=DTs, in0=DT, in1=SMT, op=ALU.mult),
                     reads=[DT_b, cst_b], writes=[DT_b])
                S.op("dve", lambda e: e.scalar_tensor_tensor(
                    out=AB[0][:, 128:256], in0=kkqk[:, 0:128], scalar=bcol, in1=DTs, op0=ALU.mult, op1=ALU.mult),
                    reads=[kkqk_b, bet_b, DT_b], writes=[AB_b[0]])
                S.op("pool", lambda e: e.tensor_tensor(out=PTt, in0=kkqk[:, 128:256], in1=DT, op=ALU.mult),
                     reads=[kkqk_b, DT_b], writes=[PTt_b])
                S.op("act", lambda e: e.mul(Kd, ktok, DT[:, 127:128]), reads=[ktok_b, DT_b], writes=[Kd_b])
                yield
                pa0, pa0b = psum()
                S.op("pe", lambda e: e.transpose(pa0[:, 0:128], AB[0][:, 128:256], ident_f),
                     reads=[AB_b[0], cst_b], writes=[pa0b])
                S.op("act", lambda e: e.copy(AB[0][:, 0:128], pa0[:, 0:128]), reads=[pa0b], writes=[AB_b[0]])
                S.op("dve", lambda e: e.tensor_tensor(out=Nn[0], in0=ident_f, in1=AB[0][:, 128:256], op=ALU.subtract),
                     reads=[AB_b[0], cst_b], writes=[Nn_b[0]])
                yield
                for k in range(1, 7):
                    prv, nxt_ = AB[(k - 1) % 2], AB[k % 2]
                    prvb, nxtb = AB_b[(k - 1) % 2], AB_b[k % 2]
                    pa, pab = psum()
                    S.op("pe", lambda e, pa=pa, prv=prv: e.matmul(pa[:, 0:128], prv[:, 128:256], prv[:, 0:128],
                                                                  start=True, stop=True), reads=[prvb], writes=[pab])
                    if k < 6:
                        S.op("pe", lambda e, pa=pa, prv=prv: e.matmul(pa[:, 128:256], prv[:, 0:128], prv[:, 128:256],
                                                                      start=True, stop=True), reads=[prvb], writes=[pab])
                    S.op("act", lambda e, pa=pa, nxt_=nxt_: e.copy(nxt_[:, 0:256], pa[:, 0:256]), reads=[pab],
                         writes=[nxtb])
                    yield
                    npv, npvb = Nn[(k - 1) % 2], Nn_b[(k - 1) % 2]
                    pn, pnb = psum()
                    S.op("pe", lambda e, pn=pn, nxt_=nxt_, npv=npv: e.matmul(pn[:, 0:128], nxt_[:, 0:128], npv,
                                                                             start=True, stop=True),
                         reads=[nxtb, npvb], writes=[pnb])
                    if k < 6:
                        nnx, nnxb = Nn[k % 2], Nn_b[k % 2]
                    else:
                        nnx, nnxb = TT, TT_b
                    S.op("dve", lambda e, pn=pn, npv=npv, nnx=nnx: e.tensor_tensor(out=nnx, in0=pn[:, 0:128],
                                                                                   in1=npv, op=ALU.add),
                         reads=[pnb, npvb], writes=[nnxb])
                    yield

            def scan(e_, tt):
                t = tiles(e_, tt)
                ts, hv = t["ts"], t["hv"]
                vtok, vtok_b, bcol = t["vtok"], t["vtok_b"], t["bcol"]
                eg, neg, egl, ss, rs = t["eg"], t["neg"], t["egl"], t["ss"], t["rs"]
                egb, negb, eglb, ssb, rsb = t["eg_b"], t["neg_b"], t["egl_b"], t["ss_b"], t["rs_b"]
                PTt, PTt_b, Kd, Kd_b, TT, TT_b = t["PTt"], t["PTt_b"], t["Kd"], t["Kd_b"], t["TT"], t["TT_b"]
                Rt, Rt_b, vnew, vnew_b = t["Rt"], t["Rt_b"], t["vnew"], t["vnew_b"]
                QSs, QSs_b, osb, osb_b = t["QSs"], t["QSs_b"], t["osb"], t["osb_b"]
                Sbe = Sb[:, e_, :]
                S32e = S32[:, e_, :]
                p1, p1b = psum()
                S.op("pe", lambda e: e.matmul(p1[:, 0:128], kn[:, ts], Sbe, start=True, stop=True),
                     reads=[qkb[1], S_bs[e_]], writes=[p1b])
                S.op("pe", lambda e: e.matmul(p1[:, 128:256], qn[:, ts], Sbe, start=True, stop=True),
                     reads=[qkb[0], S_bs[e_]], writes=[p1b])
                S.op("dve", lambda e: e.scalar_tensor_tensor(
                    out=Rt, in0=p1[:, 0:128], scalar=neg, in1=vtok, op0=ALU.mult, op1=ALU.add),
                    reads=[p1b, negb, vtok_b], writes=[Rt_b])
                S.op("act", lambda e: e.mul(QSs, p1[:, 128:256], eg), reads=[p1b, egb], writes=[QSs_b])
                yield
                p2, p2b = psum()
                S.op("pe", lambda e: e.matmul(p2[:, 0:128], TT, Rt, start=True, stop=True),
                     reads=[TT_b, Rt_b], writes=[p2b])
                S.op("act", lambda e: e.mul(vnew, p2[:, 0:128], bcol), reads=[p2b, bet_b], writes=[vnew_b])
                yield
                p3, p3b = psum()
                S.op("pe", lambda e: e.matmul(p3[:, 0:128], PTt, vnew, start=True, stop=True),
                     reads=[PTt_b, vnew_b], writes=[p3b])
                S.op("pe", lambda e: e.matmul(p3[:, 128:256], Kd, vnew, start=True, stop=True),
                     reads=[Kd_b, vnew_b], writes=[p3b])
                S.op("dve", lambda e: e.scalar_tensor_tensor(
                    out=S32e, in0=S32e, scalar=egl, in1=p3[:, 128:256], op0=ALU.mult, op1=ALU.add),
                    reads=[p3b, eglb, S_bs[e_]], writes=[S_bs[e_]])
                S.op("act", lambda e: e.copy(Sbe, S32e), reads=[S_bs[e_]], writes=[S_bs[e_]])
                S.op("dve", lambda e: e.tensor_tensor(out=osb, in0=p3[:, 0:128], in1=QSs, op=ALU.add),
                     reads=[p3b, QSs_b], writes=[osb_b])
                yield
                jk, jk_b = SM[2 + e_][:, 0:128], SM_b[2 + e_]
                S.op("act", lambda e: e.activation(out=jk, in_=osb, func=AF.Square, accum_out=ss),
                     reads=[osb_b], writes=[jk_b, ssb])
                S.op("act", lambda e: e.activation(out=rs, in_=ss, func=AF.Sqrt, bias=EPS, scale=1.0 / 128),
                     reads=[ssb], writes=[rsb])
                S.op("dve", lambda e: e.reciprocal(rs, rs), reads=[rsb], writes=[rsb])
                S.op("dve", lambda e: e.scalar_tensor_tensor(
                    out=osb, in0=osb, scalar=rs, in1=dnw[:], op0=ALU.mult, op1=ALU.mult),
                    reads=[osb_b, rsb, dnw_b], writes=[osb_b])
                S.op("sp", lambda e: e.dma_start(out=OA[tt * 128:(tt + 1) * 128, hv * 128:(hv + 1) * 128], in_=osb),
                     reads=[osb_b], writes=[OA_b[tt]], dma=True)
                yield

            def interleave(gens):
                gens = list(gens)
                while gens:
                    for gen in list(gens):
                        try:
                            next(gen)
                        except StopIteration:
                            gens.remove(gen)

            interleave([shared_pre(0)])
            interleave([pre(0, 0), pre(1, 0)])
            for tt in range(NT):
                gl = [scan(0, tt), scan(1, tt)]
                if tt + 1 < NT:
                    interleave([shared_pre(tt + 1)])
                    gl += [pre(0, tt + 1), pre(1, tt + 1)]
                interleave(gl)

        def sw_head(d, li, h):
            qbt, qbt_b = d["qbt"], d["qbt_b"]
            wb, wbb = d["ws"].get()
            for tg in range(4):
                S.op("sp", lambda e, tg=tg: e.dma_start(out=SM[2][:], in_=rope_d[:, 0, tg * 512:(tg + 1) * 512]),
                     writes=[SM_b[2]], dma=True)
                S.op("sp", lambda e, tg=tg: e.dma_start(out=SM[3][:], in_=rope_d[:, 1, tg * 512:(tg + 1) * 512]),
                     writes=[SM_b[3]], dma=True)
                for part in range(4):
                    pt, ptb = psum()
                    for dc in range(16):
                        S.op("pe", lambda e, pt=pt, dc=dc, tg=tg, part=part: e.matmul(
                            pt[:, :], wb[:, dc, part * 128:(part + 1) * 128], hnT[:, dc, tg * 512:(tg + 1) * 512],
                            start=(dc == 0), stop=(dc == 15)),
                            reads=[wbb, tokA] + hnT_b[tg * 4:(tg + 1) * 4], writes=[ptb])
                    S.op("act", lambda e, pt=pt: e.copy(qbt[:], pt[:, :]), reads=[ptb], writes=[qbt_b])
                    pp, ppb = psum()
                    S.op("pe", lambda e, pp=pp: e.matmul(pp[:, :], permT_b, qbt[:], start=True, stop=True),
                         reads=[qbt_b, cstb_b], writes=[ppb])
                    S.op("dve", lambda e, pt=pt: e.tensor_tensor(out=SM[0][:], in0=pt[:, :], in1=SM[2][:],
                                                                 op=ALU.mult),
                         reads=[ptb, SM_b[2]], writes=[SM_b[0]])
                    S.op("dve", lambda e, pp=pp: e.tensor_tensor(out=SM[1][:], in0=pp[:, :], in1=SM[3][:],
                                                                 op=ALU.mult),
                         reads=[ppb, SM_b[3]], writes=[SM_b[1]])
                    S.op("pool", lambda e, part=part, tg=tg: e.tensor_tensor(
                        out=BT[part][:, tg * 512:(tg + 1) * 512], in0=SM[0][:], in1=SM[1][:], op=ALU.add),
                        reads=[SM_b[0], SM_b[1]], writes=[BT_b[part]])
            wv, wvb = d["ws"].get()

            def evv(tg, pt, ptb):
                S.op("act", lambda e: e.copy(BT[4][:, tg * 512:(tg + 1) * 512], pt[:, :]), reads=[ptb], writes=[BT_b[4]])
            proj_cm(wv, wvb, h % 4, evv)
            sq, sq_b = BT[5], BT_b[5]
            kcol, kcol_b = col[0:1, 24:28], col_b[24]
            kmx, kmx_b = col[0:1, 28:29], col_b[28]
            S.op("act", lambda e: e.activation(out=sq[:], in_=BT[3][:], func=AF.Square), reads=[BT_b[3]], writes=[sq_b])
            for tg in range(4):
                pk, pkb = psum()
                S.op("pe", lambda e, pk=pk, tg=tg: e.matmul(pk[0:1, :], ones_b[:, 0:1], sq[:, tg * 512:(tg + 1) * 512],
                                                            start=True, stop=True), reads=[sq_b, cstb_b], writes=[pkb])
                S.op("dve", lambda e, pk=pk, tg=tg: e.reduce_max(out=col[0:1, 24 + tg:25 + tg], in_=pk[0:1, :],
                                                                 axis=mybir.AxisListType.X),
                     reads=[pkb], writes=[kcol_b])
            S.op("dve", lambda e: e.reduce_max(out=kmx, in_=kcol, axis=mybir.AxisListType.X), reads=[kcol_b],
                 writes=[kmx_b])
            rowf, rowf_b = SM[2], SM_b[2]
            for tg in range(4):
                pr, prb = psum()
                for gi in range(3):
                    S.op("act", lambda e, gi=gi, tg=tg: e.activation(out=qbt[:], in_=BT[gi][:, tg * 512:(tg + 1) * 512],
                                                                      func=AF.Square), reads=[BT_b[gi]], writes=[qbt_b])
                    S.op("pe", lambda e, pr=pr, gi=gi: e.matmul(pr[0:1, :], ones_b[:, 0:1], qbt[:], start=(gi == 0),
                                                                stop=(gi == 2)), reads=[qbt_b, cstb_b], writes=[prb])
                S.op("act", lambda e, pr=pr: e.activation(out=rowf[0:1, :], in_=pr[0:1, :], func=AF.Sqrt, scale=kmx),
                     reads=[prb, kmx_b], writes=[rowf_b])
                S.op("dve", lambda e, tg=tg: e.tensor_scalar_mul(sq[0:1, tg * 512:(tg + 1) * 512], rowf[0:1, :], -1.0),
                     reads=[rowf_b, sq_b], writes=[sq_b])
            negc = sq
            VA, VA_bs, oev, oev_bs, rb, rb_bs = d["VA"], d["VA_bs"], d["oev"], d["oev_bs"], d["rb"], d["rb_bs"]
            it = 0
            for (dil, gi) in SWG:
                L = T // dil
                nb = L // 128
                Qg, Qg_b = BT[gi], BT_b[gi]
                for r in range(dil):
                    acc_ps = [None, None]
                    for m in range(nb):
                        k0 = r + dil * 128 * m
                        ksl = slice(k0, k0 + dil * 127 + 1, dil)
                        nq = 2 if m + 1 < nb else 1
                        qsl = slice(k0, k0 + dil * (128 * nq - 1) + 1, dil)
                        N = 128 * nq
                        psc, pscb = psum()
                        S.op("pe", lambda e, psc=psc, ksl=ksl, qsl=qsl, N=N, Qg=Qg: e.matmul(
                            psc[:, 0:N], BT[3][:, ksl], Qg[:, qsl], start=True, stop=False),
                            reads=[BT_b[3], Qg_b], writes=[pscb])
                        S.op("pe", lambda e, psc=psc, qsl=qsl, N=N: e.matmul(
                            psc[:, 0:N], ones_b[0:1, :], negc[0:1, qsl], start=False, stop=False),
                            reads=[sq_b, cstb_b], writes=[pscb])
                        S.op("pe", lambda e, psc=psc, N=N: e.matmul(
                            psc[:, 0:N], ident_b, cstb[:, 3:3 + N // 128, :], start=False, stop=True),
                            reads=[cstb_b], writes=[pscb])
                        PTa = d["PTa"][it % 3]
                        PTa_b = d["PTa_bs"][it % 3]
                        S.op("act", lambda e, psc=psc, N=N, PTa=PTa: e.activation(
                            out=PTa[:, 0:N], in_=psc[:, 0:N], func=AF.Exp, scale=128.0 ** -0.5),
                            reads=[pscb], writes=[PTa_b])
                        va, va_b = VA[:, it % 3, :], VA_bs[it % 3]
                        p16, p16b = psum16()
                        S.op("pe", lambda e, p16=p16, ksl=ksl: e.transpose(p16, BT[4][:, ksl], ident_b),
                             reads=[BT_b[4], cstb_b], writes=[p16b])
                        S.op("dve", lambda e, p16=p16, va=va: e.tensor_copy(va[:, 0:128], p16), reads=[p16b],
                             writes=[va_b])
                        pa, pab = ps_f[ACC0 + m % 2], ps_b[ACC0 + m % 2]
                        S.op("pe", lambda e, pa=pa, PTa=PTa, va=va, m=m: e.matmul(
                            pa[:, 0:129], PTa[:, 0:128], va[:, 0:129], start=(m == 0), stop=True),
                            reads=[PTa_b, va_b], writes=[pab])
                        if nq == 2:
                            pn_, pnb_ = ps_f[ACC0 + (m + 1) % 2], ps_b[ACC0 + (m + 1) % 2]
                        ov, ov_b = oev[:, it % 2, :], oev_bs[it % 2]
                        S.op("act", lambda e, pa=pa, ov=ov: e.copy(ov[:, 0:129], pa[:, 0:129]), reads=[pab],
                             writes=[ov_b])
                        tsl = slice(k0, k0 + dil * 127 + 1, dil)
                        obw = Buf("obw")
                        d["OBw"].append(obw)
                        S.op("sp", lambda e, ov=ov, tsl=tsl, gi=gi: e.dma_start(
                            out=OB[gi, tsl, h, 0:129], in_=ov[:, 0:129]),
                            reads=[ov_b], writes=[obw], dma=True)
                        if nq == 2:
                            S.op("pe", lambda e, pn_=pn_, PTa=PTa, va=va: e.matmul(
                                pn_[:, 0:129], PTa[:, 128:256], va[:, 0:129], start=True, stop=False),
                                reads=[PTa_b, va_b], writes=[pnb_])
                        it += 1

        def combine_tile(d, tt):
            ts = slice(tt * 128, (tt + 1) * 128)
            ych, ych_b = BT[5][:, 0:512], BT_b[5]
            for ck in range(3):
                zc, zc_b = Fv[0][:, 0:512], F_b[0]
                S.op("sp", lambda e, ck=ck: e.dma_start(out=zc, in_=ZS[ts, ck * 512:(ck + 1) * 512]),
                     reads=[ZS_b[tt]], writes=[zc_b], dma=True)
                oc, oc_b = Fv[0][:, 512:1024], d["oc_b"]
                if ck < 2:
                    S.op("sp", lambda e, ck=ck: e.dma_start(out=oc, in_=OA[ts, ck * 512:(ck + 1) * 512]),
                         reads=[OA_b[tt]], writes=[oc_b], dma=True)
                else:
                    ob, ob_b = FW[:, 4104:4104 + 3 * 516].rearrange("p (g x) -> p g x", g=3), F_b[2]
                    S.op("sp", lambda e: e.dma_start(out=ob, in_=OB[:, ts, :, :].rearrange("g t h x -> t g (h x)")),
                         reads=list(d["OBw"]), writes=[ob_b], dma=True)
                    S.op("dve", lambda e: e.tensor_tensor(out=ob[:, 0, :], in0=ob[:, 0, :], in1=ob[:, 1, :], op=ALU.add),
                         reads=[ob_b], writes=[ob_b])
                    S.op("dve", lambda e: e.tensor_tensor(out=ob[:, 0, :], in0=ob[:, 0, :], in1=ob[:, 2, :], op=ALU.add),
                         reads=[ob_b], writes=[ob_b])
                    o3 = ob[:, 0, :].rearrange("p (h x) -> p h x", h=NSW)
                    rd, rd_b = col[:, 32:32 + NSW], col_b[32]
                    S.op("dve", lambda e: e.reciprocal(rd, o3[:, :, 128]), reads=[ob_b], writes=[rd_b])
                    S.op("dve", lambda e: e.tensor_tensor(
                        out=oc.rearrange("p (h x) -> p h x", h=NSW), in0=o3[:, :, 0:128],
                        in1=rd.unsqueeze(2).broadcast_to([128, NSW, 128]), op=ALU.mult),
                        reads=[ob_b, rd_b], writes=[oc_b])
                S.op("dve", lambda e: e.tensor_tensor(out=ych, in0=oc, in1=zc, op=ALU.mult),
                     reads=[oc_b, zc_b], writes=[ych_b])
                p16, p16b = psum16(full=True)
                for j in range(4):
                    S.op("pe", lambda e, p16=p16, j=j: e.transpose(p16[:, j * 128:(j + 1) * 128],
                                                                   ych[:, j * 128:(j + 1) * 128], ident_b),
                         reads=[ych_b, cstb_b], writes=[p16b])
                S.op("act", lambda e, p16=p16, ck=ck: e.copy(
                    ytile[:, ck * 4:(ck + 1) * 4, :], p16[:, 0:512].rearrange("p (a b) -> p a b", a=4)),
                    reads=[p16b], writes=[ytile_b])

        for hh in H:
            Hb[id(hh)] = bufs("H", NT)
        Hb[id(x_d)] = bufs("x", NT)
        cur = x_d
        nxt = 0
        delta = False
        for layer in layers:
            dst = H[nxt] if delta else None
            if layer % 2 == 0:
                hyb_layer(layer // 2, layer, cur, dst, delta)
            else:
                sc_layer(layer // 2, layer, cur, dst, delta)
            if delta:
                cur = dst
                nxt ^= 1
            delta = True
        if final_norm:
            final_phase(cur, delta)
        else:
            for tt in range(NT):
                load_h(cur, None, tt, delta)
                S.op("sp", lambda e, tt=tt: e.dma_start(out=out_d[tt * 128:(tt + 1) * 128, :], in_=hbuf[:]),
                     reads=[hbuf_b], writes=[out_b[tt]], dma=True)
        S.op("sp", None, reads=out_b)
        S.emit(nc, st)
    return nc


def _consts():
    idx = np.arange(128)
    ident = np.eye(128, dtype=np.float32)
    UT = (idx[:, None] <= idx[None, :]).astype(np.float32)
    MBT = np.where(idx[None, :] >= idx[:, None], 0.0, NEG).astype(np.float32)
    SMT = (idx[None, :] > idx[:, None]).astype(np.float32)
    permT = np.zeros((128, 128), np.float32)
    for m in range(16):
        permT[m + 16, m] = 1.0
        permT[m, m + 16] = 1.0
    ones = np.ones((128, 128), np.float32)
    mcur = np.where(idx[:, None] <= idx[None, :], 0.0, NEG).astype(np.float32)
    mnext = np.where(idx[:, None] >= idx[None, :], 0.0, NEG).astype(np.float32)
    c = np.stack([ident, UT, MBT, SMT, ident, permT, ones, mcur, mnext], axis=1)
    half = 16
    inv = np.power(np.float32(500000.0), -np.arange(half, dtype=np.float32) * np.float32(2.0) / np.float32(32)).astype(np.float32)
    ang = np.arange(T, dtype=np.float32)[None, :] * inv[:, None]
    cos = np.cos(ang).astype(np.float32)
    sin = np.sin(ang).astype(np.float32)
    C = np.ones((128, T), np.float32)
    Sg = np.zeros((128, T), np.float32)
    C[0:16] = cos
    C[16:32] = cos
    Sg[0:16] = -sin
    Sg[16:32] = sin
    rope = np.stack([C, Sg], axis=1)
    return np.ascontiguousarray(c), np.ascontiguousarray(rope)


def _pcn(blk):
    nb = blk.shape[0]
    return np.ascontiguousarray(blk.reshape(nb, 16, 128, 512).transpose(0, 2, 1, 3))


def _pack_hyb(w_in, j):
    blocks = []
    for g in range(4):
        gq = 4 * j + g
        cols = np.concatenate([np.arange(gq * 128, (gq + 1) * 128), 1024 + np.arange(gq * 128, (gq + 1) * 128),
                               2048 + np.arange(2 * gq * 128, (2 * gq + 2) * 128)])
        blocks.append(w_in[:, cols])
    for h in range(4):
        hq = 4 * j + h
        cols = np.concatenate([6176 + (gi * 8 + hq) * 128 + np.arange(128) for gi in range(3)] +
                              [9248 + hq * 128 + np.arange(128)])
        blocks.append(w_in[:, cols])
    blocks.append(w_in[:, 10272 + j * 512: 10272 + (j + 1) * 512])
    for k in range(2):
        blocks.append(w_in[:, 4096 + j * 1024 + k * 512: 4096 + j * 1024 + (k + 1) * 512])
    blocks.append(w_in[:, 11296 + j * 512: 11296 + (j + 1) * 512])
    return _pcn(np.stack(blocks, axis=0))


def _pack_hcw(cw, j):
    out = np.zeros((128, 16, 4), np.float32)
    for g in range(4):
        gq = 4 * j + g
        out[:, g * 4 + 0] = cw[gq * 128:(gq + 1) * 128]
        out[:, g * 4 + 1] = cw[1024 + gq * 128: 1024 + (gq + 1) * 128]
        out[:, g * 4 + 2] = cw[2048 + 2 * gq * 128: 2048 + (2 * gq + 1) * 128]
        out[:, g * 4 + 3] = cw[2048 + (2 * gq + 1) * 128: 2048 + (2 * gq + 2) * 128]
    return out


def _pack_sc(w_in, j):
    blocks = []
    for ct in range(12):
        cg = 12 * j + ct
        cols = np.concatenate([p * 3072 + cg * 128 + np.arange(128) for p in range(4)])
        blocks.append(w_in[:, cols])
    return _pcn(np.stack(blocks, axis=0))


_NC_CACHE = {}


def make_in_maps(x, norm_w, hyb_w_in, dn_conv_w, dn_a_log, dn_dt_bias, dn_norm_w, hyb_w_out,
                 sc_w_in, sc_conv_w, sc_w_out, final_norm_w):
    f = lambda a: np.ascontiguousarray(np.asarray(a, dtype=np.float32))
    consts, rope = _consts()
    shared = {
        "normw": f(np.asarray(norm_w).reshape(4, 16, 128).transpose(2, 0, 1).reshape(128, 64)),
        "fnw": f(np.asarray(final_norm_w).reshape(1, D)),
        "consts": consts, "rope": rope,
    }
    halves = []
    for j in range(2):
        m = dict(shared)
        for i in range(2):
            w_in = np.asarray(hyb_w_in[i])
            m[f"hw{i}"] = _pack_hyb(w_in, j)
            m[f"hab{i}"] = f(np.concatenate([w_in[:, 6144 + 8 * j: 6144 + 8 * j + 8],
                                             w_in[:, 6160 + 8 * j: 6160 + 8 * j + 8]], axis=1))
            m[f"hcw{i}"] = _pack_hcw(np.asarray(dn_conv_w[i]), j)
            m[f"hal{i}"] = f(np.asarray(dn_a_log[i])[8 * j: 8 * j + 8].reshape(1, 8))
            m[f"hdt{i}"] = f(np.asarray(dn_dt_bias[i])[8 * j: 8 * j + 8].reshape(1, 8))
            m[f"hnw{i}"] = f(np.asarray(dn_norm_w[i]).reshape(1, 128))
            wo_ = np.asarray(hyb_w_out[i])
            m[f"hwo{i}"] = f(np.concatenate([wo_[1024 * j: 1024 * (j + 1)], wo_[2048 + 512 * j: 2048 + 512 * (j + 1)]], axis=0))
            m[f"sw{i}"] = _pack_sc(np.asarray(sc_w_in[i]), j)
            m[f"scw{i}"] = f(np.asarray(sc_conv_w[i])[1536 * j: 1536 * (j + 1)].reshape(12, 128, 3).transpose(1, 0, 2))
            m[f"swo{i}"] = f(np.asarray(sc_w_out[i])[1536 * j: 1536 * (j + 1)])
        halves.append(m)
    maps = []
    for c in range(8):
        m = dict(halves[c % 2])
        m["x"] = f(np.asarray(x)[c // 2])
        maps.append(m)
    return maps


def kernel(x, norm_w, hyb_w_in, dn_conv_w, dn_a_log, dn_dt_bias, dn_norm_w, hyb_w_out,
           sc_w_in, sc_conv_w, sc_w_out, final_norm_w):
    maps = make_in_maps(x, norm_w, hyb_w_in, dn_conv_w, dn_a_log, dn_dt_bias, dn_norm_w, hyb_w_out,
                        sc_w_in, sc_conv_w, sc_w_out, final_norm_w)
    if "nc" not in _NC_CACHE:
        _NC_CACHE["nc"] = build_program()
    res = run_bass_kernel_spmd(_NC_CACHE["nc"], maps, core_ids=list(range(8)))
    out = np.stack([np.asarray(res.results[2 * b]["out"], dtype=np.float32) for b in range(4)], axis=0)
    return out
```

```python
import numpy as np
from contextlib import ExitStack
import concourse.bass as bass
import concourse.mybir as mybir
from concourse.bass_utils import run_bass_kernel_spmd

F32 = mybir.dt.float32
BF16 = mybir.dt.bfloat16
AF = mybir.ActivationFunctionType
ALU = mybir.AluOpType

T = 2048
D = 2048
NT = 16
EPS = 1e-6
NEG = -30000.0

EPOCH = 12000
DMA_SLOTS = 8


class Buf:
    __slots__ = ("name", "w", "rs", "excl")

    def __init__(self, name="", excl=False):
        self.name = name
        self.w = None
        self.rs = []
        self.excl = excl


class Op:
    __slots__ = ("stream", "fn", "deps", "dma", "flagged", "fidx", "slot", "use", "n", "cc")

    def __init__(self, stream, fn, dma):
        self.stream = stream
        self.fn = fn
        self.dma = dma
        self.deps = []
        self.flagged = False
        self.fidx = 0
        self.slot = 0
        self.use = 0
        self.n = 0
        self.cc = False


class Sched:
    STREAMS = ("pe", "dve", "act", "pool", "sp")

    def __init__(self):
        self.ops = {s: [] for s in self.STREAMS}
        self.ndma = {s: 0 for s in self.STREAMS}
        self.nops = 0

    def op(self, stream, fn, reads=(), writes=(), dma=False, cc=False):
        o = Op(stream, fn, dma or cc)
        o.cc = cc
        o.n = self.nops
        self.nops += 1
        ex = [b for b in reads if b.excl]
        if ex:
            reads = [b for b in reads if not b.excl]
            writes = list(writes) + ex
        deps = {}
        for b in reads:
            if b.w is not None:
                deps[id(b.w)] = b.w
        for b in writes:
            if b.w is not None:
                deps[id(b.w)] = b.w
            for r in b.rs:
                deps[id(r)] = r
        best = {}
        for d in deps.values():
            if d is o:
                continue
            if d.dma:
                o.deps.append(d)
                continue
            if d.stream == stream and stream == "pe" and not dma:
                continue
            cur = best.get(d.stream)
            if cur is None or d.n > cur.n:
                best[d.stream] = d
        for d in best.values():
            o.deps.append(d)
            d.flagged = True
        for b in reads:
            b.rs.append(o)
        for b in writes:
            b.w = o
            b.rs = []
        if cc:
            self.ncc = getattr(self, "ncc", 0) + 1
            o.use = self.ncc
        elif dma:
            n = self.ndma[stream]
            self.ndma[stream] = n + 1
            o.slot = n % DMA_SLOTS
            o.use = n // DMA_SLOTS + 1
        self.ops[stream].append(o)
        return o

    def emit(self, nc, stack):
        nsem = {}
        for s in self.STREAMS:
            c = 0
            for o in self.ops[s]:
                if o.flagged and not o.dma:
                    c += 1
                    o.fidx = c
            nsem[s] = (max(c - 1, 0) // EPOCH) + 1
        csem = {}
        for s in self.STREAMS:
            for e in range(nsem[s]):
                csem[(s, e)] = stack.enter_context(nc.semaphore(f"c_{s}_{e}"))
        dsem = {}
        for s in self.STREAMS:
            if self.ndma[s] > 0:
                for k in range(DMA_SLOTS):
                    dsem[(s, k)] = stack.enter_context(nc.semaphore(f"d_{s}_{k}"))
        ccsem = stack.enter_context(nc.semaphore("ccsem")) if getattr(self, "ncc", 0) else None
        block = stack.enter_context(nc.Block())

        def run_stream(s, eng):
            waited = {}
            for o in self.ops[s]:
                need = {}
                for d in o.deps:
                    if d.cc:
                        key = ("cc", 0, 0)
                        val = d.use
                    elif d.dma:
                        key = ("d", d.stream, d.slot)
                        val = 16 * d.use
                    else:
                        e = (d.fidx - 1) // EPOCH
                        key = ("c", d.stream, e)
                        val = d.fidx - e * EPOCH
                    if need.get(key, 0) < val:
                        need[key] = val
                if o.cc:
                    if o.use > 1:
                        need[("cc", 0, 0)] = max(need.get(("cc", 0, 0), 0), o.use - 1)
                elif o.dma and o.use > 1:
                    key = ("d", s, o.slot)
                    val = 16 * (o.use - 1)
                    if need.get(key, 0) < val:
                        need[key] = val
                for key, val in need.items():
                    if waited.get(key, 0) >= val:
                        continue
                    waited[key] = val
                    sem = ccsem if key[0] == "cc" else (dsem[(key[1], key[2])] if key[0] == "d" else csem[(key[1], key[2])])
                    eng.wait_ge(sem, val)
                if o.fn is None:
                    continue
                ins = o.fn(eng)
                if o.cc:
                    ins.then_inc(ccsem, 1)
                elif o.dma:
                    ins.then_inc(dsem[(s, o.slot)], 16)
                elif o.flagged:
                    e = (o.fidx - 1) // EPOCH
                    ins.then_inc(csem[(s, e)], 1)

        if self.ops["pe"]:
            block.tensor(lambda eng: run_stream("pe", eng))
        if self.ops["dve"]:
            block.vector(lambda eng: run_stream("dve", eng))
        if self.ops["act"]:
            block.scalar(lambda eng: run_stream("act", eng))
        if self.ops["pool"]:
            block.gpsimd(lambda eng: run_stream("pool", eng))
        if self.ops["sp"]:
            block.sync(lambda eng: run_stream("sp", eng))


NQK = 4
NV = 8
NSW = 4
SWG = ((1, 0), (4, 1), (16, 2))


def build_program(layers=(0, 1, 2, 3), final_norm=True):
    nc = bass.Bass("TRN2", target_bir_lowering=False)

    def din(name, shape, dt=F32):
        return nc.dram_tensor(name, list(shape), dt, kind="ExternalInput").ap()

    def dscr(name, shape, dt=F32):
        return nc.dram_tensor(name, list(shape), dt).ap()

    x_d = din("x", [T, D])
    normw_d = din("normw", [128, 64])
    fnw_d = din("fnw", [1, D])
    consts_d = din("consts", [128, 9, 128])
    rope_d = din("rope", [128, 2, T])
    hyb_w = [din(f"hw{i}", [12, 128, 16, 512]) for i in range(2)]
    hyb_ab = [din(f"hab{i}", [D, 16]) for i in range(2)]
    hyb_cw = [din(f"hcw{i}", [128, 16, 4]) for i in range(2)]
    hyb_al = [din(f"hal{i}", [1, 8]) for i in range(2)]
    hyb_dt = [din(f"hdt{i}", [1, 8]) for i in range(2)]
    hyb_nw = [din(f"hnw{i}", [1, 128]) for i in range(2)]
    hyb_wo = [din(f"hwo{i}", [1536, D]) for i in range(2)]
    sc_w = [din(f"sw{i}", [12, 128, 16, 512]) for i in range(2)]
    sc_cw = [din(f"scw{i}", [128, 12, 3]) for i in range(2)]
    sc_wo = [din(f"swo{i}", [1536, D]) for i in range(2)]
    out_d = nc.dram_tensor("out", [T, D], F32, kind="ExternalOutput").ap()

    H = [dscr("Hs0", [T, D]), dscr("Hs1", [T, D])]
    YT = dscr("YT", [1536, T], BF16)
    ZS = dscr("ZS", [T, 1536])
    OA = dscr("OA", [T, 1024])
    Pp = dscr("Pp", [T, D])
    Ps = dscr("Ps", [T, D])
    OB = dscr("OB", [3, T, NSW, 129])

    S = Sched()
    with ExitStack() as st:
        def sb(name, shape, dt=F32):
            return st.enter_context(nc.sbuf_tensor(name, list(shape), dt))

        def bufs(name, n):
            return [Buf(f"{name}{i}") for i in range(n)]

        ps_f = [st.enter_context(nc.psum_tensor(f"ps{i}", [128, 512], F32)) for i in range(6)]
        ps_b = [Buf(f"ps{i}", excl=True) for i in range(6)]
        ps16 = [st.enter_context(nc.psum_tensor(f"ps16_{i}", [128, 1024], BF16)) for i in range(2)]
        ps16_b = [Buf(f"ps16_{i}", excl=True) for i in range(2)]
        ps_ctr = [0, 0]
        ACC0 = 4

        ps_nrot = [6]

        def psum():
            i = ps_ctr[0] % ps_nrot[0]
            ps_ctr[0] += 1
            return ps_f[i], ps_b[i]

        def psum16(full=False):
            i = ps_ctr[1] % 2
            ps_ctr[1] += 1
            if full:
                return ps16[i], ps16_b[i]
            return ps16[i][:, 0:128], ps16_b[i]

        cst = sb("cst", [128, 4, 128])
        cst_b = Buf("cst")
        S.op("sp", lambda e: e.dma_start(out=cst[:], in_=consts_d[:, 0:4, :]), writes=[cst_b], dma=True)
        cstb = sb("cstb", [128, 5, 128], BF16)
        cstb_b = Buf("cstb")
        S.op("pool", lambda e: e.dma_start(out=cstb[:], in_=consts_d[:, 4:9, :]), writes=[cstb_b], dma=True)
        ident_f, UT, MBT, SMT = cst[:, 0, :], cst[:, 1, :], cst[:, 2, :], cst[:, 3, :]
        ident_b, permT_b, ones_b = cstb[:, 0, :], cstb[:, 1, :], cstb[:, 2, :]
        normw = sb("normw_s", [128, 64])
        normw_b = Buf("normw")
        S.op("sp", lambda e: e.dma_start(out=normw[:], in_=normw_d[:, :]), writes=[normw_b], dma=True)

        arena = sb("arena", [128, 49152], BF16)
        hnT = arena[:, 0:32768].rearrange("p (c t) -> p c t", c=16)
        wblk = [arena[:, 32768 + i * 8192: 32768 + (i + 1) * 8192].rearrange("p (c n) -> p c n", c=16)
                for i in range(2)]
        wo = arena[:, 0:24576].rearrange("p (c n) -> p c n", c=12)
        tokA = Buf("tokA")
        hnT_b = bufs("hnT", NT)
        wblk_b = bufs("wblk", 2)
        wo_b = bufs("wo", 12)
        fence = sb("fence", [128, 4])
        wctr = [0]

        def load_wblk(src_ap, ncols=512):
            i = wctr[0] % 2
            wctr[0] += 1
            S.op("pool", lambda e: e.dma_start(out=wblk[i][:, :, 0:ncols], in_=src_ap),
                 reads=[tokA], writes=[wblk_b[i]], dma=True)
            return wblk[i], wblk_b[i]

        class WStream:
            def __init__(self, srcs):
                self.srcs, self.pos, self.q = list(srcs), 0, []

            def _pf(self):
                if self.pos < len(self.srcs):
                    self.q.append(load_wblk(self.srcs[self.pos]))
                    self.pos += 1

            def get(self):
                if not self.q:
                    self._pf()
                r = self.q.pop(0)
                self._pf()
                return r

        FW = sb("FW", [128, 4 * 2052])
        Fv = [FW[:, i * 2052:(i + 1) * 2052] for i in range(4)]
        F_b = bufs("F", 4)
        BT = [sb(f"BT{i}", [128, T], BF16) for i in range(6)]
        BT_b = bufs("BT", 6)
        hbuf = sb("hbuf", [128, D])
        hbuf_b = Buf("hbuf")
        SM = [sb(f"SM{i}", [128, 512]) for i in range(4)]
        SM_b = bufs("SM", 4)
        col = sb("colstat", [128, 64])
        col_b = bufs("col", 64)
        ytile2 = sb("ytile", [128, 2, 12, 128], BF16)
        ytiles = [ytile2[:, 0, :, :], ytile2[:, 1, :, :]]
        ytiles_b = bufs("ytile", 2)
        Hb = {}
        YT_b = bufs("YT", 12)
        Pp_b = bufs("Pp", NT)
        Ps_b = bufs("Ps", 4)
        ZS_b = bufs("ZS", NT)
        OA_b = bufs("OA", NT)
        OB_b = bufs("OB", NT)
        out_b = bufs("out", NT)
        hs, hs_b = Fv[3][:, 0:D], F_b[3]
        junk, junk_b = BT[5], BT_b[5]

        def load_h(h_src, h_dst, tt, delta):
            S.op("sp", lambda e: e.dma_start(out=hbuf[:], in_=h_src[tt * 128:(tt + 1) * 128, :]),
                 reads=[Hb[id(h_src)][tt]], writes=[hbuf_b], dma=True)
            if delta:
                S.op("sp", lambda e: e.dma_start(out=hs, in_=Ps[tt * 128:(tt + 1) * 128, :]),
                     reads=[Ps_b[tt // 4]], writes=[hs_b], dma=True)
                S.op("dve", lambda e: e.tensor_tensor(out=hbuf[:], in0=hbuf[:], in1=hs, op=ALU.add),
                     reads=[hbuf_b, hs_b], writes=[hbuf_b])
                if h_dst is not None:
                    S.op("sp", lambda e: e.dma_start(out=h_dst[tt * 128:(tt + 1) * 128, :], in_=hbuf[:]),
                         reads=[hbuf_b], writes=[Hb[id(h_dst)][tt]], dma=True)

        def norm_phase(h_src, h_dst, delta, layer):
            S.op("dve", lambda e: e.memset(fence[:, 0:1], 0.0), writes=[tokA])
            for tt in range(NT):
                load_h(h_src, h_dst, tt, delta)
                c0, c0b = col[:, 0:1], col_b[0]
                c1, c1b = col[:, 1:2], col_b[1]
                S.op("act", lambda e: e.activation(out=junk[:], in_=hbuf[:], func=AF.Square, accum_out=c0),
                     reads=[hbuf_b], writes=[junk_b, c0b])
                S.op("act", lambda e: e.activation(out=c1, in_=c0, func=AF.Sqrt, bias=EPS, scale=1.0 / D),
                     reads=[c0b], writes=[c1b])
                S.op("dve", lambda e: e.reciprocal(c1, c1), reads=[c1b], writes=[c1b])
                S.op("act", lambda e: e.mul(hs, hbuf[:], c1), reads=[hbuf_b, c1b], writes=[hs_b])
                for q in range(4):
                    pt, ptb = psum()
                    for j in range(4):
                        dc = q * 4 + j
                        S.op("pe", lambda e, pt=pt, j=j, dc=dc: e.transpose(
                            pt[:, j * 128:(j + 1) * 128], hs[:, dc * 128:(dc + 1) * 128], ident_f),
                            reads=[hs_b, cst_b], writes=[ptb])
                    nwv = normw[:, layer * 16 + q * 4: layer * 16 + q * 4 + 4].unsqueeze(2).broadcast_to([128, 4, 128])
                    S.op("dve", lambda e, pt=pt, q=q, tt=tt, nwv=nwv: e.tensor_tensor(
                        out=hnT[:, q * 4:(q + 1) * 4, tt * 128:(tt + 1) * 128],
                        in0=pt[:, :].rearrange("p (a b) -> p a b", a=4), in1=nwv, op=ALU.mult),
                        reads=[ptb, normw_b, tokA], writes=[hnT_b[tt]])

        def load_wo(wo_d):
            for c in range(12):
                S.op("pool", lambda e, c=c: e.dma_start(out=wo[:, c, :], in_=wo_d[c * 128:(c + 1) * 128, :]),
                     writes=[tokA, wo_b[c]] if c == 0 else [wo_b[c]], reads=[] if c == 0 else [tokA], dma=True)

        RG = [[0, 1], [2, 3], [4, 5], [6, 7]]

        def outproj_tile(tt):
            ytile, ytile_b = ytiles[tt % 2], ytiles_b[tt % 2]
            if tt % 2 == 0:
                po, po_b = Fv[1][:, 0:D], F_b[1]
            else:
                po, po_b = hbuf[:, :], hbuf_b
            for cb in range(4):
                pt, ptb = psum()
                for c in range(12):
                    S.op("pe", lambda e, pt=pt, c=c, cb=cb: e.matmul(
                        pt[:, :], ytile[:, c, :], wo[:, c, cb * 512:(cb + 1) * 512], start=(c == 0), stop=(c == 11)),
                        reads=[ytile_b, wo_b[c], tokA], writes=[ptb])
                if cb % 2 == 0:
                    S.op("act", lambda e, pt=pt, cb=cb: e.copy(po[:, cb * 512:(cb + 1) * 512], pt[:, :]),
                         reads=[ptb], writes=[po_b])
                else:
                    S.op("dve", lambda e, pt=pt, cb=cb: e.tensor_copy(po[:, cb * 512:(cb + 1) * 512], pt[:, :]),
                         reads=[ptb], writes=[po_b])
            S.op("sp", lambda e: e.dma_start(out=Pp[tt * 128:(tt + 1) * 128, :], in_=po),
                 reads=[po_b], writes=[Pp_b[tt]], dma=True)
            if tt % 4 == 3:
                q = tt // 4
                S.op("pool", lambda e: e.collective_compute(
                    "AllReduce", ALU.add, replica_groups=RG,
                    ins=[Pp[q * 512:(q + 1) * 512, :]], outs=[Ps[q * 512:(q + 1) * 512, :]]),
                    reads=Pp_b[q * 4:(q + 1) * 4], writes=[Ps_b[q]], cc=True)

        cu, cu_b = Fv[0][:, 0:2 + T], F_b[0]
        gate, gate_b = Fv[1][:, 0:T], F_b[1]
        acc, acc_b = Fv[2][:, 0:T], F_b[2]
        sccw = sb("sccw", [128, 12, 3])
        sccw_b = Buf("sccw")

        def sc_layer(li, layer, h_src, h_dst, delta):
            norm_phase(h_src, h_dst, delta, layer)
            S.op("sp", lambda e: e.dma_start(out=sccw[:], in_=sc_cw[li][:, :, :]), writes=[sccw_b], dma=True)
            ws = WStream([sc_w[li][ct] for ct in range(12)])
            for ct in range(12):
                wb, wbb = ws.get()
                S.op("dve", lambda e: e.memset(cu[:, 0:2], 0.0), writes=[cu_b])
                for tg in range(4):
                    pp = []
                    for part in range(4):
                        pt, ptb = psum()
                        for dc in range(16):
                            S.op("pe", lambda e, pt=pt, dc=dc, part=part, tg=tg, wb=wb: e.matmul(
                                pt[:, :], wb[:, dc, part * 128:(part + 1) * 128], hnT[:, dc, tg * 512:(tg + 1) * 512],
                                start=(dc == 0), stop=(dc == 15)),
                                reads=[wbb, tokA] + hnT_b[tg * 4:(tg + 1) * 4], writes=[ptb])
                        pp.append((pt, ptb))
                    (pb_, pbb), (pc_, pcb), (pu_, pub), (pz_, pzb) = pp
                    ua, uab = SM[0], SM_b[0]
                    za, zab = SM[1], SM_b[1]
                    S.op("act", lambda e, pu_=pu_: e.copy(ua[:], pu_[:, :]), reads=[pub], writes=[uab])
                    S.op("dve", lambda e, pc_=pc_, tg=tg: e.tensor_tensor(
                        out=cu[:, 2 + tg * 512: 2 + (tg + 1) * 512], in0=pc_[:, :], in1=ua[:], op=ALU.mult),
                        reads=[pcb, uab], writes=[cu_b])
                    S.op("act", lambda e, pz_=pz_: e.activation(out=za[:], in_=pz_[:, :], func=AF.Silu),
                         reads=[pzb], writes=[zab])
                    S.op("dve", lambda e, pb_=pb_, tg=tg: e.tensor_tensor(
                        out=gate[:, tg * 512:(tg + 1) * 512], in0=pb_[:, :], in1=za[:], op=ALU.mult),
                        reads=[pbb, zab], writes=[gate_b])
                S.op("act", lambda e, ct=ct: e.mul(acc, cu[:, 2:2 + T], sccw[:, ct, 2:3]),
                     reads=[cu_b, sccw_b], writes=[acc_b])
                for i in (1, 0):
                    S.op("dve", lambda e, ct=ct, i=i: e.scalar_tensor_tensor(
                        out=acc, in0=cu[:, i:i + T], scalar=sccw[:, ct, i:i + 1], in1=acc,
                        op0=ALU.mult, op1=ALU.add), reads=[cu_b, sccw_b, acc_b], writes=[acc_b])
                yb, ybb = BT[ct % 2], BT_b[ct % 2]
                S.op("dve", lambda e, yb=yb: e.tensor_tensor(out=yb[:], in0=acc, in1=gate, op=ALU.mult),
                     reads=[acc_b, gate_b], writes=[ybb])
                S.op("sp", lambda e, yb=yb, ct=ct: e.dma_start(out=YT[ct * 128:(ct + 1) * 128, :], in_=yb[:]),
                     reads=[ybb], writes=[YT_b[ct]], dma=True)
            load_wo(sc_wo[li])
            for tt in range(NT):
                S.op("sp", lambda e, tt=tt: e.dma_start(
                    out=ytiles[tt % 2], in_=YT[:, tt * 128:(tt + 1) * 128].rearrange("(c p) t -> p c t", p=128)),
                    reads=YT_b, writes=[ytiles_b[tt % 2]], dma=True)
                outproj_tile(tt)

        def final_phase(h_src, delta):
            fw, fw_b = Fv[0][:, 0:D], F_b[0]
            S.op("sp", lambda e: e.dma_start(out=fw, in_=fnw_d[0:1, :].broadcast_to([128, D])),
                 writes=[fw_b], dma=True)
            for tt in range(NT):
                load_h(h_src, None, tt, delta)
                c0, c0b = col[:, 0:1], col_b[0]
                c1, c1b = col[:, 1:2], col_b[1]
                S.op("act", lambda e: e.activation(out=junk[:], in_=hbuf[:], func=AF.Square, accum_out=c0),
                     reads=[hbuf_b], writes=[junk_b, c0b])
                S.op("act", lambda e: e.activation(out=c1, in_=c0, func=AF.Sqrt, bias=EPS, scale=1.0 / D),
                     reads=[c0b], writes=[c1b])
                S.op("dve", lambda e: e.reciprocal(c1, c1), reads=[c1b], writes=[c1b])
                S.op("dve", lambda e: e.scalar_tensor_tensor(
                    out=hs, in0=hbuf[:], scalar=c1, in1=fw, op0=ALU.mult, op1=ALU.mult),
                    reads=[hbuf_b, c1b, fw_b], writes=[hs_b])
                S.op("sp", lambda e, tt=tt: e.dma_start(out=out_d[tt * 128:(tt + 1) * 128, :], in_=hs),
                     reads=[hs_b], writes=[out_b[tt]], dma=True)

        HYB_TILES = {}

        def hyb_tiles():
            if HYB_TILES:
                return HYB_TILES
            d = HYB_TILES
            d["cw"] = sb("hcw_s", [128, 16, 4])
            d["wab"] = sb("wab_s", [128, 16, 16], BF16)
            d["bet"] = sb("bet", [128, 16, 8])
            d["gr"] = sb("gr", [128, 16, 8])
            d["bc16"] = sb("bc16", [128, 3, 8])
            d["dnw"] = sb("dnw", [128, 128])
            d["dn"] = sb("dnwork", [128, 12, 256])
            d["rb"] = sb("rbwork", [128, 22, 128], BF16)
            d["S32"] = sb("S32", [128, 2, 128])
            d["Sb"] = sb("Sbf", [128, 2, 128], BF16)
            d["qbt"] = sb("qbt", [128, 512], BF16)
            d["VA"] = sb("VA", [128, 3, 132], BF16)
            d["oev"] = sb("oev", [128, 2, 132])
            for k in list(d.keys()):
                d[k + "_b"] = Buf(k)
            d["dn_bs"] = bufs("dnw", 12)
            d["nn_bs"] = bufs("nn", 4)
            d["qo_bs"] = bufs("qo", 4)
            d["rb_bs"] = bufs("rb", 22)
            d["PTa"] = [sb(f"PTa{i}", [128, 256], BF16) for i in range(3)]
            d["PTa_bs"] = bufs("PTa", 3)
            d["OBw"] = []
            d["oc_b"] = Buf("oc")
            d["cmb_bs"] = bufs("cmb", 10)
            d["VA_bs"] = bufs("VA", 3)
            d["oev_bs"] = bufs("oev", 2)
            d["S_bs"] = bufs("S", 2)
            return d

        def hyb_layer(li, layer, h_src, h_dst, delta):
            d = hyb_tiles()
            norm_phase(h_src, h_dst, delta, layer)
            cw, cw_b = d["cw"], d["cw_b"]
            S.op("sp", lambda e: e.dma_start(out=cw[:], in_=hyb_cw[li][:, :, :]), writes=[cw_b], dma=True)
            wab, wab_b = d["wab"], d["wab_b"]
            S.op("pool", lambda e: e.dma_start(out=wab[:], in_=hyb_ab[li].rearrange("(c p) n -> p c n", p=128)),
                 writes=[wab_b], dma=True)
            bc16, bc16_b = d["bc16"], d["bc16_b"]
            S.op("sp", lambda e: e.dma_start(out=bc16[:, 0, :], in_=hyb_dt[li][0:1, :].broadcast_to([128, 8])),
                 writes=[bc16_b], dma=True)
            S.op("sp", lambda e: e.dma_start(out=bc16[:, 1, :], in_=hyb_al[li][0:1, :].broadcast_to([128, 8])),
                 reads=[bc16_b], writes=[bc16_b], dma=True)
            dnw, dnw_b = d["dnw"], d["dnw_b"]
            S.op("sp", lambda e: e.dma_start(out=dnw[:], in_=hyb_nw[li][0:1, :].broadcast_to([128, 128])),
                 writes=[dnw_b], dma=True)
            S.op("act", lambda e: e.activation(out=bc16[:, 1, :], in_=bc16[:, 1, :], func=AF.Exp),
                 reads=[bc16_b], writes=[bc16_b])
            S.op("dve", lambda e: e.tensor_scalar_mul(bc16[:, 1, :], bc16[:, 1, :], -1.0),
                 reads=[bc16_b], writes=[bc16_b])
            bet, bet_b, gr, gr_b = d["bet"], d["bet_b"], d["gr"], d["gr_b"]
            t1, t1b = SM[2], SM_b[2]
            t2, t2b = SM[3], SM_b[3]
            for tt in range(NT):
                pt, ptb = psum()
                for dc in range(16):
                    S.op("pe", lambda e, pt=pt, dc=dc, tt=tt: e.matmul(
                        pt[:, 0:16], hnT[:, dc, tt * 128:(tt + 1) * 128], wab[:, dc, :],
                        start=(dc == 0), stop=(dc == 15)), reads=[wab_b, hnT_b[tt], tokA], writes=[ptb])
                S.op("act", lambda e, pt=pt, tt=tt: e.activation(out=bet[:, tt, :], in_=pt[:, 0:8], func=AF.Sigmoid),
                     reads=[ptb], writes=[bet_b])
                S.op("dve", lambda e, pt=pt: e.tensor_tensor(out=t1[:, 0:8], in0=pt[:, 8:16], in1=bc16[:, 0, :],
                                                             op=ALU.add), reads=[ptb, bc16_b], writes=[t1b])
                S.op("act", lambda e: e.activation(out=t2[:, 0:8], in_=t1[:, 0:8], func=AF.Abs),
                     reads=[t1b], writes=[t2b])
                S.op("act", lambda e: e.activation(out=t2[:, 0:8], in_=t2[:, 0:8], func=AF.Exp, scale=-1.0),
                     reads=[t2b], writes=[t2b])
                S.op("act", lambda e: e.activation(out=t2[:, 0:8], in_=t2[:, 0:8], func=AF.Ln, bias=1.0),
                     reads=[t2b], writes=[t2b])
                S.op("dve", lambda e: e.scalar_tensor_tensor(out=t1[:, 0:8], in0=t1[:, 0:8], scalar=0.0,
                                                             in1=t2[:, 0:8], op0=ALU.max, op1=ALU.add),
                     reads=[t1b, t2b], writes=[t1b])
                S.op("dve", lambda e, tt=tt: e.tensor_tensor(out=gr[:, tt, :], in0=t1[:, 0:8], in1=bc16[:, 1, :],
                                                             op=ALU.mult), reads=[t1b, bc16_b], writes=[gr_b])
            seq = [9, 10, 11, 0, 1, 2, 3]
            for h in range(NSW):
                seq += [4 + h, 8]
            d["ws"] = WStream([hyb_w[li][k] for k in seq])
            for zb in range(3):
                wb, wbb = d["ws"].get()
                for tt in range(NT):
                    pt, ptb = psum()
                    for dc in range(16):
                        S.op("pe", lambda e, pt=pt, dc=dc, tt=tt, wb=wb: e.matmul(
                            pt[:, :], hnT[:, dc, tt * 128:(tt + 1) * 128], wb[:, dc, :],
                            start=(dc == 0), stop=(dc == 15)), reads=[wbb, hnT_b[tt], tokA], writes=[ptb])
                    zt, ztb = SM[tt % 2], SM_b[tt % 2]
                    S.op("act", lambda e, pt=pt, zt=zt: e.activation(out=zt[:], in_=pt[:, :], func=AF.Silu),
                         reads=[ptb], writes=[ztb])
                    S.op("sp", lambda e, zt=zt, tt=tt, zb=zb: e.dma_start(
                        out=ZS[tt * 128:(tt + 1) * 128, zb * 512:(zb + 1) * 512], in_=zt[:]),
                        reads=[ztb], writes=[ZS_b[tt]], dma=True)
            for g in range(NQK):
                dn_head(d, li, g)
            S.op("dve", lambda e: e.memset(d["VA"][:, :, 128:132], 1.0), writes=d["VA_bs"])
            ps_nrot[0] = 4
            for h in range(NSW):
                sw_head(d, li, h)
            ps_nrot[0] = 6
            load_wo(hyb_wo[li])
            S.op("dve", lambda e: e.memset(fence[:, 1:2], 0.0), writes=[F_b[0], F_b[2], F_b[3], BT_b[5]])
            for tt in range(NT):
                combine_tile(d, tt)
                if tt == NT - 1:
                    d["OBw"] = []
                outproj_tile(tt)

        def proj_cm(wb, wbb, part, evac):
            for tg in range(4):
                pt, ptb = psum()
                for dc in range(16):
                    S.op("pe", lambda e, pt=pt, dc=dc, tg=tg: e.matmul(
                        pt[:, :], wb[:, dc, part * 128:(part + 1) * 128], hnT[:, dc, tg * 512:(tg + 1) * 512],
                        start=(dc == 0), stop=(dc == 15)),
                        reads=[wbb, tokA] + hnT_b[tg * 4:(tg + 1) * 4], writes=[ptb])
                evac(tg, pt, ptb)

        def dn_head(d, li, g):
            cw, cw_b = d["cw"], d["cw_b"]
            wb, wbb = d["ws"].get()
            qs, qs_b = Fv[3][:, 0:T], F_b[3]
            accv, accv_b = Fv[2][:, 0:T], F_b[2]
            sq, sq_b = BT[5], BT_b[5]
            for part in range(4):
                raw, raw_b = Fv[part % 2], F_b[part % 2]
                S.op("dve", lambda e, raw=raw: e.memset(raw[:, 0:3], 0.0), writes=[raw_b])

                def ev(tg, pt, ptb, raw=raw, raw_b=raw_b):
                    S.op("act", lambda e: e.copy(raw[:, 3 + tg * 512: 3 + (tg + 1) * 512], pt[:, :]),
                         reads=[ptb], writes=[raw_b])
                proj_cm(wb, wbb, part, ev)
                tl = g * 4 + part
                S.op("act", lambda e, raw=raw, tl=tl: e.mul(accv, raw[:, 3:3 + T], cw[:, tl, 3:4]),
                     reads=[raw_b, cw_b], writes=[accv_b])
                for i in (2, 1, 0):
                    S.op("dve", lambda e, raw=raw, tl=tl, i=i: e.scalar_tensor_tensor(
                        out=accv, in0=raw[:, i:i + T], scalar=cw[:, tl, i:i + 1], in1=accv,
                        op0=ALU.mult, op1=ALU.add), reads=[raw_b, cw_b, accv_b], writes=[accv_b])
                if part < 2:
                    S.op("act", lambda e: e.activation(out=qs, in_=accv, func=AF.Silu), reads=[accv_b], writes=[qs_b])
                    S.op("act", lambda e: e.activation(out=sq[:], in_=qs, func=AF.Square), reads=[qs_b], writes=[sq_b])
                    for tg in range(4):
                        pt, ptb = psum()
                        S.op("pe", lambda e, pt=pt, tg=tg: e.matmul(pt[:, :], ones_b, sq[:, tg * 512:(tg + 1) * 512],
                                                                    start=True, stop=True),
                             reads=[sq_b, cstb_b], writes=[ptb])
                        rn, rnb = SM[tg % 2], SM_b[tg % 2]
                        S.op("act", lambda e, pt=pt, rn=rn: e.activation(out=rn[:], in_=pt[:, :], func=AF.Sqrt,
                                                                         bias=EPS, scale=1.0),
                             reads=[ptb], writes=[rnb])
                        S.op("dve", lambda e, rn=rn: e.reciprocal(rn[:], rn[:]), reads=[rnb], writes=[rnb])
                        sc_ = (128.0 ** -0.5) if part == 0 else 1.0
                        S.op("dve", lambda e, rn=rn, tg=tg, part=part, sc_=sc_: e.scalar_tensor_tensor(
                            out=BT[part][:, tg * 512:(tg + 1) * 512], in0=qs[:, tg * 512:(tg + 1) * 512], scalar=sc_,
                            in1=rn[:], op0=ALU.mult, op1=ALU.mult), reads=[qs_b, rnb], writes=[BT_b[part]])
                else:
                    S.op("act", lambda e, part=part: e.activation(out=BT[part][:], in_=accv, func=AF.Silu),
                         reads=[accv_b], writes=[BT_b[part]])
            qn, kn = BT[0], BT[1]
            qkb = [BT_b[0], BT_b[1]]
            dn, dn_bs, rb, rb_bs = d["dn"], d["dn_bs"], d["rb"], d["rb_bs"]
            S32, Sb, S_bs = d["S32"], d["Sb"], d["S_bs"]
            bet, bet_b, gr, gr_b = d["bet"], d["bet_b"], d["gr"], d["gr_b"]
            dnw, dnw_b = d["dnw"], d["dnw_b"]
            for e_ in range(2):
                S.op("dve", lambda e, e_=e_: e.memset(S32[:, e_, :], 0.0), writes=[S_bs[e_]])
                S.op("dve", lambda e, e_=e_: e.memset(Sb[:, e_, :], 0.0), reads=[S_bs[e_]], writes=[S_bs[e_]])
            def shared_pre(tt):
                par = tt % 2
                ts = slice(tt * 128, (tt + 1) * 128)
                ktok, ktok_b = rb[:, par, :], rb_bs[par]
                p16, p16b = psum16()
                S.op("pe", lambda e: e.transpose(p16, kn[:, ts], ident_b), reads=[qkb[1], cstb_b], writes=[p16b])
                S.op("act", lambda e: e.copy(ktok, p16), reads=[p16b], writes=[ktok_b])
                kkqk, kkqk_b = dn[:, 10 + par, :], dn_bs[10 + par]
                pt, ptb = psum()
                S.op("pe", lambda e: e.matmul(pt[:, 0:128], kn[:, ts], kn[:, ts], start=True, stop=True),
                     reads=[qkb[1]], writes=[ptb])
                S.op("pe", lambda e: e.matmul(pt[:, 128:256], kn[:, ts], qn[:, ts], start=True, stop=True),
                     reads=qkb, writes=[ptb])
                S.op("act", lambda e: e.copy(kkqk, pt[:, 0:256]), reads=[ptb], writes=[kkqk_b])
                yield

            def tiles(e_, tt):
                par = tt % 2
                hv = 2 * g + e_
                t = {}
                t["ts"] = slice(tt * 128, (tt + 1) * 128)
                t["ktok"], t["ktok_b"] = rb[:, par, :], rb_bs[par]
                t["kkqk"], t["kkqk_b"] = dn[:, 10 + par, :], dn_bs[10 + par]
                i = 2 + e_ * 2 + par
                t["vtok"], t["vtok_b"] = rb[:, i, :], rb_bs[i]
                for j, nm in enumerate(("PTt", "Kd", "TT")):
                    i = 6 + (e_ * 2 + par) * 3 + j
                    t[nm], t[nm + "_b"] = rb[:, i, :], rb_bs[i]
                for j, nm in enumerate(("Rt", "vnew")):
                    i = 18 + e_ * 2 + j
                    t[nm], t[nm + "_b"] = rb[:, i, :], rb_bs[i]
                c0 = 40 + (e_ * 2 + par) * 4
                for j, nm in enumerate(("ngc", "eg", "neg", "egl")):
                    t[nm], t[nm + "_b"] = col[:, c0 + j:c0 + j + 1], col_b[c0 + j]
                c0 = 56 + e_ * 2
                for j, nm in enumerate(("ss", "rs")):
                    t[nm], t[nm + "_b"] = col[:, c0 + j:c0 + j + 1], col_b[c0 + j]
                t["gcol"] = gr[:, tt, hv:hv + 1]
                t["bcol"] = bet[:, tt, hv:hv + 1]
                w0 = e_ * 5
                t["DT"], t["DTs"], t["DT_b"] = dn[:, w0, 0:128], dn[:, w0, 128:256], dn_bs[w0]
                t["AB"] = [dn[:, w0 + 1, :], dn[:, w0 + 2, :]]
                t["AB_b"] = [dn_bs[w0 + 1], dn_bs[w0 + 2]]
                t["Nn"] = [dn[:, w0 + 3, 0:128], dn[:, w0 + 3, 128:256]]
                t["Nn_b"] = [d["nn_bs"][e_ * 2], d["nn_bs"][e_ * 2 + 1]]
                t["QSs"], t["QSs_b"] = dn[:, w0 + 4, 0:128], d["qo_bs"][e_ * 2]
                t["osb"], t["osb_b"] = dn[:, w0 + 4, 128:256], d["qo_bs"][e_ * 2 + 1]
                t["hv"] = hv
                return t

            def pre(e_, tt):
                t = tiles(e_, tt)
                ts, hv = t["ts"], t["hv"]
                vT, vT_b = BT[2 + e_], BT_b[2 + e_]
                vtok, vtok_b = t["vtok"], t["vtok_b"]
                gcol, bcol = t["gcol"], t["bcol"]
                ngc, eg, neg, egl = t["ngc"], t["eg"], t["neg"], t["egl"]
                ngcb, egb, negb, eglb = t["ngc_b"], t["eg_b"], t["neg_b"], t["egl_b"]
                DT, DTs, DT_b, AB, AB_b, Nn, Nn_b = t["DT"], t["DTs"], t["DT_b"], t["AB"], t["AB_b"], t["Nn"], t["Nn_b"]
                kkqk, kkqk_b, ktok, ktok_b = t["kkqk"], t["kkqk_b"], t["ktok"], t["ktok_b"]
                PTt, PTt_b, Kd, Kd_b, TT, TT_b = t["PTt"], t["PTt_b"], t["Kd"], t["Kd_b"], t["TT"], t["TT_b"]
                tmp, tmp_b = AB[1][:, 0:128], AB_b[1]
                p16, p16b = psum16()
                S.op("pe", lambda e: e.transpose(p16, vT[:, ts], ident_b), reads=[vT_b, cstb_b], writes=[p16b])
                S.op("act", lambda e: e.copy(vtok, p16), reads=[p16b], writes=[vtok_b])
                pg, pgb = psum()
                S.op("pe", lambda e: e.matmul(pg[:, 0:128], gcol.broadcast_to([128, 128]), UT, start=True, stop=True),
                     reads=[gr_b, cst_b], writes=[pgb])
                S.op("pe", lambda e: e.matmul(pg[:, 128:129], UT, gcol, start=True, stop=True),
                     reads=[gr_b, cst_b], writes=[pgb])
                S.op("act", lambda e: e.mul(ngc, pg[:, 128:129], -1.0), reads=[pgb], writes=[ngcb])
                S.op("act", lambda e: e.activation(out=eg, in_=pg[:, 128:129], func=AF.Exp), reads=[pgb], writes=[egb])
                S.op("act", lambda e: e.activation(out=egl, in_=pg[:, 127:128], func=AF.Exp), reads=[pgb], writes=[eglb])
                S.op("dve", lambda e: e.tensor_tensor(out=tmp, in0=pg[:, 0:128], in1=MBT, op=ALU.add),
                     reads=[pgb, cst_b], writes=[tmp_b])
                S.op("dve", lambda e: e.tensor_scalar_mul(neg, eg, -1.0), reads=[egb], writes=[negb])
                yield
                S.op("act", lambda e: e.activation(out=DT, in_=tmp, func=AF.Exp, bias=ngc, scale=1.0),
                     reads=[tmp_b, ngcb], writes=[DT_b])
                S.op("pool", lambda e: e.tensor_tensor(out=DTs, in0=DT, in1=SMT, op=ALU.mult),
                     reads=[DT_b, cst_b], writes=[DT_b])
                S.op("dve", lambda e: e.scalar_tensor_tensor(
                    out=AB[0][:, 128:256], in0=kkqk[:, 0:128], scalar=bcol, in1=DTs, op0=ALU.mult, op1=ALU.mult),
                    reads=[kkqk_b, bet_b, DT_b], writes=[AB_b[0]])
                S.op("pool", lambda e: e.tensor_tensor(out=PTt, in0=kkqk[:, 128:256], in1=DT, op=ALU.mult),
                     reads=[kkqk_b, DT_b], writes=[PTt_b])
                S.op("act", lambda e: e.mul(Kd, ktok, DT[:, 127:128]), reads=[ktok_b, DT_b], writes=[Kd_b])
                yield
                pa0, pa0b = psum()
                S.op("pe", lambda e: e.transpose(pa0[:, 0:128], AB[0][:, 128:256], ident_f),
                     reads=[AB_b[0], cst_b], writes=[pa0b])
                S.op("act", lambda e: e.copy(AB[0][:, 0:128], pa0[:, 0:128]), reads=[pa0b], writes=[AB_b[0]])
                S.op("dve", lambda e: e.tensor_tensor(out=Nn[0], in0=ident_f, in1=AB[0][:, 128:256], op=ALU.subtract),
                     reads=[AB_b[0], cst_b], writes=[Nn_b[0]])
                yield
                for k in range(1, 7):
                    prv, nxt_ = AB[(k - 1) % 2], AB[k % 2]
                    prvb, nxtb = AB_b[(k - 1) % 2], AB_b[k % 2]
                    pa, pab = psum()
                    S.op("pe", lambda e, pa=pa, prv=prv: e.matmul(pa[:, 0:128], prv[:, 128:256], prv[:, 0:128],
                                                                  start=True, stop=True), reads=[prvb], writes=[pab])
                    if k < 6:
                        S.op("pe", lambda e, pa=pa, prv=prv: e.matmul(pa[:, 128:256], prv[:, 0:128], prv[:, 128:256],
                                                                      start=True, stop=True), reads=[prvb], writes=[pab])
                    S.op("act", lambda e, pa=pa, nxt_=nxt_: e.copy(nxt_[:, 0:256], pa[:, 0:256]), reads=[pab],
                         writes=[nxtb])
                    yield
                    npv, npvb = Nn[(k - 1) % 2], Nn_b[(k - 1) % 2]
                    pn, pnb = psum()
                    S.op("pe", lambda e, pn=pn, nxt_=nxt_, npv=npv: e.matmul(pn[:, 0:128], nxt_[:, 0:128], npv,
                                                                             start=True, stop=True),
                         reads=[nxtb, npvb], writes=[pnb])
                    if k < 6:
                        nnx, nnxb = Nn[k % 2], Nn_b[k % 2]
                    else:
                        nnx, nnxb = TT, TT_b
                    S.op("dve", lambda e, pn=pn, npv=npv, nnx=nnx: e.tensor_tensor(out=nnx, in0=pn[:, 0:128],
                                                                                   in1=npv, op=ALU.add),
                         reads=[pnb, npvb], writes=[nnxb])
                    yield

            def scan(e_, tt):
                t = tiles(e_, tt)
                ts, hv = t["ts"], t["hv"]
                vtok, vtok_b, bcol = t["vtok"], t["vtok_b"], t["bcol"]
                eg, neg, egl, ss, rs = t["eg"], t["neg"], t["egl"], t["ss"], t["rs"]
                egb, negb, eglb, ssb, rsb = t["eg_b"], t["neg_b"], t["egl_b"], t["ss_b"], t["rs_b"]
                PTt, PTt_b, Kd, Kd_b, TT, TT_b = t["PTt"], t["PTt_b"], t["Kd"], t["Kd_b"], t["TT"], t["TT_b"]
                Rt, Rt_b, vnew, vnew_b = t["Rt"], t["Rt_b"], t["vnew"], t["vnew_b"]
                QSs, QSs_b, osb, osb_b = t["QSs"], t["QSs_b"], t["osb"], t["osb_b"]
                Sbe = Sb[:, e_, :]
                S32e = S32[:, e_, :]
                p1, p1b = psum()
                S.op("pe", lambda e: e.matmul(p1[:, 0:128], kn[:, ts], Sbe, start=True, stop=True),
                     reads=[qkb[1], S_bs[e_]], writes=[p1b])
                S.op("pe", lambda e: e.matmul(p1[:, 128:256], qn[:, ts], Sbe, start=True, stop=True),
                     reads=[qkb[0], S_bs[e_]], writes=[p1b])
                S.op("dve", lambda e: e.scalar_tensor_tensor(
                    out=Rt, in0=p1[:, 0:128], scalar=neg, in1=vtok, op0=ALU.mult, op1=ALU.add),
                    reads=[p1b, negb, vtok_b], writes=[Rt_b])
                S.op("act", lambda e: e.mul(QSs, p1[:, 128:256], eg), reads=[p1b, egb], writes=[QSs_b])
                yield
                p2, p2b = psum()
                S.op("pe", lambda e: e.matmul(p2[:, 0:128], TT, Rt, start=True, stop=True),
                     reads=[TT_b, Rt_b], writes=[p2b])
                S.op("act", lambda e: e.mul(vnew, p2[:, 0:128], bcol), reads=[p2b, bet_b], writes=[vnew_b])
                yield
                p3, p3b = psum()
                S.op("pe", lambda e: e.matmul(p3[:, 0:128], PTt, vnew, start=True, stop=True),
                     reads=[PTt_b, vnew_b], writes=[p3b])
                S.op("pe", lambda e: e.matmul(p3[:, 128:256], Kd, vnew, start=True, stop=True),
                     reads=[Kd_b, vnew_b], writes=[p3b])
                S.op("dve", lambda e: e.scalar_tensor_tensor(
                    out=S32e, in0=S32e, scalar=egl, in1=p3[:, 128:256], op0=ALU.mult, op1=ALU.add),
                    reads=[p3b, eglb, S_bs[e_]], writes=[S_bs[e_]])
                S.op("act", lambda e: e.copy(Sbe, S32e), reads=[S_bs[e_]], writes=[S_bs[e_]])
                S.op("dve", lambda e: e.tensor_tensor(out=osb, in0=p3[:, 0:128], in1=QSs, op=ALU.add),
                     reads=[p3b, QSs_b], writes=[osb_b])
                yield
                jk, jk_b = SM[2 + e_][:, 0:128], SM_b[2 + e_]
                S.op("act", lambda e: e.activation(out=jk, in_=osb, func=AF.Square, accum_out=ss),
                     reads=[osb_b], writes=[jk_b, ssb])
                S.op("act", lambda e: e.activation(out=rs, in_=ss, func=AF.Sqrt, bias=EPS, scale=1.0 / 128),
                     reads=[ssb], writes=[rsb])
                S.op("dve", lambda e: e.reciprocal(rs, rs), reads=[rsb], writes=[rsb])
                S.op("dve", lambda e: e.scalar_tensor_tensor(
                    out=osb, in0=osb, scalar=rs, in1=dnw[:], op0=ALU.mult, op1=ALU.mult),
                    reads=[osb_b, rsb, dnw_b], writes=[osb_b])
                S.op("sp", lambda e: e.dma_start(out=OA[tt * 128:(tt + 1) * 128, hv * 128:(hv + 1) * 128], in_=osb),
                     reads=[osb_b], writes=[OA_b[tt]], dma=True)
                yield

            def interleave(gens):
                gens = list(gens)
                while gens:
                    for gen in list(gens):
                        try:
                            next(gen)
                        except StopIteration:
                            gens.remove(gen)

            interleave([shared_pre(0)])
            interleave([pre(0, 0), pre(1, 0)])
            for tt in range(NT):
                gl = [scan(0, tt), scan(1, tt)]
                if tt + 1 < NT:
                    interleave([shared_pre(tt + 1)])
                    gl += [pre(0, tt + 1), pre(1, tt + 1)]
                interleave(gl)

        def sw_head(d, li, h):
            qbt, qbt_b = d["qbt"], d["qbt_b"]
            wb, wbb = d["ws"].get()
            for tg in range(4):
                S.op("sp", lambda e, tg=tg: e.dma_start(out=SM[2][:], in_=rope_d[:, 0, tg * 512:(tg + 1) * 512]),
                     writes=[SM_b[2]], dma=True)
                S.op("sp", lambda e, tg=tg: e.dma_start(out=SM[3][:], in_=rope_d[:, 1, tg * 512:(tg + 1) * 512]),
                     writes=[SM_b[3]], dma=True)
                for part in range(4):
                    pt, ptb = psum()
                    for dc in range(16):
                        S.op("pe", lambda e, pt=pt, dc=dc, tg=tg, part=part: e.matmul(
                            pt[:, :], wb[:, dc, part * 128:(part + 1) * 128], hnT[:, dc, tg * 512:(tg + 1) * 512],
                            start=(dc == 0), stop=(dc == 15)),
                            reads=[wbb, tokA] + hnT_b[tg * 4:(tg + 1) * 4], writes=[ptb])
                    S.op("act", lambda e, pt=pt: e.copy(qbt[:], pt[:, :]), reads=[ptb], writes=[qbt_b])
                    pp, ppb = psum()
                    S.op("pe", lambda e, pp=pp: e.matmul(pp[:, :], permT_b, qbt[:], start=True, stop=True),
                         reads=[qbt_b, cstb_b], writes=[ppb])
                    S.op("dve", lambda e, pt=pt: e.tensor_tensor(out=SM[0][:], in0=pt[:, :], in1=SM[2][:],
                                                                 op=ALU.mult),
                         reads=[ptb, SM_b[2]], writes=[SM_b[0]])
                    S.op("dve", lambda e, pp=pp: e.tensor_tensor(out=SM[1][:], in0=pp[:, :], in1=SM[3][:],
                                                                 op=ALU.mult),
                         reads=[ppb, SM_b[3]], writes=[SM_b[1]])
                    S.op("pool", lambda e, part=part, tg=tg: e.tensor_tensor(
                        out=BT[part][:, tg * 512:(tg + 1) * 512], in0=SM[0][:], in1=SM[1][:], op=ALU.add),
                        reads=[SM_b[0], SM_b[1]], writes=[BT_b[part]])
            wv, wvb = d["ws"].get()

            def evv(tg, pt, ptb):
                S.op("act", lambda e: e.copy(BT[4][:, tg * 512:(tg + 1) * 512], pt[:, :]), reads=[ptb], writes=[BT_b[4]])
            proj_cm(wv, wvb, h % 4, evv)
            sq, sq_b = BT[5], BT_b[5]
            kcol, kcol_b = col[0:1, 24:28], col_b[24]
            kmx, kmx_b = col[0:1, 28:29], col_b[28]
            S.op("act", lambda e: e.activation(out=sq[:], in_=BT[3][:], func=AF.Square), reads=[BT_b[3]], writes=[sq_b])
            for tg in range(4):
                pk, pkb = psum()
                S.op("pe", lambda e, pk=pk, tg=tg: e.matmul(pk[0:1, :], ones_b[:, 0:1], sq[:, tg * 512:(tg + 1) * 512],
                                                            start=True, stop=True), reads=[sq_b, cstb_b], writes=[pkb])
                S.op("dve", lambda e, pk=pk, tg=tg: e.reduce_max(out=col[0:1, 24 + tg:25 + tg], in_=pk[0:1, :],
                                                                 axis=mybir.AxisListType.X),
                     reads=[pkb], writes=[kcol_b])
            S.op("dve", lambda e: e.reduce_max(out=kmx, in_=kcol, axis=mybir.AxisListType.X), reads=[kcol_b],
                 writes=[kmx_b])
            rowf, rowf_b = SM[2], SM_b[2]
            for tg in range(4):
                pr, prb = psum()
                for gi in range(3):
                    S.op("act", lambda e, gi=gi, tg=tg: e.activation(out=qbt[:], in_=BT[gi][:, tg * 512:(tg + 1) * 512],
                                                                      func=AF.Square), reads=[BT_b[gi]], writes=[qbt_b])
                    S.op("pe", lambda e, pr=pr, gi=gi: e.matmul(pr[0:1, :], ones_b[:, 0:1], qbt[:], start=(gi == 0),
                                                                stop=(gi == 2)), reads=[qbt_b, cstb_b], writes=[prb])
                S.op("act", lambda e, pr=pr: e.activation(out=rowf[0:1, :], in_=pr[0:1, :], func=AF.Sqrt, scale=kmx),
                     reads=[prb, kmx_b], writes=[rowf_b])
                S.op("dve", lambda e, tg=tg: e.tensor_scalar_mul(sq[0:1, tg * 512:(tg + 1) * 512], rowf[0:1, :], -1.0),
                     reads=[rowf_b, sq_b], writes=[sq_b])
            negc = sq
            VA, VA_bs, oev, oev_bs, rb, rb_bs = d["VA"], d["VA_bs"], d["oev"], d["oev_bs"], d["rb"], d["rb_bs"]
            it = 0
            for (dil, gi) in SWG:
                L = T // dil
                nb = L // 128
                Qg, Qg_b = BT[gi], BT_b[gi]
                for r in range(dil):
                    acc_ps = [None, None]
                    for m in range(nb):
                        k0 = r + dil * 128 * m
                        ksl = slice(k0, k0 + dil * 127 + 1, dil)
                        nq = 2 if m + 1 < nb else 1
                        qsl = slice(k0, k0 + dil * (128 * nq - 1) + 1, dil)
                        N = 128 * nq
                        psc, pscb = psum()
                        S.op("pe", lambda e, psc=psc, ksl=ksl, qsl=qsl, N=N, Qg=Qg: e.matmul(
                            psc[:, 0:N], BT[3][:, ksl], Qg[:, qsl], start=True, stop=False),
                            reads=[BT_b[3], Qg_b], writes=[pscb])
                        S.op("pe", lambda e, psc=psc, qsl=qsl, N=N: e.matmul(
                            psc[:, 0:N], ones_b[0:1, :], negc[0:1, qsl], start=False, stop=False),
                            reads=[sq_b, cstb_b], writes=[pscb])
                        S.op("pe", lambda e, psc=psc, N=N: e.matmul(
                            psc[:, 0:N], ident_b, cstb[:, 3:3 + N // 128, :], start=False, stop=True),
                            reads=[cstb_b], writes=[pscb])
                        PTa = d["PTa"][it % 3]
                        PTa_b = d["PTa_bs"][it % 3]
                        S.op("act", lambda e, psc=psc, N=N, PTa=PTa: e.activation(
                            out=PTa[:, 0:N], in_=psc[:, 0:N], func=AF.Exp, scale=128.0 ** -0.5),
                            reads=[pscb], writes=[PTa_b])
                        va, va_b = VA[:, it % 3, :], VA_bs[it % 3]
                        p16, p16b = psum16()
                        S.op("pe", lambda e, p16=p16, ksl=ksl: e.transpose(p16, BT[4][:, ksl], ident_b),
                             reads=[BT_b[4], cstb_b], writes=[p16b])
                        S.op("dve", lambda e, p16=p16, va=va: e.tensor_copy(va[:, 0:128], p16), reads=[p16b],
                             writes=[va_b])
                        pa, pab = ps_f[ACC0 + m % 2], ps_b[ACC0 + m % 2]
                        S.op("pe", lambda e, pa=pa, PTa=PTa, va=va, m=m: e.matmul(
                            pa[:, 0:129], PTa[:, 0:128], va[:, 0:129], start=(m == 0), stop=True),
                            reads=[PTa_b, va_b], writes=[pab])
                        if nq == 2:
                            pn_, pnb_ = ps_f[ACC0 + (m + 1) % 2], ps_b[ACC0 + (m + 1) % 2]
                        ov, ov_b = oev[:, it % 2, :], oev_bs[it % 2]
                        S.op("act", lambda e, pa=pa, ov=ov: e.copy(ov[:, 0:129], pa[:, 0:129]), reads=[pab],
                             writes=[ov_b])
                        tsl = slice(k0, k0 + dil * 127 + 1, dil)
                        obw = Buf("obw")
                        d["OBw"].append(obw)
                        S.op("sp", lambda e, ov=ov, tsl=tsl, gi=gi: e.dma_start(
                            out=OB[gi, tsl, h, 0:129], in_=ov[:, 0:129]),
                            reads=[ov_b], writes=[obw], dma=True)
                        if nq == 2:
                            S.op("pe", lambda e, pn_=pn_, PTa=PTa, va=va: e.matmul(
                                pn_[:, 0:129], PTa[:, 128:256], va[:, 0:129], start=True, stop=False),
                                reads=[PTa_b, va_b], writes=[pnb_])
                        it += 1

        def combine_tile(d, tt):
            ts = slice(tt * 128, (tt + 1) * 128)
            ytile, ytile_b = ytiles[tt % 2], ytiles_b[tt % 2]
            cb_ = d["cmb_bs"]
            zcs = [Fv[0][:, k * 512:(k + 1) * 512] for k in range(3)]
            ocs = [Fv[3][:, k * 512:(k + 1) * 512] for k in range(3)]
            ychs = [BT[5][:, k * 512:(k + 1) * 512] for k in range(3)]
            zc_bs, oc_bs, ych_bs, ob_b = cb_[0:3], cb_[3:6], cb_[6:9], cb_[9]
            ob = Fv[2][:, 0:3 * 516].rearrange("p (g x) -> p g x", g=3)
            for ck in range(3):
                S.op("sp", lambda e, ck=ck: e.dma_start(out=zcs[ck], in_=ZS[ts, ck * 512:(ck + 1) * 512]),
                     reads=[ZS_b[tt], F_b[0]], writes=[zc_bs[ck]], dma=True)
                if ck < 2:
                    S.op("sp", lambda e, ck=ck: e.dma_start(out=ocs[ck], in_=OA[ts, ck * 512:(ck + 1) * 512]),
                         reads=[OA_b[tt], F_b[3]], writes=[oc_bs[ck]], dma=True)
            S.op("sp", lambda e: e.dma_start(out=ob, in_=OB[:, ts, :, :].rearrange("g t h x -> t g (h x)")),
                 reads=list(d["OBw"]) + [F_b[2]], writes=[ob_b], dma=True)
            for ck in range(3):
                if ck == 2:
                    S.op("dve", lambda e: e.tensor_tensor(out=ob[:, 0, :], in0=ob[:, 0, :], in1=ob[:, 1, :], op=ALU.add),
                         reads=[F_b[2]], writes=[ob_b])
                    S.op("dve", lambda e: e.tensor_tensor(out=ob[:, 0, :], in0=ob[:, 0, :], in1=ob[:, 2, :], op=ALU.add),
                         reads=[F_b[2]], writes=[ob_b])
                    o3 = ob[:, 0, :].rearrange("p (h x) -> p h x", h=NSW)
                    rd, rd_b = col[:, 32:32 + NSW], col_b[32]
                    S.op("dve", lambda e: e.reciprocal(rd, o3[:, :, 128]), reads=[ob_b], writes=[rd_b])
                    S.op("dve", lambda e: e.tensor_tensor(
                        out=ocs[2].rearrange("p (h x) -> p h x", h=NSW), in0=o3[:, :, 0:128],
                        in1=rd.unsqueeze(2).broadcast_to([128, NSW, 128]), op=ALU.mult),
                        reads=[ob_b, rd_b, F_b[3]], writes=[oc_bs[2]])
                S.op("dve", lambda e, ck=ck: e.tensor_tensor(out=ychs[ck], in0=ocs[ck], in1=zcs[ck], op=ALU.mult),
                     reads=[oc_bs[ck], zc_bs[ck], F_b[0], F_b[3], BT_b[5]], writes=[ych_bs[ck]])
                p16, p16b = psum16(full=True)
                for j in range(4):
                    S.op("pe", lambda e, p16=p16, j=j, ck=ck: e.transpose(
                        p16[:, j * 128:(j + 1) * 128], ychs[ck][:, j * 128:(j + 1) * 128], ident_b),
                        reads=[ych_bs[ck], BT_b[5], cstb_b], writes=[p16b])
                S.op("act", lambda e, p16=p16, ck=ck: e.copy(
                    ytile[:, ck * 4:(ck + 1) * 4, :], p16[:, 0:512].rearrange("p (a b) -> p a b", a=4)),
                    reads=[p16b], writes=[ytile_b])

        for hh in H:
            Hb[id(hh)] = bufs("H", NT)
        Hb[id(x_d)] = bufs("x", NT)
        cur = x_d
        nxt = 0
        delta = False
        for layer in layers:
            dst = H[nxt] if delta else None
            if layer % 2 == 0:
                hyb_layer(layer // 2, layer, cur, dst, delta)
            else:
                sc_layer(layer // 2, layer, cur, dst, delta)
            if delta:
                cur = dst
                nxt ^= 1
            delta = True
        if final_norm:
            final_phase(cur, delta)
        else:
            for tt in range(NT):
                load_h(cur, None, tt, delta)
                S.op("sp", lambda e, tt=tt: e.dma_start(out=out_d[tt * 128:(tt + 1) * 128, :], in_=hbuf[:]),
                     reads=[hbuf_b], writes=[out_b[tt]], dma=True)
        S.op("sp", None, reads=out_b)
        S.emit(nc, st)
    return nc


def _consts():
    idx = np.arange(128)
    ident = np.eye(128, dtype=np.float32)
    UT = (idx[:, None] <= idx[None, :]).astype(np.float32)
    MBT = np.where(idx[None, :] >= idx[:, None], 0.0, NEG).astype(np.float32)
    SMT = (idx[None, :] > idx[:, None]).astype(np.float32)
    permT = np.zeros((128, 128), np.float32)
    for m in range(16):
        permT[m + 16, m] = 1.0
        permT[m, m + 16] = 1.0
    ones = np.ones((128, 128), np.float32)
    mcur = np.where(idx[:, None] <= idx[None, :], 0.0, NEG).astype(np.float32)
    mnext = np.where(idx[:, None] >= idx[None, :], 0.0, NEG).astype(np.float32)
    c = np.stack([ident, UT, MBT, SMT, ident, permT, ones, mcur, mnext], axis=1)
    half = 16
    inv = np.power(np.float32(500000.0), -np.arange(half, dtype=np.float32) * np.float32(2.0) / np.float32(32)).astype(np.float32)
    ang = np.arange(T, dtype=np.float32)[None, :] * inv[:, None]
    cos = np.cos(ang).astype(np.float32)
    sin = np.sin(ang).astype(np.float32)
    C = np.ones((128, T), np.float32)
    Sg = np.zeros((128, T), np.float32)
    C[0:16] = cos
    C[16:32] = cos
    Sg[0:16] = -sin
    Sg[16:32] = sin
    rope = np.stack([C, Sg], axis=1)
    return np.ascontiguousarray(c), np.ascontiguousarray(rope)


def _pcn(blk):
    nb = blk.shape[0]
    return np.ascontiguousarray(blk.reshape(nb, 16, 128, 512).transpose(0, 2, 1, 3))


def _pack_hyb(w_in, j):
    blocks = []
    for g in range(4):
        gq = 4 * j + g
        cols = np.concatenate([np.arange(gq * 128, (gq + 1) * 128), 1024 + np.arange(gq * 128, (gq + 1) * 128),
                               2048 + np.arange(2 * gq * 128, (2 * gq + 2) * 128)])
        blocks.append(w_in[:, cols])
    for h in range(4):
        hq = 4 * j + h
        cols = np.concatenate([6176 + (gi * 8 + hq) * 128 + np.arange(128) for gi in range(3)] +
                              [9248 + hq * 128 + np.arange(128)])
        blocks.append(w_in[:, cols])
    blocks.append(w_in[:, 10272 + j * 512: 10272 + (j + 1) * 512])
    for k in range(2):
        blocks.append(w_in[:, 4096 + j * 1024 + k * 512: 4096 + j * 1024 + (k + 1) * 512])
    blocks.append(w_in[:, 11296 + j * 512: 11296 + (j + 1) * 512])
    return _pcn(np.stack(blocks, axis=0))


def _pack_hcw(cw, j):
    out = np.zeros((128, 16, 4), np.float32)
    for g in range(4):
        gq = 4 * j + g
        out[:, g * 4 + 0] = cw[gq * 128:(gq + 1) * 128]
        out[:, g * 4 + 1] = cw[1024 + gq * 128: 1024 + (gq + 1) * 128]
        out[:, g * 4 + 2] = cw[2048 + 2 * gq * 128: 2048 + (2 * gq + 1) * 128]
        out[:, g * 4 + 3] = cw[2048 + (2 * gq + 1) * 128: 2048 + (2 * gq + 2) * 128]
    return out


def _pack_sc(w_in, j):
    blocks = []
    for ct in range(12):
        cg = 12 * j + ct
        cols = np.concatenate([p * 3072 + cg * 128 + np.arange(128) for p in range(4)])
        blocks.append(w_in[:, cols])
    return _pcn(np.stack(blocks, axis=0))


_NC_CACHE = {}


def make_in_maps(x, norm_w, hyb_w_in, dn_conv_w, dn_a_log, dn_dt_bias, dn_norm_w, hyb_w_out,
                 sc_w_in, sc_conv_w, sc_w_out, final_norm_w):
    f = lambda a: np.ascontiguousarray(np.asarray(a, dtype=np.float32))
    consts, rope = _consts()
    shared = {
        "normw": f(np.asarray(norm_w).reshape(4, 16, 128).transpose(2, 0, 1).reshape(128, 64)),
        "fnw": f(np.asarray(final_norm_w).reshape(1, D)),
        "consts": consts, "rope": rope,
    }
    halves = []
    for j in range(2):
        m = dict(shared)
        for i in range(2):
            w_in = np.asarray(hyb_w_in[i])
            m[f"hw{i}"] = _pack_hyb(w_in, j)
            m[f"hab{i}"] = f(np.concatenate([w_in[:, 6144 + 8 * j: 6144 + 8 * j + 8],
                                             w_in[:, 6160 + 8 * j: 6160 + 8 * j + 8]], axis=1))
            m[f"hcw{i}"] = _pack_hcw(np.asarray(dn_conv_w[i]), j)
            m[f"hal{i}"] = f(np.asarray(dn_a_log[i])[8 * j: 8 * j + 8].reshape(1, 8))
            m[f"hdt{i}"] = f(np.asarray(dn_dt_bias[i])[8 * j: 8 * j + 8].reshape(1, 8))
            m[f"hnw{i}"] = f(np.asarray(dn_norm_w[i]).reshape(1, 128))
            wo_ = np.asarray(hyb_w_out[i])
            m[f"hwo{i}"] = f(np.concatenate([wo_[1024 * j: 1024 * (j + 1)], wo_[2048 + 512 * j: 2048 + 512 * (j + 1)]], axis=0))
            m[f"sw{i}"] = _pack_sc(np.asarray(sc_w_in[i]), j)
            m[f"scw{i}"] = f(np.asarray(sc_conv_w[i])[1536 * j: 1536 * (j + 1)].reshape(12, 128, 3).transpose(1, 0, 2))
            m[f"swo{i}"] = f(np.asarray(sc_w_out[i])[1536 * j: 1536 * (j + 1)])
        halves.append(m)
    maps = []
    for c in range(8):
        m = dict(halves[c % 2])
        m["x"] = f(np.asarray(x)[c // 2])
        maps.append(m)
    return maps


def kernel(x, norm_w, hyb_w_in, dn_conv_w, dn_a_log, dn_dt_bias, dn_norm_w, hyb_w_out,
           sc_w_in, sc_conv_w, sc_w_out, final_norm_w):
    maps = make_in_maps(x, norm_w, hyb_w_in, dn_conv_w, dn_a_log, dn_dt_bias, dn_norm_w, hyb_w_out,
                        sc_w_in, sc_conv_w, sc_w_out, final_norm_w)
    if "nc" not in _NC_CACHE:
        _NC_CACHE["nc"] = build_program()
    res = run_bass_kernel_spmd(_NC_CACHE["nc"], maps, core_ids=list(range(8)))
    out = np.stack([np.asarray(res.results[2 * b]["out"], dtype=np.float32) for b in range(4)], axis=0)
    return out
```

```python
import numpy as np
from contextlib import ExitStack
import concourse.bass as bass
import concourse.mybir as mybir
from concourse.bass_utils import run_bass_kernel_spmd

F32 = mybir.dt.float32
BF16 = mybir.dt.bfloat16
AF = mybir.ActivationFunctionType
ALU = mybir.AluOpType

T = 2048
D = 2048
NT = 16
EPS = 1e-6
NEG = -30000.0

EPOCH = 12000
DMA_SLOTS = 8


class Buf:
    __slots__ = ("name", "w", "rs", "excl")

    def __init__(self, name="", excl=False):
        self.name = name
        self.w = None
        self.rs = []
        self.excl = excl


class Op:
    __slots__ = ("stream", "fn", "deps", "dma", "flagged", "fidx", "slot", "use", "n", "cc")

    def __init__(self, stream, fn, dma):
        self.stream = stream
        self.fn = fn
        self.dma = dma
        self.deps = []
        self.flagged = False
        self.fidx = 0
        self.slot = 0
        self.use = 0
        self.n = 0
        self.cc = False


class Sched:
    STREAMS = ("pe", "dve", "act", "pool", "sp")

    def __init__(self):
        self.ops = {s: [] for s in self.STREAMS}
        self.ndma = {s: 0 for s in self.STREAMS}
        self.nops = 0

    def op(self, stream, fn, reads=(), writes=(), dma=False, cc=False):
        o = Op(stream, fn, dma or cc)
        o.cc = cc
        o.n = self.nops
        self.nops += 1
        ex = [b for b in reads if b.excl]
        if ex:
            reads = [b for b in reads if not b.excl]
            writes = list(writes) + ex
        deps = {}
        for b in reads:
            if b.w is not None:
                deps[id(b.w)] = b.w
        for b in writes:
            if b.w is not None:
                deps[id(b.w)] = b.w
            for r in b.rs:
                deps[id(r)] = r
        best = {}
        for d in deps.values():
            if d is o:
                continue
            if d.dma:
                o.deps.append(d)
                continue
            if d.stream == stream and stream == "pe" and not dma:
                continue
            cur = best.get(d.stream)
            if cur is None or d.n > cur.n:
                best[d.stream] = d
        for d in best.values():
            o.deps.append(d)
            d.flagged = True
        for b in reads:
            b.rs.append(o)
        for b in writes:
            b.w = o
            b.rs = []
        if cc:
            self.ncc = getattr(self, "ncc", 0) + 1
            o.use = self.ncc
        elif dma:
            n = self.ndma[stream]
            self.ndma[stream] = n + 1
            o.slot = n % DMA_SLOTS
            o.use = n // DMA_SLOTS + 1
        self.ops[stream].append(o)
        return o

    def emit(self, nc, stack):
        nsem = {}
        for s in self.STREAMS:
            c = 0
            for o in self.ops[s]:
                if o.flagged and not o.dma:
                    c += 1
                    o.fidx = c
            nsem[s] = (max(c - 1, 0) // EPOCH) + 1
        csem = {}
        for s in self.STREAMS:
            for e in range(nsem[s]):
                csem[(s, e)] = stack.enter_context(nc.semaphore(f"c_{s}_{e}"))
        dsem = {}
        for s in self.STREAMS:
            if self.ndma[s] > 0:
                for k in range(DMA_SLOTS):
                    dsem[(s, k)] = stack.enter_context(nc.semaphore(f"d_{s}_{k}"))
        ccsem = stack.enter_context(nc.semaphore("ccsem")) if getattr(self, "ncc", 0) else None
        block = stack.enter_context(nc.Block())

        def run_stream(s, eng):
            waited = {}
            for o in self.ops[s]:
                need = {}
                for d in o.deps:
                    if d.cc:
                        key = ("cc", 0, 0)
                        val = d.use
                    elif d.dma:
                        key = ("d", d.stream, d.slot)
                        val = 16 * d.use
                    else:
                        e = (d.fidx - 1) // EPOCH
                        key = ("c", d.stream, e)
                        val = d.fidx - e * EPOCH
                    if need.get(key, 0) < val:
                        need[key] = val
                if o.cc:
                    if o.use > 1:
                        need[("cc", 0, 0)] = max(need.get(("cc", 0, 0), 0), o.use - 1)
                elif o.dma and o.use > 1:
                    key = ("d", s, o.slot)
                    val = 16 * (o.use - 1)
                    if need.get(key, 0) < val:
                        need[key] = val
                for key, val in need.items():
                    if waited.get(key, 0) >= val:
                        continue
                    waited[key] = val
                    sem = ccsem if key[0] == "cc" else (dsem[(key[1], key[2])] if key[0] == "d" else csem[(key[1], key[2])])
                    eng.wait_ge(sem, val)
                if o.fn is None:
                    continue
                ins = o.fn(eng)
                if o.cc:
                    ins.then_inc(ccsem, 1)
                elif o.dma:
                    ins.then_inc(dsem[(s, o.slot)], 16)
                elif o.flagged:
                    e = (o.fidx - 1) // EPOCH
                    ins.then_inc(csem[(s, e)], 1)

        if self.ops["pe"]:
            block.tensor(lambda eng: run_stream("pe", eng))
        if self.ops["dve"]:
            block.vector(lambda eng: run_stream("dve", eng))
        if self.ops["act"]:
            block.scalar(lambda eng: run_stream("act", eng))
        if self.ops["pool"]:
            block.gpsimd(lambda eng: run_stream("pool", eng))
        if self.ops["sp"]:
            block.sync(lambda eng: run_stream("sp", eng))


NQK = 4
NV = 8
NSW = 4
SWG = ((1, 0), (4, 1), (16, 2))


def build_program(layers=(0, 1, 2, 3), final_norm=True):
    nc = bass.Bass("TRN2", target_bir_lowering=False)

    def din(name, shape, dt=F32):
        return nc.dram_tensor(name, list(shape), dt, kind="ExternalInput").ap()

    def dscr(name, shape, dt=F32):
        return nc.dram_tensor(name, list(shape), dt).ap()

    x_d = din("x", [T, D])
    normw_d = din("normw", [128, 64])
    fnw_d = din("fnw", [1, D])
    consts_d = din("consts", [128, 9, 128])
    rope_d = din("rope", [128, 2, T])
    hyb_w = [din(f"hw{i}", [12, 128, 16, 512]) for i in range(2)]
    hyb_ab = [din(f"hab{i}", [D, 16]) for i in range(2)]
    hyb_cw = [din(f"hcw{i}", [128, 16, 4]) for i in range(2)]
    hyb_al = [din(f"hal{i}", [1, 8]) for i in range(2)]
    hyb_dt = [din(f"hdt{i}", [1, 8]) for i in range(2)]
    hyb_nw = [din(f"hnw{i}", [1, 128]) for i in range(2)]
    hyb_wo = [din(f"hwo{i}", [1536, D]) for i in range(2)]
    sc_w = [din(f"sw{i}", [12, 128, 16, 512]) for i in range(2)]
    sc_cw = [din(f"scw{i}", [128, 12, 3]) for i in range(2)]
    sc_wo = [din(f"swo{i}", [1536, D]) for i in range(2)]
    out_d = nc.dram_tensor("out", [T, D], F32, kind="ExternalOutput").ap()

    H = [dscr("Hs0", [T, D]), dscr("Hs1", [T, D])]
    YT = dscr("YT", [1536, T], BF16)
    ZS = dscr("ZS", [T, 1536])
    OA = dscr("OA", [T, 1024])
    Pp = dscr("Pp", [T, D])
    Ps = dscr("Ps", [T, D])
    OB = dscr("OB", [3, T, NSW, 129])

    S = Sched()
    with ExitStack() as st:
        def sb(name, shape, dt=F32):
            return st.enter_context(nc.sbuf_tensor(name, list(shape), dt))

        def bufs(name, n):
            return [Buf(f"{name}{i}") for i in range(n)]

        ps_f = [st.enter_context(nc.psum_tensor(f"ps{i}", [128, 512], F32)) for i in range(6)]
        ps_b = [Buf(f"ps{i}", excl=True) for i in range(6)]
        ps16 = [st.enter_context(nc.psum_tensor(f"ps16_{i}", [128, 1024], BF16)) for i in range(2)]
        ps16_b = [Buf(f"ps16_{i}", excl=True) for i in range(2)]
        ps_ctr = [0, 0]
        ACC0 = 4

        ps_nrot = [6]

        def psum():
            i = ps_ctr[0] % ps_nrot[0]
            ps_ctr[0] += 1
            return ps_f[i], ps_b[i]

        def psum16(full=False):
            i = ps_ctr[1] % 2
            ps_ctr[1] += 1
            if full:
                return ps16[i], ps16_b[i]
            return ps16[i][:, 0:128], ps16_b[i]

        cst = sb("cst", [128, 4, 128])
        cst_b = Buf("cst")
        S.op("sp", lambda e: e.dma_start(out=cst[:], in_=consts_d[:, 0:4, :]), writes=[cst_b], dma=True)
        cstb = sb("cstb", [128, 5, 128], BF16)
        cstb_b = Buf("cstb")
        S.op("pool", lambda e: e.dma_start(out=cstb[:], in_=consts_d[:, 4:9, :]), writes=[cstb_b], dma=True)
        ident_f, UT, MBT, SMT = cst[:, 0, :], cst[:, 1, :], cst[:, 2, :], cst[:, 3, :]
        ident_b, permT_b, ones_b = cstb[:, 0, :], cstb[:, 1, :], cstb[:, 2, :]
        normw = sb("normw_s", [128, 64])
        normw_b = Buf("normw")
        S.op("sp", lambda e: e.dma_start(out=normw[:], in_=normw_d[:, :]), writes=[normw_b], dma=True)

        arena = sb("arena", [128, 49152], BF16)
        hnT = arena[:, 0:32768].rearrange("p (c t) -> p c t", c=16)
        wblk = [arena[:, 32768 + i * 8192: 32768 + (i + 1) * 8192].rearrange("p (c n) -> p c n", c=16)
                for i in range(2)]
        wo = arena[:, 0:24576].rearrange("p (c n) -> p c n", c=12)
        tokA = Buf("tokA")
        hnT_b = bufs("hnT", NT)
        wblk_b = bufs("wblk", 2)
        wo_b = bufs("wo", 12)
        fence = sb("fence", [128, 4])
        wctr = [0]

        def load_wblk(src_ap, ncols=512):
            i = wctr[0] % 2
            wctr[0] += 1
            S.op("pool", lambda e: e.dma_start(out=wblk[i][:, :, 0:ncols], in_=src_ap),
                 reads=[tokA], writes=[wblk_b[i]], dma=True)
            return wblk[i], wblk_b[i]

        class WStream:
            def __init__(self, srcs):
                self.srcs, self.pos, self.q = list(srcs), 0, []

            def _pf(self):
                if self.pos < len(self.srcs):
                    self.q.append(load_wblk(self.srcs[self.pos]))
                    self.pos += 1

            def get(self):
                if not self.q:
                    self._pf()
                r = self.q.pop(0)
                self._pf()
                return r

        FW = sb("FW", [128, 4 * 2052])
        Fv = [FW[:, i * 2052:(i + 1) * 2052] for i in range(4)]
        F_b = bufs("F", 4)
        BT = [sb(f"BT{i}", [128, T], BF16) for i in range(6)]
        BT_b = bufs("BT", 6)
        hbuf = sb("hbuf", [128, D])
        hbuf_b = Buf("hbuf")
        SM = [sb(f"SM{i}", [128, 512]) for i in range(4)]
        SM_b = bufs("SM", 4)
        col = sb("colstat", [128, 64])
        col_b = bufs("col", 64)
        ytile2 = sb("ytile", [128, 2, 12, 128], BF16)
        ytiles = [ytile2[:, 0, :, :], ytile2[:, 1, :, :]]
        ytiles_b = bufs("ytile", 2)
        Hb = {}
        YT_b = bufs("YT", 12)
        Pp_b = bufs("Pp", NT)
        Ps_b = bufs("Ps", 4)
        ZS_b = bufs("ZS", NT)
        OA_b = bufs("OA", NT)
        OB_b = bufs("OB", NT)
        out_b = bufs("out", NT)
        hs, hs_b = Fv[3][:, 0:D], F_b[3]
        junk, junk_b = BT[5], BT_b[5]

        def load_h(h_src, h_dst, tt, delta):
            S.op("sp", lambda e: e.dma_start(out=hbuf[:], in_=h_src[tt * 128:(tt + 1) * 128, :]),
                 reads=[Hb[id(h_src)][tt]], writes=[hbuf_b], dma=True)
            if delta:
                S.op("sp", lambda e: e.dma_start(out=hs, in_=Ps[tt * 128:(tt + 1) * 128, :]),
                     reads=[Ps_b[tt // 4]], writes=[hs_b], dma=True)
                S.op("dve", lambda e: e.tensor_tensor(out=hbuf[:], in0=hbuf[:], in1=hs, op=ALU.add),
                     reads=[hbuf_b, hs_b], writes=[hbuf_b])
                if h_dst is not None:
                    S.op("sp", lambda e: e.dma_start(out=h_dst[tt * 128:(tt + 1) * 128, :], in_=hbuf[:]),
                         reads=[hbuf_b], writes=[Hb[id(h_dst)][tt]], dma=True)

        def norm_phase(h_src, h_dst, delta, layer):
            S.op("dve", lambda e: e.memset(fence[:, 0:1], 0.0), writes=[tokA])
            for tt in range(NT):
                load_h(h_src, h_dst, tt, delta)
                c0, c0b = col[:, 0:1], col_b[0]
                c1, c1b = col[:, 1:2], col_b[1]
                S.op("act", lambda e: e.activation(out=junk[:], in_=hbuf[:], func=AF.Square, accum_out=c0),
                     reads=[hbuf_b], writes=[junk_b, c0b])
                S.op("act", lambda e: e.activation(out=c1, in_=c0, func=AF.Sqrt, bias=EPS, scale=1.0 / D),
                     reads=[c0b], writes=[c1b])
                S.op("dve", lambda e: e.reciprocal(c1, c1), reads=[c1b], writes=[c1b])
                S.op("act", lambda e: e.mul(hs, hbuf[:], c1), reads=[hbuf_b, c1b], writes=[hs_b])
                for q in range(4):
                    pt, ptb = psum()
                    for j in range(4):
                        dc = q * 4 + j
                        S.op("pe", lambda e, pt=pt, j=j, dc=dc: e.transpose(
                            pt[:, j * 128:(j + 1) * 128], hs[:, dc * 128:(dc + 1) * 128], ident_f),
                            reads=[hs_b, cst_b], writes=[ptb])
                    nwv = normw[:, layer * 16 + q * 4: layer * 16 + q * 4 + 4].unsqueeze(2).broadcast_to([128, 4, 128])
                    S.op("dve", lambda e, pt=pt, q=q, tt=tt, nwv=nwv: e.tensor_tensor(
                        out=hnT[:, q * 4:(q + 1) * 4, tt * 128:(tt + 1) * 128],
                        in0=pt[:, :].rearrange("p (a b) -> p a b", a=4), in1=nwv, op=ALU.mult),
                        reads=[ptb, normw_b, tokA], writes=[hnT_b[tt]])

        def load_wo(wo_d):
            for c in range(12):
                S.op("pool", lambda e, c=c: e.dma_start(out=wo[:, c, :], in_=wo_d[c * 128:(c + 1) * 128, :]),
                     writes=[tokA, wo_b[c]] if c == 0 else [wo_b[c]], reads=[] if c == 0 else [tokA], dma=True)

        RG = [[0, 1], [2, 3], [4, 5], [6, 7]]

        def outproj_tile(tt):
            ytile, ytile_b = ytiles[tt % 2], ytiles_b[tt % 2]
            if tt % 2 == 0:
                po, po_b = Fv[1][:, 0:D], F_b[1]
            else:
                po, po_b = hbuf[:, :], hbuf_b
            for cb in range(4):
                pt, ptb = psum()
                for c in range(12):
                    S.op("pe", lambda e, pt=pt, c=c, cb=cb: e.matmul(
                        pt[:, :], ytile[:, c, :], wo[:, c, cb * 512:(cb + 1) * 512], start=(c == 0), stop=(c == 11)),
                        reads=[ytile_b, wo_b[c], tokA], writes=[ptb])
                if cb % 2 == 0:
                    S.op("act", lambda e, pt=pt, cb=cb: e.copy(po[:, cb * 512:(cb + 1) * 512], pt[:, :]),
                         reads=[ptb], writes=[po_b])
                else:
                    S.op("dve", lambda e, pt=pt, cb=cb: e.tensor_copy(po[:, cb * 512:(cb + 1) * 512], pt[:, :]),
                         reads=[ptb], writes=[po_b])
            S.op("sp", lambda e: e.dma_start(out=Pp[tt * 128:(tt + 1) * 128, :], in_=po),
                 reads=[po_b], writes=[Pp_b[tt]], dma=True)
            if tt % 4 == 3:
                q = tt // 4
                S.op("pool", lambda e: e.collective_compute(
                    "AllReduce", ALU.add, replica_groups=RG,
                    ins=[Pp[q * 512:(q + 1) * 512, :]], outs=[Ps[q * 512:(q + 1) * 512, :]]),
                    reads=Pp_b[q * 4:(q + 1) * 4], writes=[Ps_b[q]], cc=True)

        cu, cu_b = Fv[0][:, 0:2 + T], F_b[0]
        gate, gate_b = Fv[1][:, 0:T], F_b[1]
        acc, acc_b = Fv[2][:, 0:T], F_b[2]
        sccw = sb("sccw", [128, 12, 3])
        sccw_b = Buf("sccw")

        def sc_layer(li, layer, h_src, h_dst, delta):
            norm_phase(h_src, h_dst, delta, layer)
            S.op("sp", lambda e: e.dma_start(out=sccw[:], in_=sc_cw[li][:, :, :]), writes=[sccw_b], dma=True)
            ws = WStream([sc_w[li][ct] for ct in range(12)])
            for ct in range(12):
                wb, wbb = ws.get()
                S.op("dve", lambda e: e.memset(cu[:, 0:2], 0.0), writes=[cu_b])
                for tg in range(4):
                    pp = []
                    for part in range(4):
                        pt, ptb = psum()
                        for dc in range(16):
                            S.op("pe", lambda e, pt=pt, dc=dc, part=part, tg=tg, wb=wb: e.matmul(
                                pt[:, :], wb[:, dc, part * 128:(part + 1) * 128], hnT[:, dc, tg * 512:(tg + 1) * 512],
                                start=(dc == 0), stop=(dc == 15)),
                                reads=[wbb, tokA] + hnT_b[tg * 4:(tg + 1) * 4], writes=[ptb])
                        pp.append((pt, ptb))
                    (pb_, pbb), (pc_, pcb), (pu_, pub), (pz_, pzb) = pp
                    ua, uab = SM[0], SM_b[0]
                    za, zab = SM[1], SM_b[1]
                    S.op("act", lambda e, pu_=pu_: e.copy(ua[:], pu_[:, :]), reads=[pub], writes=[uab])
                    S.op("dve", lambda e, pc_=pc_, tg=tg: e.tensor_tensor(
                        out=cu[:, 2 + tg * 512: 2 + (tg + 1) * 512], in0=pc_[:, :], in1=ua[:], op=ALU.mult),
                        reads=[pcb, uab], writes=[cu_b])
                    S.op("act", lambda e, pz_=pz_: e.activation(out=za[:], in_=pz_[:, :], func=AF.Silu),
                         reads=[pzb], writes=[zab])
                    S.op("dve", lambda e, pb_=pb_, tg=tg: e.tensor_tensor(
                        out=gate[:, tg * 512:(tg + 1) * 512], in0=pb_[:, :], in1=za[:], op=ALU.mult),
                        reads=[pbb, zab], writes=[gate_b])
                S.op("act", lambda e, ct=ct: e.mul(acc, cu[:, 2:2 + T], sccw[:, ct, 2:3]),
                     reads=[cu_b, sccw_b], writes=[acc_b])
                for i in (1, 0):
                    S.op("dve", lambda e, ct=ct, i=i: e.scalar_tensor_tensor(
                        out=acc, in0=cu[:, i:i + T], scalar=sccw[:, ct, i:i + 1], in1=acc,
                        op0=ALU.mult, op1=ALU.add), reads=[cu_b, sccw_b, acc_b], writes=[acc_b])
                yb, ybb = BT[ct % 2], BT_b[ct % 2]
                S.op("dve", lambda e, yb=yb: e.tensor_tensor(out=yb[:], in0=acc, in1=gate, op=ALU.mult),
                     reads=[acc_b, gate_b], writes=[ybb])
                S.op("sp", lambda e, yb=yb, ct=ct: e.dma_start(out=YT[ct * 128:(ct + 1) * 128, :], in_=yb[:]),
                     reads=[ybb], writes=[YT_b[ct]], dma=True)
            load_wo(sc_wo[li])
            def yload(tt):
                S.op("sp", lambda e: e.dma_start(
                    out=ytiles[tt % 2], in_=YT[:, tt * 128:(tt + 1) * 128].rearrange("(c p) t -> p c t", p=128)),
                    reads=YT_b, writes=[ytiles_b[tt % 2]], dma=True)
            yload(0)
            for tt in range(NT):
                if tt + 1 < NT:
                    yload(tt + 1)
                outproj_tile(tt)

        def final_phase(h_src, delta):
            fw, fw_b = Fv[0][:, 0:D], F_b[0]
            S.op("sp", lambda e: e.dma_start(out=fw, in_=fnw_d[0:1, :].broadcast_to([128, D])),
                 writes=[fw_b], dma=True)
            for tt in range(NT):
                load_h(h_src, None, tt, delta)
                c0, c0b = col[:, 0:1], col_b[0]
                c1, c1b = col[:, 1:2], col_b[1]
                S.op("act", lambda e: e.activation(out=junk[:], in_=hbuf[:], func=AF.Square, accum_out=c0),
                     reads=[hbuf_b], writes=[junk_b, c0b])
                S.op("act", lambda e: e.activation(out=c1, in_=c0, func=AF.Sqrt, bias=EPS, scale=1.0 / D),
                     reads=[c0b], writes=[c1b])
                S.op("dve", lambda e: e.reciprocal(c1, c1), reads=[c1b], writes=[c1b])
                S.op("dve", lambda e: e.scalar_tensor_tensor(
                    out=hs, in0=hbuf[:], scalar=c1, in1=fw, op0=ALU.mult, op1=ALU.mult),
                    reads=[hbuf_b, c1b, fw_b], writes=[hs_b])
                S.op("sp", lambda e, tt=tt: e.dma_start(out=out_d[tt * 128:(tt + 1) * 128, :], in_=hs),
                     reads=[hs_b], writes=[out_b[tt]], dma=True)

        HYB_TILES = {}

        def hyb_tiles():
            if HYB_TILES:
                return HYB_TILES
            d = HYB_TILES
            d["cw"] = sb("hcw_s", [128, 16, 4])
            d["wab"] = sb("wab_s", [128, 16, 16], BF16)
            d["bet"] = sb("bet", [128, 16, 8])
            d["gr"] = sb("gr", [128, 16, 8])
            d["bc16"] = sb("bc16", [128, 3, 8])
            d["dnw"] = sb("dnw", [128, 128])
            d["dn"] = sb("dnwork", [128, 12, 256])
            d["rb"] = sb("rbwork", [128, 22, 128], BF16)
            d["S32"] = sb("S32", [128, 2, 128])
            d["Sb"] = sb("Sbf", [128, 2, 128], BF16)
            d["qbt"] = sb("qbt", [128, 512], BF16)
            d["VA"] = sb("VA", [128, 3, 132], BF16)
            d["oev"] = sb("oev", [128, 2, 132])
            for k in list(d.keys()):
                d[k + "_b"] = Buf(k)
            d["dn_bs"] = bufs("dnw", 12)
            d["nn_bs"] = bufs("nn", 4)
            d["qo_bs"] = bufs("qo", 4)
            d["rb_bs"] = bufs("rb", 22)
            d["PTa"] = [sb(f"PTa{i}", [128, 256], BF16) for i in range(3)]
            d["PTa_bs"] = bufs("PTa", 3)
            d["OBw"] = []
            d["oc_b"] = Buf("oc")
            d["cmb_bs"] = bufs("cmb", 10)
            d["VA_bs"] = bufs("VA", 3)
            d["oev_bs"] = bufs("oev", 2)
            d["S_bs"] = bufs("S", 2)
            return d

        def hyb_layer(li, layer, h_src, h_dst, delta):
            d = hyb_tiles()
            norm_phase(h_src, h_dst, delta, layer)
            cw, cw_b = d["cw"], d["cw_b"]
            S.op("sp", lambda e: e.dma_start(out=cw[:], in_=hyb_cw[li][:, :, :]), writes=[cw_b], dma=True)
            wab, wab_b = d["wab"], d["wab_b"]
            S.op("pool", lambda e: e.dma_start(out=wab[:], in_=hyb_ab[li].rearrange("(c p) n -> p c n", p=128)),
                 writes=[wab_b], dma=True)
            bc16, bc16_b = d["bc16"], d["bc16_b"]
            S.op("sp", lambda e: e.dma_start(out=bc16[:, 0, :], in_=hyb_dt[li][0:1, :].broadcast_to([128, 8])),
                 writes=[bc16_b], dma=True)
            S.op("sp", lambda e: e.dma_start(out=bc16[:, 1, :], in_=hyb_al[li][0:1, :].broadcast_to([128, 8])),
                 reads=[bc16_b], writes=[bc16_b], dma=True)
            dnw, dnw_b = d["dnw"], d["dnw_b"]
            S.op("sp", lambda e: e.dma_start(out=dnw[:], in_=hyb_nw[li][0:1, :].broadcast_to([128, 128])),
                 writes=[dnw_b], dma=True)
            S.op("act", lambda e: e.activation(out=bc16[:, 1, :], in_=bc16[:, 1, :], func=AF.Exp),
                 reads=[bc16_b], writes=[bc16_b])
            S.op("dve", lambda e: e.tensor_scalar_mul(bc16[:, 1, :], bc16[:, 1, :], -1.0),
                 reads=[bc16_b], writes=[bc16_b])
            bet, bet_b, gr, gr_b = d["bet"], d["bet_b"], d["gr"], d["gr_b"]
            t1, t1b = SM[2], SM_b[2]
            t2, t2b = SM[3], SM_b[3]
            for tt in range(NT):
                pt, ptb = psum()
                for dc in range(16):
                    S.op("pe", lambda e, pt=pt, dc=dc, tt=tt: e.matmul(
                        pt[:, 0:16], hnT[:, dc, tt * 128:(tt + 1) * 128], wab[:, dc, :],
                        start=(dc == 0), stop=(dc == 15)), reads=[wab_b, hnT_b[tt], tokA], writes=[ptb])
                S.op("act", lambda e, pt=pt, tt=tt: e.activation(out=bet[:, tt, :], in_=pt[:, 0:8], func=AF.Sigmoid),
                     reads=[ptb], writes=[bet_b])
                S.op("dve", lambda e, pt=pt: e.tensor_tensor(out=t1[:, 0:8], in0=pt[:, 8:16], in1=bc16[:, 0, :],
                                                             op=ALU.add), reads=[ptb, bc16_b], writes=[t1b])
                S.op("act", lambda e: e.activation(out=t2[:, 0:8], in_=t1[:, 0:8], func=AF.Abs),
                     reads=[t1b], writes=[t2b])
                S.op("act", lambda e: e.activation(out=t2[:, 0:8], in_=t2[:, 0:8], func=AF.Exp, scale=-1.0),
                     reads=[t2b], writes=[t2b])
                S.op("act", lambda e: e.activation(out=t2[:, 0:8], in_=t2[:, 0:8], func=AF.Ln, bias=1.0),
                     reads=[t2b], writes=[t2b])
                S.op("dve", lambda e: e.scalar_tensor_tensor(out=t1[:, 0:8], in0=t1[:, 0:8], scalar=0.0,
                                                             in1=t2[:, 0:8], op0=ALU.max, op1=ALU.add),
                     reads=[t1b, t2b], writes=[t1b])
                S.op("dve", lambda e, tt=tt: e.tensor_tensor(out=gr[:, tt, :], in0=t1[:, 0:8], in1=bc16[:, 1, :],
                                                             op=ALU.mult), reads=[t1b, bc16_b], writes=[gr_b])
            seq = [9, 10, 11, 0, 1, 2, 3]
            for h in range(NSW):
                seq += [4 + h, 8]
            d["ws"] = WStream([hyb_w[li][k] for k in seq])
            for zb in range(3):
                wb, wbb = d["ws"].get()
                for tt in range(NT):
                    pt, ptb = psum()
                    for dc in range(16):
                        S.op("pe", lambda e, pt=pt, dc=dc, tt=tt, wb=wb: e.matmul(
                            pt[:, :], hnT[:, dc, tt * 128:(tt + 1) * 128], wb[:, dc, :],
                            start=(dc == 0), stop=(dc == 15)), reads=[wbb, hnT_b[tt], tokA], writes=[ptb])
                    zt, ztb = SM[tt % 2], SM_b[tt % 2]
                    S.op("act", lambda e, pt=pt, zt=zt: e.activation(out=zt[:], in_=pt[:, :], func=AF.Silu),
                         reads=[ptb], writes=[ztb])
                    S.op("sp", lambda e, zt=zt, tt=tt, zb=zb: e.dma_start(
                        out=ZS[tt * 128:(tt + 1) * 128, zb * 512:(zb + 1) * 512], in_=zt[:]),
                        reads=[ztb], writes=[ZS_b[tt]], dma=True)
            for g in range(NQK):
                dn_head(d, li, g)
            S.op("dve", lambda e: e.memset(d["VA"][:, :, 128:132], 1.0), writes=d["VA_bs"])
            ps_nrot[0] = 4
            for h in range(NSW):
                sw_head(d, li, h)
            ps_nrot[0] = 6
            load_wo(hyb_wo[li])
            S.op("dve", lambda e: e.memset(fence[:, 1:2], 0.0), writes=[F_b[0], F_b[2], F_b[3], BT_b[5]])
            combine_tile(d, 0, 0)
            for tt in range(NT):
                combine_tile(d, tt, 1)
                if tt + 1 < NT:
                    combine_tile(d, tt + 1, 0)
                else:
                    d["OBw"] = []
                outproj_tile(tt)

        def proj_cm(wb, wbb, part, evac):
            for tg in range(4):
                pt, ptb = psum()
                for dc in range(16):
                    S.op("pe", lambda e, pt=pt, dc=dc, tg=tg: e.matmul(
                        pt[:, :], wb[:, dc, part * 128:(part + 1) * 128], hnT[:, dc, tg * 512:(tg + 1) * 512],
                        start=(dc == 0), stop=(dc == 15)),
                        reads=[wbb, tokA] + hnT_b[tg * 4:(tg + 1) * 4], writes=[ptb])
                evac(tg, pt, ptb)

        def dn_head(d, li, g):
            cw, cw_b = d["cw"], d["cw_b"]
            wb, wbb = d["ws"].get()
            qs, qs_b = Fv[3][:, 0:T], F_b[3]
            accv, accv_b = Fv[2][:, 0:T], F_b[2]
            sq, sq_b = BT[5], BT_b[5]
            for part in range(4):
                raw, raw_b = Fv[part % 2], F_b[part % 2]
                S.op("dve", lambda e, raw=raw: e.memset(raw[:, 0:3], 0.0), writes=[raw_b])

                def ev(tg, pt, ptb, raw=raw, raw_b=raw_b):
                    S.op("act", lambda e: e.copy(raw[:, 3 + tg * 512: 3 + (tg + 1) * 512], pt[:, :]),
                         reads=[ptb], writes=[raw_b])
                proj_cm(wb, wbb, part, ev)
                tl = g * 4 + part
                S.op("act", lambda e, raw=raw, tl=tl: e.mul(accv, raw[:, 3:3 + T], cw[:, tl, 3:4]),
                     reads=[raw_b, cw_b], writes=[accv_b])
                for i in (2, 1, 0):
                    S.op("dve", lambda e, raw=raw, tl=tl, i=i: e.scalar_tensor_tensor(
                        out=accv, in0=raw[:, i:i + T], scalar=cw[:, tl, i:i + 1], in1=accv,
                        op0=ALU.mult, op1=ALU.add), reads=[raw_b, cw_b, accv_b], writes=[accv_b])
                if part < 2:
                    S.op("act", lambda e: e.activation(out=qs, in_=accv, func=AF.Silu), reads=[accv_b], writes=[qs_b])
                    S.op("act", lambda e: e.activation(out=sq[:], in_=qs, func=AF.Square), reads=[qs_b], writes=[sq_b])
                    for tg in range(4):
                        pt, ptb = psum()
                        S.op("pe", lambda e, pt=pt, tg=tg: e.matmul(pt[:, :], ones_b, sq[:, tg * 512:(tg + 1) * 512],
                                                                    start=True, stop=True),
                             reads=[sq_b, cstb_b], writes=[ptb])
                        rn, rnb = SM[tg % 2], SM_b[tg % 2]
                        S.op("act", lambda e, pt=pt, rn=rn: e.activation(out=rn[:], in_=pt[:, :], func=AF.Sqrt,
                                                                         bias=EPS, scale=1.0),
                             reads=[ptb], writes=[rnb])
                        S.op("dve", lambda e, rn=rn: e.reciprocal(rn[:], rn[:]), reads=[rnb], writes=[rnb])
                        sc_ = (128.0 ** -0.5) if part == 0 else 1.0
                        S.op("dve", lambda e, rn=rn, tg=tg, part=part, sc_=sc_: e.scalar_tensor_tensor(
                            out=BT[part][:, tg * 512:(tg + 1) * 512], in0=qs[:, tg * 512:(tg + 1) * 512], scalar=sc_,
                            in1=rn[:], op0=ALU.mult, op1=ALU.mult), reads=[qs_b, rnb], writes=[BT_b[part]])
                else:
                    S.op("act", lambda e, part=part: e.activation(out=BT[part][:], in_=accv, func=AF.Silu),
                         reads=[accv_b], writes=[BT_b[part]])
            qn, kn = BT[0], BT[1]
            qkb = [BT_b[0], BT_b[1]]
            dn, dn_bs, rb, rb_bs = d["dn"], d["dn_bs"], d["rb"], d["rb_bs"]
            S32, Sb, S_bs = d["S32"], d["Sb"], d["S_bs"]
            bet, bet_b, gr, gr_b = d["bet"], d["bet_b"], d["gr"], d["gr_b"]
            dnw, dnw_b = d["dnw"], d["dnw_b"]
            for e_ in range(2):
                S.op("dve", lambda e, e_=e_: e.memset(S32[:, e_, :], 0.0), writes=[S_bs[e_]])
                S.op("dve", lambda e, e_=e_: e.memset(Sb[:, e_, :], 0.0), reads=[S_bs[e_]], writes=[S_bs[e_]])
            def shared_pre(tt):
                par = tt % 2
                ts = slice(tt * 128, (tt + 1) * 128)
                ktok, ktok_b = rb[:, par, :], rb_bs[par]
                p16, p16b = psum16()
                S.op("pe", lambda e: e.transpose(p16, kn[:, ts], ident_b), reads=[qkb[1], cstb_b], writes=[p16b])
                S.op("act", lambda e: e.copy(ktok, p16), reads=[p16b], writes=[ktok_b])
                kkqk, kkqk_b = dn[:, 10 + par, :], dn_bs[10 + par]
                pt, ptb = psum()
                S.op("pe", lambda e: e.matmul(pt[:, 0:128], kn[:, ts], kn[:, ts], start=True, stop=True),
                     reads=[qkb[1]], writes=[ptb])
                S.op("pe", lambda e: e.matmul(pt[:, 128:256], kn[:, ts], qn[:, ts], start=True, stop=True),
                     reads=qkb, writes=[ptb])
                S.op("act", lambda e: e.copy(kkqk, pt[:, 0:256]), reads=[ptb], writes=[kkqk_b])
                yield

            def tiles(e_, tt):
                par = tt % 2
                hv = 2 * g + e_
                t = {}
                t["ts"] = slice(tt * 128, (tt + 1) * 128)
                t["ktok"], t["ktok_b"] = rb[:, par, :], rb_bs[par]
                t["kkqk"], t["kkqk_b"] = dn[:, 10 + par, :], dn_bs[10 + par]
                i = 2 + e_ * 2 + par
                t["vtok"], t["vtok_b"] = rb[:, i, :], rb_bs[i]
                for j, nm in enumerate(("PTt", "Kd", "TT")):
                    i = 6 + (e_ * 2 + par) * 3 + j
                    t[nm], t[nm + "_b"] = rb[:, i, :], rb_bs[i]
                for j, nm in enumerate(("Rt", "vnew")):
                    i = 18 + e_ * 2 + j
                    t[nm], t[nm + "_b"] = rb[:, i, :], rb_bs[i]
                c0 = 40 + (e_ * 2 + par) * 4
                for j, nm in enumerate(("ngc", "eg", "neg", "egl")):
                    t[nm], t[nm + "_b"] = col[:, c0 + j:c0 + j + 1], col_b[c0 + j]
                c0 = 56 + e_ * 2
                for j, nm in enumerate(("ss", "rs")):
                    t[nm], t[nm + "_b"] = col[:, c0 + j:c0 + j + 1], col_b[c0 + j]
                t["gcol"] = gr[:, tt, hv:hv + 1]
                t["bcol"] = bet[:, tt, hv:hv + 1]
                w0 = e_ * 5
                t["DT"], t["DTs"], t["DT_b"] = dn[:, w0, 0:128], dn[:, w0, 128:256], dn_bs[w0]
                t["AB"] = [dn[:, w0 + 1, :], dn[:, w0 + 2, :]]
                t["AB_b"] = [dn_bs[w0 + 1], dn_bs[w0 + 2]]
                t["Nn"] = [dn[:, w0 + 3, 0:128], dn[:, w0 + 3, 128:256]]
                t["Nn_b"] = [d["nn_bs"][e_ * 2], d["nn_bs"][e_ * 2 + 1]]
                t["QSs"], t["QSs_b"] = dn[:, w0 + 4, 0:128], d["qo_bs"][e_ * 2]
                t["osb"], t["osb_b"] = dn[:, w0 + 4, 128:256], d["qo_bs"][e_ * 2 + 1]
                t["hv"] = hv
                return t

            def pre(e_, tt):
                t = tiles(e_, tt)
                ts, hv = t["ts"], t["hv"]
                vT, vT_b = BT[2 + e_], BT_b[2 + e_]
                vtok, vtok_b = t["vtok"], t["vtok_b"]
                gcol, bcol = t["gcol"], t["bcol"]
                ngc, eg, neg, egl = t["ngc"], t["eg"], t["neg"], t["egl"]
                ngcb, egb, negb, eglb = t["ngc_b"], t["eg_b"], t["neg_b"], t["egl_b"]
                DT, DTs, DT_b, AB, AB_b, Nn, Nn_b = t["DT"], t["DTs"], t["DT_b"], t["AB"], t["AB_b"], t["Nn"], t["Nn_b"]
                kkqk, kkqk_b, ktok, ktok_b = t["kkqk"], t["kkqk_b"], t["ktok"], t["ktok_b"]
                PTt, PTt_b, Kd, Kd_b, TT, TT_b = t["PTt"], t["PTt_b"], t["Kd"], t["Kd_b"], t["TT"], t["TT_b"]
                tmp, tmp_b = AB[1][:, 0:128], AB_b[1]
                p16, p16b = psum16()
                S.op("pe", lambda e: e.transpose(p16, vT[:, ts], ident_b), reads=[vT_b, cstb_b], writes=[p16b])
                S.op("act", lambda e: e.copy(vtok, p16), reads=[p16b], writes=[vtok_b])
                pg, pgb = psum()
                S.op("pe", lambda e: e.matmul(pg[:, 0:128], gcol.broadcast_to([128, 128]), UT, start=True, stop=True),
                     reads=[gr_b, cst_b], writes=[pgb])
                S.op("pe", lambda e: e.matmul(pg[:, 128:129], UT, gcol, start=True, stop=True),
                     reads=[gr_b, cst_b], writes=[pgb])
                S.op("act", lambda e: e.mul(ngc, pg[:, 128:129], -1.0), reads=[pgb], writes=[ngcb])
                S.op("act", lambda e: e.activation(out=eg, in_=pg[:, 128:129], func=AF.Exp), reads=[pgb], writes=[egb])
                S.op("act", lambda e: e.activation(out=egl, in_=pg[:, 127:128], func=AF.Exp), reads=[pgb], writes=[eglb])
                S.op("dve", lambda e: e.tensor_tensor(out=tmp, in0=pg[:, 0:128], in1=MBT, op=ALU.add),
                     reads=[pgb, cst_b], writes=[tmp_b])
                S.op("dve", lambda e: e.tensor_scalar_mul(neg, eg, -1.0), reads=[egb], writes=[negb])
                yield
                S.op("act", lambda e: e.activation(out=DT, in_=tmp, func=AF.Exp, bias=ngc, scale=1.0),
                     reads=[tmp_b, ngcb], writes=[DT_b])
                S.op("pool", lambda e: e.tensor_tensor(out=DTs, in0=DT, in1=SMT, op=ALU.mult),
                     reads=[DT_b, cst_b], writes=[DT_b])
                S.op("dve", lambda e: e.scalar_tensor_tensor(
                    out=AB[0][:, 128:256], in0=kkqk[:, 0:128], scalar=bcol, in1=DTs, op0=ALU.mult, op1=ALU.mult),
                    reads=[kkqk_b, bet_b, DT_b], writes=[AB_b[0]])
                S.op("pool", lambda e: e.tensor_tensor(out=PTt, in0=kkqk[:, 128:256], in1=DT, op=ALU.mult),
                     reads=[kkqk_b, DT_b], writes=[PTt_b])
                S.op("act", lambda e: e.mul(Kd, ktok, DT[:, 127:128]), reads=[ktok_b, DT_b], writes=[Kd_b])
                yield
                pa0, pa0b = psum()
                S.op("pe", lambda e: e.transpose(pa0[:, 0:128], AB[0][:, 128:256], ident_f),
                     reads=[AB_b[0], cst_b], writes=[pa0b])
                S.op("act", lambda e: e.copy(AB[0][:, 0:128], pa0[:, 0:128]), reads=[pa0b], writes=[AB_b[0]])
                S.op("dve", lambda e: e.tensor_tensor(out=Nn[0], in0=ident_f, in1=AB[0][:, 128:256], op=ALU.subtract),
                     reads=[AB_b[0], cst_b], writes=[Nn_b[0]])
                yield
                for k in range(1, 7):
                    prv, nxt_ = AB[(k - 1) % 2], AB[k % 2]
                    prvb, nxtb = AB_b[(k - 1) % 2], AB_b[k % 2]
                    pa, pab = psum()
                    S.op("pe", lambda e, pa=pa, prv=prv: e.matmul(pa[:, 0:128], prv[:, 128:256], prv[:, 0:128],
                                                                  start=True, stop=True), reads=[prvb], writes=[pab])
                    if k < 6:
                        S.op("pe", lambda e, pa=pa, prv=prv: e.matmul(pa[:, 128:256], prv[:, 0:128], prv[:, 128:256],
                                                                      start=True, stop=True), reads=[prvb], writes=[pab])
                    S.op("act", lambda e, pa=pa, nxt_=nxt_: e.copy(nxt_[:, 0:256], pa[:, 0:256]), reads=[pab],
                         writes=[nxtb])
                    yield
                    npv, npvb = Nn[(k - 1) % 2], Nn_b[(k - 1) % 2]
                    pn, pnb = psum()
                    S.op("pe", lambda e, pn=pn, nxt_=nxt_, npv=npv: e.matmul(pn[:, 0:128], nxt_[:, 0:128], npv,
                                                                             start=True, stop=True),
                         reads=[nxtb, npvb], writes=[pnb])
                    if k < 6:
                        nnx, nnxb = Nn[k % 2], Nn_b[k % 2]
                    else:
                        nnx, nnxb = TT, TT_b
                    S.op("dve", lambda e, pn=pn, npv=npv, nnx=nnx: e.tensor_tensor(out=nnx, in0=pn[:, 0:128],
                                                                                   in1=npv, op=ALU.add),
                         reads=[pnb, npvb], writes=[nnxb])
                    yield

            def scan(e_, tt):
                t = tiles(e_, tt)
                ts, hv = t["ts"], t["hv"]
                vtok, vtok_b, bcol = t["vtok"], t["vtok_b"], t["bcol"]
                eg, neg, egl, ss, rs = t["eg"], t["neg"], t["egl"], t["ss"], t["rs"]
                egb, negb, eglb, ssb, rsb = t["eg_b"], t["neg_b"], t["egl_b"], t["ss_b"], t["rs_b"]
                PTt, PTt_b, Kd, Kd_b, TT, TT_b = t["PTt"], t["PTt_b"], t["Kd"], t["Kd_b"], t["TT"], t["TT_b"]
                Rt, Rt_b, vnew, vnew_b = t["Rt"], t["Rt_b"], t["vnew"], t["vnew_b"]
                QSs, QSs_b, osb, osb_b = t["QSs"], t["QSs_b"], t["osb"], t["osb_b"]
                Sbe = Sb[:, e_, :]
                S32e = S32[:, e_, :]
                p1, p1b = psum()
                S.op("pe", lambda e: e.matmul(p1[:, 0:128], kn[:, ts], Sbe, start=True, stop=True),
                     reads=[qkb[1], S_bs[e_]], writes=[p1b])
                S.op("pe", lambda e: e.matmul(p1[:, 128:256], qn[:, ts], Sbe, start=True, stop=True),
                     reads=[qkb[0], S_bs[e_]], writes=[p1b])
                S.op("dve", lambda e: e.scalar_tensor_tensor(
                    out=Rt, in0=p1[:, 0:128], scalar=neg, in1=vtok, op0=ALU.mult, op1=ALU.add),
                    reads=[p1b, negb, vtok_b], writes=[Rt_b])
                S.op("act", lambda e: e.mul(QSs, p1[:, 128:256], eg), reads=[p1b, egb], writes=[QSs_b])
                yield
                p2, p2b = psum()
                S.op("pe", lambda e: e.matmul(p2[:, 0:128], TT, Rt, start=True, stop=True),
                     reads=[TT_b, Rt_b], writes=[p2b])
                S.op("act", lambda e: e.mul(vnew, p2[:, 0:128], bcol), reads=[p2b, bet_b], writes=[vnew_b])
                yield
                p3, p3b = psum()
                S.op("pe", lambda e: e.matmul(p3[:, 0:128], PTt, vnew, start=True, stop=True),
                     reads=[PTt_b, vnew_b], writes=[p3b])
                S.op("pe", lambda e: e.matmul(p3[:, 128:256], Kd, vnew, start=True, stop=True),
                     reads=[Kd_b, vnew_b], writes=[p3b])
                S.op("dve", lambda e: e.scalar_tensor_tensor(
                    out=S32e, in0=S32e, scalar=egl, in1=p3[:, 128:256], op0=ALU.mult, op1=ALU.add),
                    reads=[p3b, eglb, S_bs[e_]], writes=[S_bs[e_]])
                S.op("act", lambda e: e.copy(Sbe, S32e), reads=[S_bs[e_]], writes=[S_bs[e_]])
                S.op("dve", lambda e: e.tensor_tensor(out=osb, in0=p3[:, 0:128], in1=QSs, op=ALU.add),
                     reads=[p3b, QSs_b], writes=[osb_b])
                yield
                jk, jk_b = SM[2 + e_][:, 0:128], SM_b[2 + e_]
                S.op("act", lambda e: e.activation(out=jk, in_=osb, func=AF.Square, accum_out=ss),
                     reads=[osb_b], writes=[jk_b, ssb])
                S.op("act", lambda e: e.activation(out=rs, in_=ss, func=AF.Sqrt, bias=EPS, scale=1.0 / 128),
                     reads=[ssb], writes=[rsb])
                S.op("dve", lambda e: e.reciprocal(rs, rs), reads=[rsb], writes=[rsb])
                S.op("dve", lambda e: e.scalar_tensor_tensor(
                    out=osb, in0=osb, scalar=rs, in1=dnw[:], op0=ALU.mult, op1=ALU.mult),
                    reads=[osb_b, rsb, dnw_b], writes=[osb_b])
                S.op("sp", lambda e: e.dma_start(out=OA[tt * 128:(tt + 1) * 128, hv * 128:(hv + 1) * 128], in_=osb),
                     reads=[osb_b], writes=[OA_b[tt]], dma=True)
                yield

            def interleave(gens):
                gens = list(gens)
                while gens:
                    for gen in list(gens):
                        try:
                            next(gen)
                        except StopIteration:
                            gens.remove(gen)

            interleave([shared_pre(0)])
            interleave([pre(0, 0), pre(1, 0)])
            for tt in range(NT):
                gl = [scan(0, tt), scan(1, tt)]
                if tt + 1 < NT:
                    interleave([shared_pre(tt + 1)])
                    gl += [pre(0, tt + 1), pre(1, tt + 1)]
                interleave(gl)

        def sw_head(d, li, h):
            qbt, qbt_b = d["qbt"], d["qbt_b"]
            wb, wbb = d["ws"].get()
            for tg in range(4):
                S.op("sp", lambda e, tg=tg: e.dma_start(out=SM[2][:], in_=rope_d[:, 0, tg * 512:(tg + 1) * 512]),
                     writes=[SM_b[2]], dma=True)
                S.op("sp", lambda e, tg=tg: e.dma_start(out=SM[3][:], in_=rope_d[:, 1, tg * 512:(tg + 1) * 512]),
                     writes=[SM_b[3]], dma=True)
                for part in range(4):
                    pt, ptb = psum()
                    for dc in range(16):
                        S.op("pe", lambda e, pt=pt, dc=dc, tg=tg, part=part: e.matmul(
                            pt[:, :], wb[:, dc, part * 128:(part + 1) * 128], hnT[:, dc, tg * 512:(tg + 1) * 512],
                            start=(dc == 0), stop=(dc == 15)),
                            reads=[wbb, tokA] + hnT_b[tg * 4:(tg + 1) * 4], writes=[ptb])
                    S.op("act", lambda e, pt=pt: e.copy(qbt[:], pt[:, :]), reads=[ptb], writes=[qbt_b])
                    pp, ppb = psum()
                    S.op("pe", lambda e, pp=pp: e.matmul(pp[:, :], permT_b, qbt[:], start=True, stop=True),
                         reads=[qbt_b, cstb_b], writes=[ppb])
                    S.op("dve", lambda e, pt=pt: e.tensor_tensor(out=SM[0][:], in0=pt[:, :], in1=SM[2][:],
                                                                 op=ALU.mult),
                         reads=[ptb, SM_b[2]], writes=[SM_b[0]])
                    S.op("dve", lambda e, pp=pp: e.tensor_tensor(out=SM[1][:], in0=pp[:, :], in1=SM[3][:],
                                                                 op=ALU.mult),
                         reads=[ppb, SM_b[3]], writes=[SM_b[1]])
                    S.op("pool", lambda e, part=part, tg=tg: e.tensor_tensor(
                        out=BT[part][:, tg * 512:(tg + 1) * 512], in0=SM[0][:], in1=SM[1][:], op=ALU.add),
                        reads=[SM_b[0], SM_b[1]], writes=[BT_b[part]])
            wv, wvb = d["ws"].get()

            def evv(tg, pt, ptb):
                S.op("act", lambda e: e.copy(BT[4][:, tg * 512:(tg + 1) * 512], pt[:, :]), reads=[ptb], writes=[BT_b[4]])
            proj_cm(wv, wvb, h % 4, evv)
            sq, sq_b = BT[5], BT_b[5]
            kcol, kcol_b = col[0:1, 24:28], col_b[24]
            kmx, kmx_b = col[0:1, 28:29], col_b[28]
            S.op("act", lambda e: e.activation(out=sq[:], in_=BT[3][:], func=AF.Square), reads=[BT_b[3]], writes=[sq_b])
            for tg in range(4):
                pk, pkb = psum()
                S.op("pe", lambda e, pk=pk, tg=tg: e.matmul(pk[0:1, :], ones_b[:, 0:1], sq[:, tg * 512:(tg + 1) * 512],
                                                            start=True, stop=True), reads=[sq_b, cstb_b], writes=[pkb])
                S.op("dve", lambda e, pk=pk, tg=tg: e.reduce_max(out=col[0:1, 24 + tg:25 + tg], in_=pk[0:1, :],
                                                                 axis=mybir.AxisListType.X),
                     reads=[pkb], writes=[kcol_b])
            S.op("dve", lambda e: e.reduce_max(out=kmx, in_=kcol, axis=mybir.AxisListType.X), reads=[kcol_b],
                 writes=[kmx_b])
            rowf, rowf_b = SM[2], SM_b[2]
            for tg in range(4):
                pr, prb = psum()
                for gi in range(3):
                    S.op("act", lambda e, gi=gi, tg=tg: e.activation(out=qbt[:], in_=BT[gi][:, tg * 512:(tg + 1) * 512],
                                                                      func=AF.Square), reads=[BT_b[gi]], writes=[qbt_b])
                    S.op("pe", lambda e, pr=pr, gi=gi: e.matmul(pr[0:1, :], ones_b[:, 0:1], qbt[:], start=(gi == 0),
                                                                stop=(gi == 2)), reads=[qbt_b, cstb_b], writes=[prb])
                S.op("act", lambda e, pr=pr: e.activation(out=rowf[0:1, :], in_=pr[0:1, :], func=AF.Sqrt, scale=kmx),
                     reads=[prb, kmx_b], writes=[rowf_b])
                S.op("dve", lambda e, tg=tg: e.tensor_scalar_mul(sq[0:1, tg * 512:(tg + 1) * 512], rowf[0:1, :], -1.0),
                     reads=[rowf_b, sq_b], writes=[sq_b])
            negc = sq
            VA, VA_bs, oev, oev_bs, rb, rb_bs = d["VA"], d["VA_bs"], d["oev"], d["oev_bs"], d["rb"], d["rb_bs"]
            it = 0
            for (dil, gi) in SWG:
                L = T // dil
                nb = L // 128
                Qg, Qg_b = BT[gi], BT_b[gi]
                for r in range(dil):
                    acc_ps = [None, None]
                    for m in range(nb):
                        k0 = r + dil * 128 * m
                        ksl = slice(k0, k0 + dil * 127 + 1, dil)
                        nq = 2 if m + 1 < nb else 1
                        qsl = slice(k0, k0 + dil * (128 * nq - 1) + 1, dil)
                        N = 128 * nq
                        psc, pscb = psum()
                        S.op("pe", lambda e, psc=psc, ksl=ksl, qsl=qsl, N=N, Qg=Qg: e.matmul(
                            psc[:, 0:N], BT[3][:, ksl], Qg[:, qsl], start=True, stop=False),
                            reads=[BT_b[3], Qg_b], writes=[pscb])
                        S.op("pe", lambda e, psc=psc, qsl=qsl, N=N: e.matmul(
                            psc[:, 0:N], ones_b[0:1, :], negc[0:1, qsl], start=False, stop=False),
                            reads=[sq_b, cstb_b], writes=[pscb])
                        S.op("pe", lambda e, psc=psc, N=N: e.matmul(
                            psc[:, 0:N], ident_b, cstb[:, 3:3 + N // 128, :], start=False, stop=True),
                            reads=[cstb_b], writes=[pscb])
                        PTa = d["PTa"][it % 3]
                        PTa_b = d["PTa_bs"][it % 3]
                        S.op("act", lambda e, psc=psc, N=N, PTa=PTa: e.activation(
                            out=PTa[:, 0:N], in_=psc[:, 0:N], func=AF.Exp, scale=128.0 ** -0.5),
                            reads=[pscb], writes=[PTa_b])
                        va, va_b = VA[:, it % 3, :], VA_bs[it % 3]
                        p16, p16b = psum16()
                        S.op("pe", lambda e, p16=p16, ksl=ksl: e.transpose(p16, BT[4][:, ksl], ident_b),
                             reads=[BT_b[4], cstb_b], writes=[p16b])
                        S.op("dve", lambda e, p16=p16, va=va: e.tensor_copy(va[:, 0:128], p16), reads=[p16b],
                             writes=[va_b])
                        pa, pab = ps_f[ACC0 + m % 2], ps_b[ACC0 + m % 2]
                        S.op("pe", lambda e, pa=pa, PTa=PTa, va=va, m=m: e.matmul(
                            pa[:, 0:129], PTa[:, 0:128], va[:, 0:129], start=(m == 0), stop=True),
                            reads=[PTa_b, va_b], writes=[pab])
                        if nq == 2:
                            pn_, pnb_ = ps_f[ACC0 + (m + 1) % 2], ps_b[ACC0 + (m + 1) % 2]
                        ov, ov_b = oev[:, it % 2, :], oev_bs[it % 2]
                        S.op("act", lambda e, pa=pa, ov=ov: e.copy(ov[:, 0:129], pa[:, 0:129]), reads=[pab],
                             writes=[ov_b])
                        tsl = slice(k0, k0 + dil * 127 + 1, dil)
                        obw = Buf("obw")
                        d["OBw"].append(obw)
                        S.op("sp", lambda e, ov=ov, tsl=tsl, gi=gi: e.dma_start(
                            out=OB[gi, tsl, h, 0:129], in_=ov[:, 0:129]),
                            reads=[ov_b], writes=[obw], dma=True)
                        if nq == 2:
                            S.op("pe", lambda e, pn_=pn_, PTa=PTa, va=va: e.matmul(
                                pn_[:, 0:129], PTa[:, 128:256], va[:, 0:129], start=True, stop=False),
                                reads=[PTa_b, va_b], writes=[pnb_])
                        it += 1

        def combine_tile(d, tt, phase):
            ts = slice(tt * 128, (tt + 1) * 128)
            ytile, ytile_b = ytiles[tt % 2], ytiles_b[tt % 2]
            cb_ = d["cmb_bs"]
            zcs = [Fv[0][:, k * 512:(k + 1) * 512] for k in range(3)]
            ocs = [Fv[3][:, k * 512:(k + 1) * 512] for k in range(3)]
            ychs = [BT[5][:, k * 512:(k + 1) * 512] for k in range(3)]
            zc_bs, oc_bs, ych_bs, ob_b = cb_[0:3], cb_[3:6], cb_[6:9], cb_[9]
            ob = Fv[2][:, 0:3 * 516].rearrange("p (g x) -> p g x", g=3)
            for ck in (range(3) if phase == 0 else ()):
                S.op("sp", lambda e, ck=ck: e.dma_start(out=zcs[ck], in_=ZS[ts, ck * 512:(ck + 1) * 512]),
                     reads=[ZS_b[tt], F_b[0]], writes=[zc_bs[ck]], dma=True)
                if ck < 2:
                    S.op("sp", lambda e, ck=ck: e.dma_start(out=ocs[ck], in_=OA[ts, ck * 512:(ck + 1) * 512]),
                         reads=[OA_b[tt], F_b[3]], writes=[oc_bs[ck]], dma=True)
            if phase == 0:
                S.op("sp", lambda e: e.dma_start(out=ob, in_=OB[:, ts, :, :].rearrange("g t h x -> t g (h x)")),
                     reads=list(d["OBw"]) + [F_b[2]], writes=[ob_b], dma=True)
                return
            for ck in range(3):
                if ck == 2:
                    S.op("dve", lambda e: e.tensor_tensor(out=ob[:, 0, :], in0=ob[:, 0, :], in1=ob[:, 1, :], op=ALU.add),
                         reads=[F_b[2]], writes=[ob_b])
                    S.op("dve", lambda e: e.tensor_tensor(out=ob[:, 0, :], in0=ob[:, 0, :], in1=ob[:, 2, :], op=ALU.add),
                         reads=[F_b[2]], writes=[ob_b])
                    o3 = ob[:, 0, :].rearrange("p (h x) -> p h x", h=NSW)
                    rd, rd_b = col[:, 32:32 + NSW], col_b[32]
                    S.op("dve", lambda e: e.reciprocal(rd, o3[:, :, 128]), reads=[ob_b], writes=[rd_b])
                    S.op("dve", lambda e: e.tensor_tensor(
                        out=ocs[2].rearrange("p (h x) -> p h x", h=NSW), in0=o3[:, :, 0:128],
                        in1=rd.unsqueeze(2).broadcast_to([128, NSW, 128]), op=ALU.mult),
                        reads=[ob_b, rd_b, F_b[3]], writes=[oc_bs[2]])
                S.op("dve", lambda e, ck=ck: e.tensor_tensor(out=ychs[ck], in0=ocs[ck], in1=zcs[ck], op=ALU.mult),
                     reads=[oc_bs[ck], zc_bs[ck], F_b[0], F_b[3], BT_b[5]], writes=[ych_bs[ck]])
                p16, p16b = psum16(full=True)
                for j in range(4):
                    S.op("pe", lambda e, p16=p16, j=j, ck=ck: e.transpose(
                        p16[:, j * 128:(j + 1) * 128], ychs[ck][:, j * 128:(j + 1) * 128], ident_b),
                        reads=[ych_bs[ck], BT_b[5], cstb_b], writes=[p16b])
                S.op("act", lambda e, p16=p16, ck=ck: e.copy(
                    ytile[:, ck * 4:(ck + 1) * 4, :], p16[:, 0:512].rearrange("p (a b) -> p a b", a=4)),
                    reads=[p16b], writes=[ytile_b])

        for hh in H:
            Hb[id(hh)] = bufs("H", NT)
        Hb[id(x_d)] = bufs("x", NT)
        cur = x_d
        nxt = 0
        delta = False
        for layer in layers:
            dst = H[nxt] if delta else None
            if layer % 2 == 0:
                hyb_layer(layer // 2, layer, cur, dst, delta)
            else:
                sc_layer(layer // 2, layer, cur, dst, delta)
            if delta:
                cur = dst
                nxt ^= 1
            delta = True
        if final_norm:
            final_phase(cur, delta)
        else:
            for tt in range(NT):
                load_h(cur, None, tt, delta)
                S.op("sp", lambda e, tt=tt: e.dma_start(out=out_d[tt * 128:(tt + 1) * 128, :], in_=hbuf[:]),
                     reads=[hbuf_b], writes=[out_b[tt]], dma=True)
        S.op("sp", None, reads=out_b)
        S.emit(nc, st)
    return nc


def _consts():
    idx = np.arange(128)
    ident = np.eye(128, dtype=np.float32)
    UT = (idx[:, None] <= idx[None, :]).astype(np.float32)
    MBT = np.where(idx[None, :] >= idx[:, None], 0.0, NEG).astype(np.float32)
    SMT = (idx[None, :] > idx[:, None]).astype(np.float32)
    permT = np.zeros((128, 128), np.float32)
    for m in range(16):
        permT[m + 16, m] = 1.0
        permT[m, m + 16] = 1.0
    ones = np.ones((128, 128), np.float32)
    mcur = np.where(idx[:, None] <= idx[None, :], 0.0, NEG).astype(np.float32)
    mnext = np.where(idx[:, None] >= idx[None, :], 0.0, NEG).astype(np.float32)
    c = np.stack([ident, UT, MBT, SMT, ident, permT, ones, mcur, mnext], axis=1)
    half = 16
    inv = np.power(np.float32(500000.0), -np.arange(half, dtype=np.float32) * np.float32(2.0) / np.float32(32)).astype(np.float32)
    ang = np.arange(T, dtype=np.float32)[None, :] * inv[:, None]
    cos = np.cos(ang).astype(np.float32)
    sin = np.sin(ang).astype(np.float32)
    C = np.ones((128, T), np.float32)
    Sg = np.zeros((128, T), np.float32)
    C[0:16] = cos
    C[16:32] = cos
    Sg[0:16] = -sin
    Sg[16:32] = sin
    rope = np.stack([C, Sg], axis=1)
    return np.ascontiguousarray(c), np.ascontiguousarray(rope)


def _pcn(blk):
    nb = blk.shape[0]
    return np.ascontiguousarray(blk.reshape(nb, 16, 128, 512).transpose(0, 2, 1, 3))


def _pack_hyb(w_in, j):
    blocks = []
    for g in range(4):
        gq = 4 * j + g
        cols = np.concatenate([np.arange(gq * 128, (gq + 1) * 128), 1024 + np.arange(gq * 128, (gq + 1) * 128),
                               2048 + np.arange(2 * gq * 128, (2 * gq + 2) * 128)])
        blocks.append(w_in[:, cols])
    for h in range(4):
        hq = 4 * j + h
        cols = np.concatenate([6176 + (gi * 8 + hq) * 128 + np.arange(128) for gi in range(3)] +
                              [9248 + hq * 128 + np.arange(128)])
        blocks.append(w_in[:, cols])
    blocks.append(w_in[:, 10272 + j * 512: 10272 + (j + 1) * 512])
    for k in range(2):
        blocks.append(w_in[:, 4096 + j * 1024 + k * 512: 4096 + j * 1024 + (k + 1) * 512])
    blocks.append(w_in[:, 11296 + j * 512: 11296 + (j + 1) * 512])
    return _pcn(np.stack(blocks, axis=0))


def _pack_hcw(cw, j):
    out = np.zeros((128, 16, 4), np.float32)
    for g in range(4):
        gq = 4 * j + g
        out[:, g * 4 + 0] = cw[gq * 128:(gq + 1) * 128]
        out[:, g * 4 + 1] = cw[1024 + gq * 128: 1024 + (gq + 1) * 128]
        out[:, g * 4 + 2] = cw[2048 + 2 * gq * 128: 2048 + (2 * gq + 1) * 128]
        out[:, g * 4 + 3] = cw[2048 + (2 * gq + 1) * 128: 2048 + (2 * gq + 2) * 128]
    return out


def _pack_sc(w_in, j):
    blocks = []
    for ct in range(12):
        cg = 12 * j + ct
        cols = np.concatenate([p * 3072 + cg * 128 + np.arange(128) for p in range(4)])
        blocks.append(w_in[:, cols])
    return _pcn(np.stack(blocks, axis=0))


_NC_CACHE = {}


def make_in_maps(x, norm_w, hyb_w_in, dn_conv_w, dn_a_log, dn_dt_bias, dn_norm_w, hyb_w_out,
                 sc_w_in, sc_conv_w, sc_w_out, final_norm_w):
    f = lambda a: np.ascontiguousarray(np.asarray(a, dtype=np.float32))
    consts, rope = _consts()
    shared = {
        "normw": f(np.asarray(norm_w).reshape(4, 16, 128).transpose(2, 0, 1).reshape(128, 64)),
        "fnw": f(np.asarray(final_norm_w).reshape(1, D)),
        "consts": consts, "rope": rope,
    }
    halves = []
    for j in range(2):
        m = dict(shared)
        for i in range(2):
            w_in = np.asarray(hyb_w_in[i])
            m[f"hw{i}"] = _pack_hyb(w_in, j)
            m[f"hab{i}"] = f(np.concatenate([w_in[:, 6144 + 8 * j: 6144 + 8 * j + 8],
                                             w_in[:, 6160 + 8 * j: 6160 + 8 * j + 8]], axis=1))
            m[f"hcw{i}"] = _pack_hcw(np.asarray(dn_conv_w[i]), j)
            m[f"hal{i}"] = f(np.asarray(dn_a_log[i])[8 * j: 8 * j + 8].reshape(1, 8))
            m[f"hdt{i}"] = f(np.asarray(dn_dt_bias[i])[8 * j: 8 * j + 8].reshape(1, 8))
            m[f"hnw{i}"] = f(np.asarray(dn_norm_w[i]).reshape(1, 128))
            wo_ = np.asarray(hyb_w_out[i])
            m[f"hwo{i}"] = f(np.concatenate([wo_[1024 * j: 1024 * (j + 1)], wo_[2048 + 512 * j: 2048 + 512 * (j + 1)]], axis=0))
            m[f"sw{i}"] = _pack_sc(np.asarray(sc_w_in[i]), j)
            m[f"scw{i}"] = f(np.asarray(sc_conv_w[i])[1536 * j: 1536 * (j + 1)].reshape(12, 128, 3).transpose(1, 0, 2))
            m[f"swo{i}"] = f(np.asarray(sc_w_out[i])[1536 * j: 1536 * (j + 1)])
        halves.append(m)
    maps = []
    for c in range(8):
        m = dict(halves[c % 2])
        m["x"] = f(np.asarray(x)[c // 2])
        maps.append(m)
    return maps


def kernel(x, norm_w, hyb_w_in, dn_conv_w, dn_a_log, dn_dt_bias, dn_norm_w, hyb_w_out,
           sc_w_in, sc_conv_w, sc_w_out, final_norm_w):
    maps = make_in_maps(x, norm_w, hyb_w_in, dn_conv_w, dn_a_log, dn_dt_bias, dn_norm_w, hyb_w_out,
                        sc_w_in, sc_conv_w, sc_w_out, final_norm_w)
    if "nc" not in _NC_CACHE:
        _NC_CACHE["nc"] = build_program()
    res = run_bass_kernel_spmd(_NC_CACHE["nc"], maps, core_ids=list(range(8)))
    out = np.stack([np.asarray(res.results[2 * b]["out"], dtype=np.float32) for b in range(4)], axis=0)
    return out
```
